# Optimizing a Trainium2 kernel written in Bass

```python
import math
import jax
import jax.numpy as jnp
from jax import lax
import numpy as np

D_MODEL = 2048
BATCH = 4
SEQ = 4096
DEPTH = 1

GRID_W = 64
CTX_LEN = 256
EPS = 1e-6
ATT_HEADS = 16
ATT_KV_HEADS = 4
HEAD_DIM = 64
D_ATT = ATT_HEADS * HEAD_DIM
D_KV = ATT_KV_HEADS * HEAD_DIM
WINDOW = 128
BLOCK = 128
ROPE_THETA = 10000.0
SSD_HEADS = 16
SSD_HEAD_DIM = 64
D_SSD = SSD_HEADS * SSD_HEAD_DIM
SSD_GROUPS = 4
SSD_HPG = SSD_HEADS // SSD_GROUPS
D_STATE = 128
D_XBC = D_SSD + 2 * SSD_GROUPS * D_STATE
CONV_W = 3
CHUNK = 128
D_MIX = D_ATT + D_SSD
N_IN = 2 * D_ATT + 2 * D_KV + D_SSD + D_XBC + 2 * SSD_HEADS
SPLIT_IDX = (D_ATT, D_ATT + D_KV, D_ATT + 2 * D_KV, 2 * D_ATT + 2 * D_KV, 2 * D_ATT + 2 * D_KV + D_SSD, 2 * D_ATT + 2 * D_KV + D_SSD + D_XBC, 2 * D_ATT + 2 * D_KV + D_SSD + D_XBC + SSD_HEADS)

kernel_name = 'hybrid_swa_ssd_prefix_dit_layer'


def rmsnorm(x, g):
    xf = x.astype(jnp.float32)
    y = xf * lax.rsqrt(jnp.mean(xf * xf, axis=-1, keepdims=True) + EPS)
    return (y * g.astype(jnp.float32)).astype(x.dtype)


def axial_rope(t):
    n = t.shape[1]
    rows = n // GRID_W
    row = jnp.repeat(jnp.arange(rows), GRID_W).astype(jnp.float32)
    col = jnp.tile(jnp.arange(GRID_W), rows).astype(jnp.float32)
    half = HEAD_DIM // 2
    quarter = half // 2
    freq = 1.0 / (ROPE_THETA ** (jnp.arange(quarter, dtype=jnp.float32) / quarter))

    def rot(u, pos):
        ang = pos[:, None] * freq[None, :]
        cos = jnp.cos(ang)[None, :, None, :].astype(u.dtype)
        sin = jnp.sin(ang)[None, :, None, :].astype(u.dtype)
        u1, u2 = u[..., :quarter], u[..., quarter:]
        return jnp.concatenate([u1 * cos - u2 * sin, u2 * cos + u1 * sin], axis=-1)

    return jnp.concatenate([rot(t[..., :half], row), rot(t[..., half:], col)], axis=-1)


def band_blocks(t, nb):
    b, _, h, d = t.shape
    tp = jnp.pad(t, ((0, 0), (BLOCK, BLOCK), (0, 0), (0, 0))).reshape(b, nb + 2, BLOCK, h, d)
    return jnp.concatenate([tp[:, :-2], tp[:, 1:-1], tp[:, 2:]], axis=2)


def windowed_gqa(q, k, v, k_ctx, v_ctx, sink):
    b, s, h, hd = q.shape
    kvh = k.shape[2]
    grp = h // kvh
    nb = s // BLOCK
    nl = 3 * BLOCK
    nctx = k_ctx.shape[1]
    scale = hd ** -0.5
    qb = q.reshape(b, nb, BLOCK, kvh, grp, hd)
    kw, vw = band_blocks(k, nb), band_blocks(v, nb)
    s_loc = jnp.einsum('bnqkgd,bnjkd->bnkgqj', qb, kw).astype(jnp.float32) * scale
    s_ctx = jnp.einsum('bnqkgd,bckd->bnkgqc', qb, k_ctx).astype(jnp.float32) * scale
    qi = jnp.arange(BLOCK)[:, None]
    kj = jnp.arange(nl)[None, :]
    kpos = (jnp.arange(nb)[:, None, None] - 1) * BLOCK + kj
    valid = (jnp.abs(kj - BLOCK - qi) <= WINDOW) & (kpos >= 0) & (kpos < s)
    s_loc = jnp.where(valid[None, :, None, None], s_loc, -jnp.inf)
    s_sink = jnp.broadcast_to(sink.astype(jnp.float32).reshape(1, 1, kvh, grp, 1, 1), s_loc.shape[:-1] + (1,))
    p = jax.nn.softmax(jnp.concatenate([s_loc, s_ctx, s_sink], axis=-1), axis=-1).astype(v.dtype)
    o = jnp.einsum('bnkgqj,bnjkd->bnqkgd', p[..., :nl], vw) + jnp.einsum('bnkgqc,bckd->bnqkgd', p[..., nl:nl + nctx], v_ctx)
    return o.reshape(b, s, h * hd)


def ctx_attention(q, k, v, sink):
    b, n, h, hd = q.shape
    kvh = k.shape[2]
    grp = h // kvh
    qg = q.reshape(b, n, kvh, grp, hd)
    s = jnp.einsum('bqkgd,bckd->bkgqc', qg, k).astype(jnp.float32) * hd ** -0.5
    s_sink = jnp.broadcast_to(sink.astype(jnp.float32).reshape(1, kvh, grp, 1, 1), s.shape[:-1] + (1,))
    p = jax.nn.softmax(jnp.concatenate([s, s_sink], axis=-1), axis=-1)[..., :-1].astype(v.dtype)
    o = jnp.einsum('bkgqc,bckd->bqkgd', p, v)
    return o.reshape(b, n, h * hd)


def segsum(a):
    cs = jnp.cumsum(a, axis=-1)
    t = a.shape[-1]
    mask = jnp.tril(jnp.ones((t, t), dtype=bool))
    return jnp.where(mask, cs[..., :, None] - cs[..., None, :], -jnp.inf)


def ssd_scan(x, dt, a, bm, cm, h0):
    b, l, g, r, p = x.shape
    n = bm.shape[-1]
    nc = l // CHUNK
    xf = x.astype(jnp.float32).reshape(b, nc, CHUNK, g, r, p)
    dtc = dt.reshape(b, nc, CHUNK, g, r)
    bc = bm.astype(jnp.float32).reshape(b, nc, CHUNK, g, n)
    cc = cm.astype(jnp.float32).reshape(b, nc, CHUNK, g, n)
    xdt = xf * dtc[..., None]
    a_t = jnp.moveaxis(dtc * a, (1, 2), (3, 4))
    a_cs = jnp.cumsum(a_t, axis=-1)
    cb = jnp.einsum('bclgn,bcsgn->bgcls', cc, bc)
    wts = cb[:, :, None] * jnp.exp(segsum(a_t))
    y_diag = jnp.einsum('bgrcls,bcsgrp->bclgrp', wts, xdt)
    decay_states = jnp.exp(a_cs[..., -1:] - a_cs)
    states = jnp.einsum('bclgn,bgrcl,bclgrp->bcgrpn', bc, decay_states, xdt)
    states = jnp.concatenate([h0[:, None], states], axis=1)
    a_last = jnp.pad(a_cs[..., -1], ((0, 0), (0, 0), (0, 0), (1, 0)))
    decay_chunk = jnp.exp(segsum(a_last))
    new_states = jnp.einsum('bgrzc,bcgrpn->bzgrpn', decay_chunk, states)
    prev, final = new_states[:, :-1], new_states[:, -1]
    y_off = jnp.einsum('bclgn,bcgrpn,bgrcl->bclgrp', cc, prev, jnp.exp(a_cs))
    return (y_diag + y_off).reshape(b, l, g, r, p), final


def dwconv(u, w, bias):
    out = lax.conv_general_dilated(u, w[:, None, :], window_strides=(1,), padding=[(CONV_W // 2, CONV_W // 2)], dimension_numbers=('NWC', 'WIO', 'NWC'), feature_group_count=u.shape[-1])
    return out + bias


def ssd_inputs(xbc, dt_f, dt_b, conv_w, conv_b, dt_bias_f, dt_bias_b):
    b, n, _ = xbc.shape
    u = jax.nn.silu(dwconv(xbc, conv_w, conv_b))
    xs, bm, cm = jnp.split(u, [D_SSD, D_SSD + SSD_GROUPS * D_STATE], axis=-1)
    xs = xs.reshape(b, n, SSD_GROUPS, SSD_HPG, SSD_HEAD_DIM)
    bm = bm.reshape(b, n, SSD_GROUPS, D_STATE)
    cm = cm.reshape(b, n, SSD_GROUPS, D_STATE)
    dtf = jax.nn.softplus((dt_f + dt_bias_f).astype(jnp.float32)).reshape(b, n, SSD_GROUPS, SSD_HPG)
    dtb = jax.nn.softplus((dt_b + dt_bias_b).astype(jnp.float32)).reshape(b, n, SSD_GROUPS, SSD_HPG)
    return xs, bm, cm, dtf, dtb


def bi_ssd(xs, bm, cm, dtf, dtb, a_f, a_b, d_skip, h0_f, h0_b):
    flip = lambda t: jnp.flip(t, axis=1)
    y_f, h_f = ssd_scan(xs, dtf, a_f, bm, cm, h0_f)
    y_b, h_b = ssd_scan(flip(xs), flip(dtb), a_b, flip(bm), flip(cm), h0_b)
    y = y_f + flip(y_b) + xs.astype(jnp.float32) * d_skip.astype(jnp.float32).reshape(SSD_GROUPS, SSD_HPG, 1)
    b, n = xs.shape[:2]
    return y.astype(xs.dtype).reshape(b, n, D_SSD), h_f, h_b


def merge_groups(att, g_att, y, z, att_norm_g, ssd_norm_g, w_out):
    h = jnp.concatenate([rmsnorm(att * jax.nn.silu(g_att), att_norm_g), rmsnorm(y * jax.nn.silu(z), ssd_norm_g)], axis=-1)
    return h @ w_out


def hybrid_layer(x, ctx, c, c_ctx, w_mod, b_mod, norm_g, w_in, conv_w, conv_b, a_log_f, a_log_b, dt_bias_f, dt_bias_b, d_skip, att_norm_g, ssd_norm_g, sink, w_out, update_ctx):
    b, s, _ = x.shape
    nc = ctx.shape[1]
    shift, scale, gate = jnp.split((jax.nn.silu(c) @ w_mod + b_mod)[:, None, :], 3, axis=-1)
    shift_c, scale_c, gate_c = jnp.split(jax.nn.silu(c_ctx) @ w_mod + b_mod, 3, axis=-1)
    px = (rmsnorm(x, norm_g) * (1 + scale) + shift) @ w_in
    pc = (rmsnorm(ctx, norm_g) * (1 + scale_c) + shift_c) @ w_in
    qx, kx, vx, gx, zx, xbcx, dtfx, dtbx = jnp.split(px, list(SPLIT_IDX), axis=-1)
    qc, kc, vc, gc, zc, xbcc, dtfc, dtbc = jnp.split(pc, list(SPLIT_IDX), axis=-1)
    kc_h = kc.reshape(b, nc, ATT_KV_HEADS, HEAD_DIM)
    vc_h = vc.reshape(b, nc, ATT_KV_HEADS, HEAD_DIM)
    qx_h = axial_rope(qx.reshape(b, s, ATT_HEADS, HEAD_DIM))
    kx_h = axial_rope(kx.reshape(b, s, ATT_KV_HEADS, HEAD_DIM))
    vx_h = vx.reshape(b, s, ATT_KV_HEADS, HEAD_DIM)
    att_x = windowed_gqa(qx_h, kx_h, vx_h, kc_h, vc_h, sink)
    a_f = -jnp.exp(a_log_f.astype(jnp.float32)).reshape(SSD_GROUPS, SSD_HPG)
    a_b = -jnp.exp(a_log_b.astype(jnp.float32)).reshape(SSD_GROUPS, SSD_HPG)
    xs_c, bm_c, cm_c, dtf_c, dtb_c = ssd_inputs(xbcc, dtfc, dtbc, conv_w, conv_b, dt_bias_f, dt_bias_b)
    h0 = jnp.zeros((b, SSD_GROUPS, SSD_HPG, SSD_HEAD_DIM, D_STATE), jnp.float32)
    y_c, h_cf, h_cb = bi_ssd(xs_c, bm_c, cm_c, dtf_c, dtb_c, a_f, a_b, d_skip, h0, h0)
    xs_x, bm_x, cm_x, dtf_x, dtb_x = ssd_inputs(xbcx, dtfx, dtbx, conv_w, conv_b, dt_bias_f, dt_bias_b)
    y_x, _, _ = bi_ssd(xs_x, bm_x, cm_x, dtf_x, dtb_x, a_f, a_b, d_skip, h_cf, h_cb)
    x = x + gate * merge_groups(att_x, gx, y_x, zx, att_norm_g, ssd_norm_g, w_out)
    if update_ctx:
        att_c = ctx_attention(qc.reshape(b, nc, ATT_HEADS, HEAD_DIM), kc_h, vc_h, sink)
        ctx = ctx + gate_c * merge_groups(att_c, gc, y_c, zc, att_norm_g, ssd_norm_g, w_out)
    return x, ctx


def setup_inputs(seed: int = 0) -> dict:
    key = jax.random.key(seed)
    ks = jax.random.split(key, 20)
    f32 = jnp.float32
    nrm = lambda k, shape, sc: jax.random.normal(k, shape, f32) * sc
    dt0 = jnp.exp(jax.random.uniform(ks[10], (2, DEPTH, SSD_HEADS), f32, math.log(1e-3), math.log(1e-1)))
    dt_bias = dt0 + jnp.log(-jnp.expm1(-dt0))
    a_log = jnp.log(jax.random.uniform(ks[11], (2, DEPTH, SSD_HEADS), f32, 1.0, 16.0))
    return {
        'x': nrm(ks[0], (BATCH, SEQ, D_MODEL), 1.0),
        'c': nrm(ks[1], (BATCH, D_MODEL), 1.0),
        'ctx': nrm(ks[2], (BATCH, CTX_LEN, D_MODEL), 1.0),
        'c_ctx': nrm(ks[3], (D_MODEL,), 1.0),
        'w_mod': nrm(ks[4], (DEPTH, D_MODEL, 3 * D_MODEL), 0.5 * D_MODEL ** -0.5),
        'b_mod': nrm(ks[5], (DEPTH, 3 * D_MODEL), 0.01),
        'norm_g': 1.0 + nrm(ks[6], (DEPTH, D_MODEL), 0.02),
        'w_in': nrm(ks[7], (DEPTH, D_MODEL, N_IN), D_MODEL ** -0.5),
        'conv_w': nrm(ks[8], (DEPTH, CONV_W, D_XBC), CONV_W ** -0.5),
        'conv_b': nrm(ks[9], (DEPTH, D_XBC), 0.01),
        'a_log_f': a_log[0],
        'a_log_b': a_log[1],
        'dt_bias_f': dt_bias[0],
        'dt_bias_b': dt_bias[1],
        'd_skip': 1.0 + nrm(ks[12], (DEPTH, SSD_HEADS), 0.02),
        'att_norm_g': 1.0 + nrm(ks[13], (DEPTH, D_ATT), 0.02),
        'ssd_norm_g': 1.0 + nrm(ks[14], (DEPTH, D_SSD), 0.02),
        'sink': nrm(ks[15], (DEPTH, ATT_HEADS), 0.5),
        'w_out': nrm(ks[16], (DEPTH, D_MIX, D_MODEL), D_MIX ** -0.5),
        'final_norm_g': 1.0 + nrm(ks[17], (D_MODEL,), 0.02),
    }


def reference(x, c, ctx, c_ctx, w_mod, b_mod, norm_g, w_in, conv_w, conv_b, a_log_f, a_log_b, dt_bias_f, dt_bias_b, d_skip, att_norm_g, ssd_norm_g, sink, w_out, final_norm_g):
    for l in range(DEPTH):
        x, ctx = hybrid_layer(x, ctx, c, c_ctx, w_mod[l], b_mod[l], norm_g[l], w_in[l], conv_w[l], conv_b[l], a_log_f[l], a_log_b[l], dt_bias_f[l], dt_bias_b[l], d_skip[l], att_norm_g[l], ssd_norm_g[l], sink[l], w_out[l], l < DEPTH - 1)
    return rmsnorm(x, final_norm_g)
```

```python
import contextlib
import os
import numpy as np
CUT = int(os.environ.get('KCUT', '99'))
CUT2 = int(os.environ.get('KCUT2', '99'))
import concourse.bass as bass
import concourse.mybir as mybir
from concourse.bass_utils import run_bass_kernel_spmd

F32 = mybir.dt.float32
BF16 = mybir.dt.bfloat16
AF = mybir.ActivationFunctionType
ALU = mybir.AluOpType

D = 2048
NT = 2048
NTH = 2176
SEQ = 4096
NCTX = 256
NIN = 5664
OFF_Q, OFF_K, OFF_V, OFF_G, OFF_Z, OFF_XS, OFF_B, OFF_C = 0, 1024, 1280, 1536, 2560, 3584, 4608, 5120
OFF_DTF, OFF_DTB = 5632, 5648
EPS = 1e-6


class Sched:
    ENGS = ["pe", "act", "dve", "pool", "sp"]
    NDMASEM = 4
    WINDOW = 192

    def __init__(self, nc):
        self.nc = nc
        self.ops = {e: [] for e in self.ENGS}
        self.lastw = {}
        self.readers = {}
        self.ndma = {e: 0 for e in self.ENGS}
        self.epoch = 0

    def barrier(self):
        self.epoch += 1
        self.lastw = {}
        self.readers = {}

    def add(self, eng, fn, reads=(), writes=(), dma=False, cost=0.3):
        idx = len(self.ops[eng])
        psr = [k for k in reads if k.startswith("ps")]
        if psr:
            reads = [k for k in reads if not k.startswith("ps")]
            writes = list(writes) + [k for k in psr if k not in writes]
        deps = set()
        for k in reads:
            if k in self.lastw:
                deps.add(self.lastw[k])
        for k in writes:
            if k in self.lastw:
                deps.add(self.lastw[k])
            for r in self.readers.get(k, ()):
                deps.add(r)
        deps.discard((eng, idx))
        odeps = set()
        if eng == "pe":
            odeps = set(d for d in deps if d[0] == "pe")
            deps = deps - odeps
        op = dict(fn=fn, deps=deps, odeps=odeps, dma=dma, marked=False, cost=cost, epoch=self.epoch)
        self.ops[eng].append(op)
        for k in reads:
            self.readers.setdefault(k, []).append((eng, idx))
        for k in writes:
            self.lastw[k] = (eng, idx)
            self.readers[k] = []
        return (eng, idx)

    def pe(self, fn, reads=(), writes=(), cost=0.15):
        return self.add("pe", fn, reads, writes, cost=cost)

    def act(self, fn, reads=(), writes=(), cost=0.4):
        return self.add("act", fn, reads, writes, cost=cost)

    def dve(self, fn, reads=(), writes=(), cost=0.4):
        return self.add("dve", fn, reads, writes, cost=cost)

    def dma(self, q, fn, reads=(), writes=(), cost=4.0):
        return self.add(q, fn, reads, writes, dma=True, cost=cost)

    def schedule(self):
        ops = self.ops
        pending = {e: list(range(len(ops[e]))) for e in self.ENGS}
        done = {e: [None] * len(ops[e]) for e in self.ENGS}
        free = {e: 0.0 for e in self.ENGS}
        order = {e: [] for e in self.ENGS}
        dma_free = [0.0]
        self.fence_pending = {}
        ntot = sum(len(v) for v in pending.values())
        SYNC = 0.08
        epoch = 0
        last_exec = {}
        for _ in range(ntot):
            while not any(pending[e] and ops[e][pending[e][0]]["epoch"] == epoch for e in self.ENGS):
                epoch += 1
                tmax = max(free.values())
                for e in self.ENGS:
                    free[e] = tmax
                    if pending[e]:
                        nxt = [i for i in pending[e] if ops[e][i]["epoch"] == epoch]
                        if nxt:
                            self.fence_pending.setdefault(e, set()).update(last_exec.values())
                for k in [k for k in last_exec if isinstance(k, tuple)]:
                    del last_exec[k]
            best = None
            for e in self.ENGS:
                pl = pending[e]
                fe = free[e]
                for wi in range(min(self.WINDOW, len(pl))):
                    i = pl[wi]
                    op = ops[e][i]
                    if op["epoch"] != epoch:
                        break
                    st = fe
                    ok = True
                    for (de, di) in op["deps"]:
                        t = done[de][di]
                        if t is None:
                            ok = False
                            break
                        if t + SYNC > st:
                            st = t + SYNC
                    if ok:
                        for (de, di) in op["odeps"]:
                            if done[de][di] is None:
                                ok = False
                                break
                    if not ok:
                        continue
                    key = (st, wi)
                    if best is None or key < best[0]:
                        best = (key, e, wi, i, st)
                    if st <= fe:
                        break
            assert best is not None, "scheduler deadlock"
            _, e, wi, i, st = best
            op = ops[e][i]
            if op["dma"]:
                free[e] = st + 0.15
                t0 = max(st + 2.0, dma_free[0])
                dma_free[0] = t0 + op["cost"]
                done[e][i] = dma_free[0]
            else:
                free[e] = st + op["cost"]
                done[e][i] = free[e]
            fp = self.fence_pending.pop(e, None)
            if fp:
                op["deps"] = set(op["deps"]) | set(d for d in fp if d != (e, i))
            if op["dma"]:
                last_exec[(e, i)] = (e, i)
            else:
                last_exec[e] = (e, i)
            order[e].append(i)
            del pending[e][wi]
        self.est_time = max(max([0.0] + [t for t in done[e] if t is not None]) for e in self.ENGS)
        return order

    def emit(self, final_waits=()):
        nc = self.nc
        ops = self.ops
        fin = dict(fn=None, deps=set(final_waits), odeps=set(), dma=False, marked=False, cost=0.0, epoch=self.epoch)
        ops["sp"].append(fin)
        order = self.schedule()
        for e in self.ENGS:
            for op in ops[e]:
                for (de, di) in op["deps"]:
                    ops[de][di]["marked"] = True
        for e in self.ENGS:
            c = 0
            nd = 0
            for i in order[e]:
                op = ops[e][i]
                if op["dma"]:
                    op["dmaslot"] = nd
                    nd += 1
                elif op["marked"]:
                    c += 1
                    op["cnt"] = c
        with contextlib.ExitStack() as st:
            esem = {e: st.enter_context(nc.semaphore("s_" + e)) for e in self.ENGS}
            dsem = {}
            for e in self.ENGS:
                if self.ndma[e] or any(op["dma"] for op in ops[e]):
                    dsem[e] = [st.enter_context(nc.semaphore("d_%s%d" % (e, i)))
                               for i in range(self.NDMASEM)]
            block = st.enter_context(nc.Block())
            engobj = {"pe": "tensor", "act": "scalar", "dve": "vector", "pool": "gpsimd", "sp": "sync"}

            def run(e, eng):
                seen = {}
                seen_dma = set()
                for i in order[e]:
                    op = ops[e][i]
                    waits = []
                    if op["dma"]:
                        slot = op["dmaslot"]
                        if slot >= self.NDMASEM:
                            waits.append((dsem[e][slot % self.NDMASEM], 16 * (slot // self.NDMASEM)))
                    cneed = {}
                    for (de, di) in op["deps"]:
                        dop = ops[de][di]
                        if dop["dma"]:
                            if (de, di) in seen_dma:
                                continue
                            seen_dma.add((de, di))
                            s = dop["dmaslot"]
                            waits.append((dsem[de][s % self.NDMASEM], 16 * (s // self.NDMASEM + 1)))
                        else:
                            c = dop["cnt"]
                            if seen.get(de, 0) >= c:
                                continue
                            cneed[de] = max(cneed.get(de, 0), c)
                    for de, c in cneed.items():
                        seen[de] = c
                        waits.append((esem[de], c))
                    for (s, v) in waits:
                        eng.wait_ge(s, v)
                    if op["fn"] is None:
                        continue
                    ins = op["fn"](eng)
                    if op["dma"]:
                        ins.then_inc(dsem[e][op["dmaslot"] % self.NDMASEM], 16)
                    elif op["marked"]:
                        ins.then_inc(esem[e], 1)

            for e in self.ENGS:
                if not ops[e]:
                    continue

                def mk(e):
                    def _f(eng):
                        run(e, eng)
                    return _f
                getattr(block, engobj[e])(mk(e))


def build(stage=99, dbg=(), ncores=8):
    nc = bass.Bass("TRN2", target_bir_lowering=False)

    def din(name, shape):
        return nc.dram_tensor(name, list(shape), F32, kind="ExternalInput").ap()

    xloc = din("xloc", [NTH, D])
    ctxf = din("ctxf", [NCTX, D])
    cvec = din("cvec", [128, 32])
    w_mod_sh = din("w_mod_sh", [D, 3 * D // 2])
    bmod_l = din("bmod_l", [128, 16])
    bgate_l = din("bgate_l", [1, D // 2])
    rmask = din("rmask", [128, 2])
    normgT = din("normgT", [128, 16])
    w_in = din("w_in", [D, NIN])
    w_dt = din("w_dt", [D, 32])
    convT = din("convT", [128, 16 * 4])
    dtb = din("dtb", [128, 32])
    alog = din("alog", [128, 32])
    dskipT = din("dskipT", [128, 8])
    attgT = din("attgT", [128, 8])
    ssdgT = din("ssdgT", [128, 8])
    sinkb = din("sinkb", [128, 16])
    dskb = din("dskb", [128, 16])
    w_out = din("w_out", [D, D])
    fng = din("fng", [1, D])
    cosT = din("cosT", [128, NTH])
    ssinT = din("ssinT", [128, NTH])
    consts = din("consts", [128, 6 * 128])
    out = nc.dram_tensor("out", [NT, D], F32, kind="ExternalOutput").ap()
    gate_scr = nc.dram_tensor("gate_scr", [1, D], F32).ap()
    ccg_in = nc.dram_tensor("ccg_in", [1, D], F32)
    ccm_in = nc.dram_tensor("ccm_in", [128, 64], F32)
    ccm_out = nc.dram_tensor("ccm_out", [128, 64], F32)
    h_scr = nc.dram_tensor("h_scr", [16, 128, 8, 128], BF16).ap()
    cc_in = [nc.dram_tensor("cc_in%d" % g, [128, 256], F32) for g in range(4)]
    cc_out = [nc.dram_tensor("cc_out%d" % g, [128, 256], F32) for g in range(4)]
    dbg_out = {}
    for (nm, shp) in dbg:
        dbg_out[nm] = nc.dram_tensor("dbg_" + nm, list(shp), F32, kind="ExternalOutput").ap()

    st = contextlib.ExitStack()
    with st:
        S = Sched(nc)
        ARENA = 189 * 1024
        arena = st.enter_context(nc.sbuf_tensor("arena", [128, ARENA // 2], BF16))
        cst = st.enter_context(nc.sbuf_tensor("cst", [128, 1472], F32))
        pbank = [st.enter_context(nc.psum_tensor("pb%d" % i, [128, 512], F32)) for i in range(8)]

        class Ar:
            def __init__(self):
                self.off = 0

            def take(self, nbytes, dt, parts=128):
                nbytes = (nbytes + 63) // 64 * 64
                o = self.off
                self.off += nbytes
                assert self.off <= ARENA, ("arena overflow", self.off)
                v = arena[0:parts, o // 2:(o + nbytes) // 2]
                if dt == F32:
                    v = v.bitcast(F32)
                return v

            def mark(self):
                return self.off

            def reset(self, m):
                self.off = m
                S.barrier()

        A = Ar()
        coff = [0]

        def ctake(n):
            o = coff[0]
            coff[0] += n
            assert coff[0] <= 1472
            return cst[:, o:o + n]

        def fsz(ap):
            n = 1
            for d in ap.shape[1:]:
                n *= d
            return n

        def dma(q, o, i, reads=(), writes=()):
            esz = 4 if i.dtype == F32 else 2
            nbytes = fsz(i) * i.shape[0] * esz
            return S.dma(q, lambda e: e.dma_start(out=o, in_=i), reads, writes, cost=nbytes / 250e3)

        def mm(o, lhsT, rhs, start, stop, reads, writes):
            c = 0.065 + fsz(rhs) / 2400.0
            if rhs.dtype == F32:
                c = 0.1 + fsz(rhs) / 600.0
            return S.pe(lambda e: e.matmul(o, lhsT=lhsT, rhs=rhs, start=start, stop=stop), reads, writes, cost=c)

        def tr(o, i, reads, writes):
            return S.pe(lambda e: e.transpose(o, i, ident_b), list(reads) + ["cb16"], writes, cost=0.12)

        def act(o, i, func, reads, writes, bias=None, scale=None, accum=None):
            kw = {}
            c = 0.22 + fsz(o) / 1400.0
            if bias is not None:
                kw["bias"] = bias
            if scale is not None:
                kw["scale"] = scale
            if accum is not None:
                kw["accum_out"] = accum
                c += 0.1
            return S.act(lambda e: e.activation(out=o, in_=i, func=func, **kw), reads, writes, cost=c)

        def dcost(o, two=False):
            n = fsz(o)
            if o.dtype == F32 or two:
                return 0.08 + n / 960.0
            return 0.08 + n / 1200.0

        def tt(o, a, b, op, reads, writes, eng="dve"):
            return S.add(eng, lambda e: e.tensor_tensor(out=o, in0=a, in1=b, op=op), reads, writes, cost=dcost(o, True))

        def ts(o, a, s1, s2, op0, op1, reads, writes, eng="dve"):
            if op1 is None:
                return S.add(eng, lambda e: e.tensor_scalar(out=o, in0=a, scalar1=s1, scalar2=None, op0=op0), reads, writes,
                             cost=dcost(o))
            return S.add(eng, lambda e: e.tensor_scalar(out=o, in0=a, scalar1=s1, scalar2=s2, op0=op0, op1=op1), reads, writes,
                         cost=dcost(o))

        def stt(o, a, s, b, op0, op1, reads, writes):
            return S.dve(lambda e: e.scalar_tensor_tensor(out=o, in0=a, scalar=s, in1=b, op0=op0, op1=op1), reads, writes,
                         cost=dcost(o, True))

        def cp(o, i, reads, writes, eng="dve"):
            return S.add(eng, lambda e: e.tensor_copy(out=o, in_=i), reads, writes, cost=dcost(o))

        def memset(o, v, writes, eng="dve"):
            return S.add(eng, lambda e: e.memset(o, v), (), writes, cost=dcost(o))

        def pbf(i):
            return pbank[i][:, :].bitcast(BF16)

        Hst = st.enter_context(nc.sbuf_tensor("Hst", [128, 2048], F32))
        cf = ctake(768)
        dma("sp", cf, consts, writes=["cf"])
        ident_f = cf[:, 0:128]
        TLEf = cf[:, 128:256]
        TGEf = cf[:, 256:384]
        SUf = cf[:, 384:512]
        SLf = cf[:, 512:640]
        cb16 = A.take(6 * 128 * 2, BF16)
        cp(cb16, cf, ["cf"], ["cb16"])
        ident_b = cb16[:, 0:128]
        TLEb = cb16[:, 128:256]
        TGEb = cb16[:, 256:384]
        perm_b = cb16[:, 640:768]
        onesf = ctake(128)
        memset(onesf, 1.0, ["onesf"])
        small = ctake(32 + 32 + 16 + 64 + 32 + 32 + 8 + 8 + 8 + 16 + 16)
        rmask_s = ctake(2)
        cv = small[:, 0:32]
        bmod_s = small[:, 32:48]
        normg_s = small[:, 64:80]
        conv_s = small[:, 80:144]
        dtb_s = small[:, 144:176]
        alog_s = small[:, 176:208]
        dskip_s = small[:, 208:216]
        attg_s = small[:, 216:224]
        ssdg_s = small[:, 224:232]
        sink_s = small[:, 232:248]
        dskb_s = small[:, 248:264]
        for (dst, src, k) in [(cv, cvec, "cv"), (bmod_s, bmod_l, "bmod"), (rmask_s, rmask, "rmask"), (normg_s, normgT, "normg"), (conv_s, convT, "conv"),
                              (dtb_s, dtb, "dtb"), (alog_s, alog, "alog"), (dskip_s, dskipT, "dskip"), (attg_s, attgT, "attg"),
                              (ssdg_s, ssdgT, "ssdg"), (sink_s, sinkb, "sink"), (dskb_s, dskb, "dskb")]:
            dma("sp", dst, src, writes=[k])
        a_s = ctake(32)
        act(a_s, alog_s, AF.Exp, ["alog"], ["a_s"])
        ts(a_s, a_s, -1.0, None, ALU.mult, None, ["a_s"], ["a_s"])
        esink = ctake(16)
        act(esink, sink_s, AF.Exp, ["sink"], ["esink"])
        scb = A.take(32 * 2, BF16)
        act(scb, cv, AF.Silu, ["cv"], ["scb"])

        modT = ctake(64)
        PAIRS = [[2 * i, 2 * i + 1] for i in range(ncores // 2)]
        top = (ARENA - 41984) // 2
        wmb = [arena[:, top + i * 8192:top + (i + 1) * 8192].rearrange("p (k n) -> p k n", k=16) for i in range(2)]
        G2 = arena[:, top + 16384:top + 16384 + 2048].bitcast(F32)
        bgl = arena[:, top + 18432:top + 18432 + 2048].bitcast(F32)
        ml = arena[:, top + 20480:top + 20480 + 64].bitcast(F32)
        modl = arena[:, top + 20544:top + 20544 + 128].bitcast(F32)
        dma("sp", bgl[0:1, :], bgate_l, writes=["bgl"])
        psmod = pbank[3][:, 0:32]
        for lb in range(4):
            wb_ = wmb[lb % 2]
            dma("pool", wb_, w_mod_sh[:, lb * 512:(lb + 1) * 512].rearrange("(k p) n -> p k n", p=128), writes=["wmb%d" % (lb % 2)])
            for c4 in range(4):
                i = lb * 4 + c4
                for kc in range(16):
                    mm(psmod[:, i * 2:i * 2 + 2], wb_[:, kc, c4 * 128:(c4 + 1) * 128], scb[:, kc * 2:kc * 2 + 2],
                       kc == 0, kc == 15, ["wmb%d" % (lb % 2), "scb"], ["ps3"])
        tt(ml.rearrange("p (c j) -> p c j", j=2), psmod.rearrange("p (c j) -> p c j", j=2),
           bmod_s.rearrange("p (c o) -> p c o", o=1).to_broadcast([128, 16, 2]), ALU.add, ["ps3", "bmod"], ["ml"])
        tt(modl.rearrange("p (c r j) -> p c r j", r=2, j=2),
           ml.rearrange("p (c o j) -> p c o j", o=1, j=2).to_broadcast([128, 16, 2, 2]),
           rmask_s.rearrange("p (o r q) -> p o r q", o=1, q=1).to_broadcast([128, 16, 2, 2]), ALU.mult, ["ml", "rmask"], ["modl"])
        dma("sp", ccm_in.ap(), modl, ["modl"], ["ccm_in"])
        S.add("pool", lambda e: e.collective_compute("AllReduce", ALU.add, replica_groups=PAIRS,
                                                     ins=[ccm_in.ap().opt()], outs=[ccm_out.ap().opt()]),
              ["ccm_in"], ["ccm_out"], cost=12.0)
        dma("sp", modT, ccm_out.ap(), ["ccm_out"], ["modT"])
        for gb in range(2):
            wb_ = wmb[gb % 2]
            dma("pool", wb_, w_mod_sh[:, 2048 + gb * 512:2048 + (gb + 1) * 512].rearrange("(k p) n -> p k n", p=128),
                writes=["wmb%d" % (gb % 2)])
            for kc in range(16):
                mm(pbank[5][:, :], scb[:, kc * 2:kc * 2 + 1].to_broadcast([128, 128]), wb_[:, kc, :], kc == 0, kc == 15,
                   ["scb", "wmb%d" % (gb % 2)], ["ps5"])
            tt(G2[0:1, gb * 512:(gb + 1) * 512], pbank[5][0:1, :], bgl[0:1, gb * 512:(gb + 1) * 512], ALU.add, ["ps5", "bgl"], ["G2"])
        tt(Hst[0:1, :].rearrange("p (g r n) -> p g r n", g=2, r=2),
           G2[0:1, :].rearrange("p (g o n) -> p g o n", g=2, o=1).to_broadcast([1, 2, 2, 512]),
           rmask_s[0:1, :].rearrange("p (o r q) -> p o r q", o=1, q=1).to_broadcast([1, 2, 2, 512]), ALU.mult,
           ["G2", "rmask"], ["HF", "HB"])
        dma("sp", ccg_in.ap(), Hst[0:1, :], ["HF", "HB"], ["ccg_in"])
        S.add("pool", lambda e: e.collective_compute("AllReduce", ALU.add, replica_groups=PAIRS,
                                                     ins=[ccg_in.ap().opt()], outs=[gate_scr.tensor.ap().opt() if hasattr(gate_scr, "tensor") else gate_scr.opt()]),
              ["ccg_in"], ["gate_scr"], cost=12.0)
        modv = modT.rearrange("p (c j) -> p c j", j=2)
        gs = [ctake(16), ctake(16)]
        sh = [ctake(16), ctake(16)]
        for j in range(2):
            stt(gs[j], modv[:, 16:32, j], 1.0, normg_s, ALU.add, ALU.mult, ["modT", "normg"], ["gs%d" % j])
            cp(sh[j], modv[:, 0:16, j], ["modT"], ["sh%d" % j])

        if "mod" in dbg_out:
            dma("sp", dbg_out["mod"], modT, ["modT"], ["dbg_mod"])

        def alloc_build_set():
            return dict(xb=[A.take(D * 4, F32) for _ in range(2)], xn=[A.take(D * 2, BF16) for _ in range(2)],
                        junk=A.take(D * 2, BF16), stat=[A.take(64, F32) for _ in range(2)])

        def xkeys(pfx):
            def f(t0, n):
                return [pfx + "%d" % t for t in range(t0 // 128, (t0 + n - 1) // 128 + 1)]
            return f

        def build_xmT(XT, xkp, src, ntiles, j, bs):
            xb, xn, junk, stat = bs["xb"], bs["xn"], bs["junk"], bs["stat"]
            for t in range(ntiles):
                b_ = t % 2
                kx, kn, ks = "b_xb%d" % b_, "b_xn%d" % b_, "b_st%d" % b_
                dma("sp", xb[b_], src[t * 128:(t + 1) * 128, :], writes=[kx])
                act(junk, xb[b_], AF.Square, [kx], ["b_junk", ks], accum=stat[b_][:, 0:1])
                ts(stat[b_][:, 1:2], stat[b_][:, 0:1], 1.0 / D, EPS, ALU.mult, ALU.add, [ks], [ks])
                act(stat[b_][:, 2:3], stat[b_][:, 1:2], AF.Sqrt, [ks], [ks])
                S.dve(lambda e, o=stat[b_][:, 3:4], i=stat[b_][:, 2:3]: e.reciprocal(out=o, in_=i), [ks], [ks], cost=0.1)
                ts(xn[b_], xb[b_], stat[b_][:, 3:4], None, ALU.mult, None, [kx, ks], [kn])
                for q4 in range(4):
                    pt = pbf([2, 4][q4 % 2])[:, 0:512]
                    pk = "ps%d" % [2, 4][q4 % 2]
                    for i4 in range(4):
                        kc = q4 * 4 + i4
                        tr(pt[:, i4 * 128:(i4 + 1) * 128], xn[b_][:, kc * 128:(kc + 1) * 128], [kn], [pk])
                    o = XT[:, q4 * 4:q4 * 4 + 4, t * 128:(t + 1) * 128]
                    pv = pt.rearrange("p (c n) -> p c n", c=4)
                    if q4 % 2 == 0:
                        cp(o, pv, [pk], [xkp + "%d" % t])
                    else:
                        act(o, pv, AF.Copy, [pk], [xkp + "%d" % t])

        def modulate(XT, xkp, ntiles, j):
            keys = [xkp + "%d" % t for t in range(ntiles)]
            for kc in range(16):
                o = XT[:, kc, 0:ntiles * 128]
                if kc % 2 == 0:
                    ts(o, o, gs[j][:, kc:kc + 1], sh[j][:, kc:kc + 1], ALU.mult, ALU.add, keys + ["gs%d" % j, "sh%d" % j], keys)
                else:
                    act(o, o, AF.Identity, keys + ["gs%d" % j, "sh%d" % j], keys, bias=sh[j][:, kc:kc + 1], scale=gs[j][:, kc:kc + 1])

        wslot = [0]

        def load_w(wbufs, src_cols, ncols):
            i = wslot[0] % len(wbufs)
            wslot[0] += 1
            wv = wbufs[i][:, :, 0:ncols]
            dma("pool", wv, src_cols.rearrange("(k p) n -> p k n", p=128), writes=["wbuf%d" % i])
            return wv, "wbuf%d" % i

        pacc = [0]

        def inproj_T(wv, wk, c0, nco, XT, xk, t0, nt):
            b_ = pacc[0] % 2
            pacc[0] += 1
            ps = pbank[b_][0:nco, 0:nt]
            for kc in range(16):
                mm(ps, wv[:, kc, c0:c0 + nco], XT[:, kc, t0:t0 + nt], kc == 0, kc == 15, [wk] + xk(t0, nt), ["ps%d" % b_])
            return ps, "ps%d" % b_

        def inproj_tok(wv, wk, c0, nco, XT, xk, t0, ps, pk):
            for kc in range(16):
                mm(ps, XT[:, kc, t0:t0 + 128], wv[:, kc, c0:c0 + nco], kc == 0, kc == 15, [wk] + xk(t0, 128), [pk])

        def pieces(ntok):
            npc = (ntok + 511) // 512
            sz = ((ntok + npc - 1) // npc + 1) // 2 * 2
            r = []
            t = 0
            while t < ntok:
                n = min(sz, ntok - t)
                r.append((t, n))
                t += n
            return r

        def conv_chunk(wv, wk, c0, XT, xk, ntok, has_halo, cc, taps, outT, outk, stage_f, tmp_f, tb=0, left=False):
            if not left:
                memset(stage_f[:, 0:1], 0.0, ["stage"])
            if not has_halo:
                memset(stage_f[:, ntok + 1:ntok + 2], 0.0, ["stage"])
            lo = tb - (1 if left else 0)
            ntot = ntok + (1 if has_halo else 0) + (1 if left else 0)
            so = 0 if left else 1
            for (t0, n) in pieces(ntot):
                ps, pk = inproj_T(wv, wk, c0, 128, XT, xk, lo + t0, n)
                act(stage_f[:, so + t0:so + t0 + n], ps, AF.Copy, [pk], ["stage"])
            cw = conv_s.rearrange("p (c f) -> p c f", f=4)
            wA = cw[:, cc, taps[0]:taps[0] + 1]
            wB = cw[:, cc, taps[1]:taps[1] + 1]
            wC = cw[:, cc, taps[2]:taps[2] + 1]
            bb = cw[:, cc, 3:4]
            ts(tmp_f[:, 0:ntok], stage_f[:, 0:ntok], wA, None, ALU.mult, None, ["stage", "conv"], ["ctmp"])
            stt(tmp_f[:, 0:ntok], stage_f[:, 1:ntok + 1], wB, tmp_f[:, 0:ntok], ALU.mult, ALU.add, ["stage", "conv", "ctmp"], ["ctmp"])
            stt(tmp_f[:, 0:ntok], stage_f[:, 2:ntok + 2], wC, tmp_f[:, 0:ntok], ALU.mult, ALU.add, ["stage", "conv", "ctmp"], ["ctmp"])
            act(outT[:, tb:tb + ntok], tmp_f[:, 0:ntok], AF.Silu, ["ctmp", "conv"], [outk], bias=bb)

        def to_tok(srcT, srck, nchunks, dst, dstk, c_off):
            for c0 in range(0, nchunks, 4):
                n = min(4, nchunks - c0)
                hb = (c0 // 4) % 2
                pt = pbf([2, 4][hb])[:, 0:n * 128]
                pk = "ps%d" % [2, 4][hb]
                for i in range(n):
                    tr(pt[:, i * 128:(i + 1) * 128], srcT[:, (c0 + i) * 128:(c0 + i + 1) * 128], [srck], [pk])
                cp(dst[:, c0:c0 + n, c_off:c_off + 128], pt.rearrange("p (c n) -> p c n", n=128), [pk], [dstk], eng="dve")

        def alloc_dt_set(nch):
            n = nch * 16
            return [A.take(n * 4, F32) for _ in range(5)]

        def dt_prep(wdt_v, wdtk, dcol, XT, xk, nch, dirn, tg, bufs):
            n = nch * 16
            dtv, adt, wgt, dec, tmp = [b[:, 0:n] for b in bufs]
            ps = pbank[3][:, 0:n]
            for c in range(nch):
                inproj_tok(wdt_v, wdtk, dcol, 16, XT, xk, c * 128, ps[:, c * 16:(c + 1) * 16], "ps3")
            v3 = lambda a: a.rearrange("p (c h) -> p c h", h=16)
            if CUT2 < 1:
                cp(tmp, ps, ["ps3"], [tg + "tmp"])
                return None
            tt(v3(tmp), v3(ps), dtb_s[:, dcol:dcol + 16].rearrange("p (o h) -> p o h", o=1).to_broadcast([128, nch, 16]), ALU.add,
               ["ps3", "dtb"], [tg + "tmp"])
            if CUT2 < 2:
                return None
            act(tmp, tmp, AF.Exp, [tg + "tmp"], [tg + "tmp"])
            if CUT2 < 3:
                return None
            act(dtv, tmp, AF.Ln, [tg + "tmp"], [tg + "dt"], bias=1.0)
            if CUT2 < 4:
                return None
            tt(v3(adt), v3(dtv), a_s[:, dcol:dcol + 16].rearrange("p (o h) -> p o h", o=1).to_broadcast([128, nch, 16]), ALU.mult,
               [tg + "dt", "a_s"], [tg + "adt"])
            if CUT2 < 5:
                return None
            pcs = pbank[3][:, 0:n]
            mm(pcs, TLEf if dirn == "F" else TGEf, adt, True, True, ["cf", tg + "adt"], ["ps3"])
            cp(tmp, pcs, ["ps3"], [tg + "tmp"])
            if CUT2 < 6:
                return None
            ptot = pbank[3][:, 0:n]
            mm(ptot, onesf, adt, True, True, ["onesf", tg + "adt"], ["ps3"])
            if CUT2 < 7:
                cp(tmp, ptot, ["ps3"], [tg + "tmp"])
                return None
            act(dec, ptot, AF.Exp, ["ps3"], [tg + "dec"])
            tt(tmp, ptot, tmp, ALU.subtract, ["ps3", tg + "tmp"], [tg + "tmp"])
            act(tmp, tmp, AF.Exp, [tg + "tmp"], [tg + "tmp"])
            tt(wgt, tmp, dtv, ALU.mult, [tg + "tmp", tg + "dt"], [tg + "wgt"])
            return dict(dt=v3(dtv), adt=v3(adt), wgt=v3(wgt), dec=v3(dec))

        def alloc_scan_set(ntok):
            nch = ntok // 128
            return dict(wdt=A.take(16 * 32 * 2, BF16).rearrange("p (k n) -> p k n", k=16), dt=alloc_dt_set(nch),
                        wb=[A.take(16 * 128 * 2, BF16).rearrange("p (k n) -> p k n", k=16) for _ in range(2)],
                        stage=[A.take((ntok + 2) * 4, F32)], tmp=[A.take(ntok * 4, F32)],
                        cT=A.take(ntok * 2, BF16),
                        xs_tok=A.take(nch * 256 * 2, BF16).rearrange("p (c n) -> p c n", n=256),
                        b_tok=A.take(nch * 128 * 2, BF16).rearrange("p (c n) -> p c n", n=128),
                        xsw=[A.take(256 * 2, BF16) for _ in range(2)])

        def scanF(XT, xk, ntok, has_halo, taps, dcol, Hst, Hk, ss):
            nch = ntok // 128
            wdt_v = ss["wdt"]
            dma("pool", wdt_v, w_dt.rearrange("(k p) n -> p k n", p=128), writes=["s_wdt"])
            dd = dt_prep(wdt_v, "s_wdt", dcol, XT, xk, nch, "F", "s_d", ss["dt"])
            wb = ss["wb"]
            stage_f, tmp_f, cT = ss["stage"][0], ss["tmp"][0], ss["cT"]
            xs_tok = ss["xs_tok"][:, 0:nch, :]
            b_tok = ss["b_tok"][:, 0:nch, :]
            for g in range(4):
                wv, wk = load_w(wb, w_in[:, OFF_B + g * 128:OFF_B + (g + 1) * 128], 128)
                conv_chunk(wv, wk, 0, XT, xk, ntok, has_halo, 8 + g, taps, cT, "s_cT", stage_f, tmp_f)
                to_tok(cT, "s_cT", nch, b_tok, "s_btok", 0)
                for pr in range(2):
                    wv, wk = load_w(wb, w_in[:, OFF_XS + g * 256 + pr * 128:OFF_XS + g * 256 + (pr + 1) * 128], 128)
                    conv_chunk(wv, wk, 0, XT, xk, ntok, has_halo, g * 2 + pr, taps, cT, "s_cT", stage_f, tmp_f)
                    to_tok(cT, "s_cT", nch, xs_tok, "s_xstok", pr * 128)
                Hg = Hst[:, g * 256:(g + 1) * 256]
                for c in range(nch):
                    xsw = ss["xsw"][c % 2]
                    xswk = "s_xsw%d" % (c % 2)
                    tt(xsw.rearrange("p (h d) -> p h d", d=64), xs_tok[:, c, :].rearrange("p (h d) -> p h d", d=64),
                       dd["wgt"][:, c, g * 4:(g + 1) * 4].rearrange("p (h o) -> p h o", o=1).to_broadcast([128, 4, 64]), ALU.mult,
                       ["s_xstok", "s_dwgt"], [xswk])
                    pst = pbank[7][:, 0:256]
                    mm(pst, b_tok[:, c, :], xsw, True, True, ["s_btok", xswk], ["ps7"])
                    tt(Hg.rearrange("p (h d) -> p h d", d=64), Hg.rearrange("p (h d) -> p h d", d=64),
                       dd["dec"][:, c, g * 4:(g + 1) * 4].rearrange("p (h o) -> p h o", o=1).to_broadcast([128, 4, 64]), ALU.mult,
                       [Hk, "s_ddec"], [Hk])
                    tt(Hg, Hg, pst, ALU.add, [Hk, "ps7"], [Hk])

        HF = Hst[:, 0:1024]
        HB = Hst[:, 1024:2048]
        memset(HF, 0.0, ["HF"])
        kctxT = st.enter_context(nc.sbuf_tensor("kctxT", [128, 4 * 256], BF16))
        kctxB = st.enter_context(nc.sbuf_tensor("kctxB", [128, 4 * 256], BF16))
        memset(kctxT[64:128, :], 0.0, ["kctxT"])
        memset(kctxB[0:64, :], 0.0, ["kctxB"])
        vctx = st.enter_context(nc.sbuf_tensor("vctx", [128, 2 * 256], BF16))
        TAPS_F = (0, 1, 2)
        TAPS_R = (2, 1, 0)
        m2 = A.mark()
        XT = A.take(16 * NTH * 2, BF16).rearrange("p (k n) -> p k n", k=16)
        XK = xkeys("XTl")
        m3 = A.mark()
        bset = alloc_build_set()
        XTc = A.take(16 * NCTX * 2, BF16).rearrange("p (k n) -> p k n", k=16)
        sset = alloc_scan_set(NCTX)
        assert A.off <= ARENA - 41984
        xkc = xkeys("XTc")
        build_xmT(XTc, "XTc", ctxf, 2, 1, bset)
        build_xmT(XT, "XTl", xloc, 17, 0, bset)
        modulate(XTc, "XTc", 2, 1)
        modulate(XT, "XTl", 17, 0)
        scanF(XTc, xkc, NCTX, False, TAPS_F, 0, HF, "HF", sset)
        wkv = sset["wb"]
        for g in range(4):
            wv, wk = load_w(wkv, w_in[:, OFF_K + g * 64:OFF_K + (g + 1) * 64], 64)
            wi = (wslot[0] - 1) % 2
            dma("pool", wkv[wi][:, :, 64:128], w_in[:, OFF_K + g * 64:OFF_K + (g + 1) * 64].rearrange("(k p) n -> p k n", p=128),
                writes=[wk])
            ps, pk = inproj_T(wkv[wi][:, :, 0:128], wk, 0, 128, XTc, xkc, 0, NCTX)
            cp(kctxT[0:64, g * 256:(g + 1) * 256], ps[0:64, :], [pk], ["kctxT"])
            cp(kctxB[64:128, g * 256:(g + 1) * 256], ps[64:128, :], [pk], ["kctxB"])
        for hv in range(2):
            wv, wk = load_w(wkv, w_in[:, OFF_V + hv * 128:OFF_V + (hv + 1) * 128], 128)
            for t in range(2):
                psv = pbank[7][:, 0:128]
                inproj_tok(wv, wk, 0, 128, XTc, xkc, t * 128, psv, "ps7")
                cp(vctx[:, t * 256 + hv * 128:t * 256 + (hv + 1) * 128], psv, ["ps7"], ["vctx"])
        A.reset(m3)

        if "H" in dbg_out:
            dma("sp", dbg_out["H"], Hst[:, :], ["HF", "HB"], ["dbg_H"])
        if "kctx" in dbg_out and stage >= 2:
            t_ = A.take(1024 * 4, F32)
            cp(t_, kctxT[:, :], ["kctxT"], ["t_"])
            dma("sp", dbg_out["kctx"], t_, ["t_"], ["dbg_kctx"])

        if stage >= 5:
            wdt_v = A.take(16 * 32 * 2, BF16).rearrange("p (k n) -> p k n", k=16)
            dma("pool", wdt_v, w_dt.rearrange("(k p) n -> p k n", p=128), writes=["l_wdt"])
            dF = dt_prep(wdt_v, "l_wdt", 0, XT, XK, 16, "F", "lF", alloc_dt_set(16))
            dB = dt_prep(wdt_v, "l_wdt", 16, XT, XK, 16, "B", "lB", alloc_dt_set(16))
            wb = [A.take(16 * 128 * 2, BF16).rearrange("p (k n) -> p k n", k=16) for _ in range(2)]
            BTs = [A.take(NT * 2, BF16) for _ in range(2)]
            CT = A.take(NT * 2, BF16)
            xs_toks = [A.take(16 * 256 * 2, BF16).rearrange("p (c n) -> p c n", n=256) for _ in range(2)]
            b_toks = [A.take(16 * 128 * 2, BF16).rearrange("p (c n) -> p c n", n=128) for _ in range(2)]
            prevFas = [A.take(16 * 256 * 2, BF16).rearrange("p (c n) -> p c n", n=256) for _ in range(2)]
            szT = A.take(2 * NT * 2, BF16).rearrange("p (c n) -> p c n", c=2)
            prevB = A.take(16 * 256 * 2, BF16).rearrange("p (c n) -> p c n", n=256)
            xswF = [A.take(256 * 2, BF16) for _ in range(2)]
            ccin_s = A.take(256 * 4, F32)
            ccout_s = A.take(256 * 4, F32)
            Dsk = A.take(4 * 128 * 2, BF16).rearrange("p (h n) -> p h n", h=4)
            HALF = 1024
            stage_f = A.take((HALF + 2) * 4, F32)
            tmp_f = A.take(HALF * 4, F32)
            xsT = A.take(NT * 2, BF16)
            lhs4 = A.take(512 * 4, F32).rearrange("p (h n) -> p h n", h=4)
            L4 = A.take(512 * 2, BF16).rearrange("p (h n) -> p h n", h=4)
            E4 = A.take(512 * 2, BF16).rearrange("p (h n) -> p h n", h=4)
            G4 = [A.take(512 * 2, BF16).rearrange("p (h n) -> p h n", h=4) for _ in range(2)]
            GE4 = [A.take(512 * 2, BF16).rearrange("p (h n) -> p h n", h=4) for _ in range(2)]
            CBm2 = A.take(256 * 2, BF16).rearrange("p (d n) -> p d n", d=2)
            lhs4s = [lhs4, A.take(512 * 4, F32).rearrange("p (h n) -> p h n", h=4)]
            xdt = [A.take(256 * 2, BF16) for _ in range(2)]
            xsw = A.take(256 * 2, BF16)
            ystg = [A.take(256 * 2, BF16).rearrange("p (c n) -> p c n", c=2) for _ in range(2)]
            h3 = lambda a: a.rearrange("p (h d) -> p h d", d=64)
            bch = lambda a, c, g: a[:, c, g * 4:(g + 1) * 4].rearrange("p (h o) -> p h o", o=1).to_broadcast([128, 4, 64])
            for g in range(4):
                par = g % 2
                BT, xs_tok, b_tok, prevFa = BTs[par], xs_toks[par], b_toks[par], prevFas[par]
                kBT, kxs, kbt, kpf, kHB = "BT%d" % par, "xstok%d" % par, "btok%d" % par, "prevFa%d" % par, "HB%d" % par
                HFg = HF[:, g * 256:(g + 1) * 256]
                HBg = HB[:, par * 256:(par + 1) * 256]
                for hf in range(2):
                    tb = hf * HALF
                    wv, wk = load_w(wb, w_in[:, OFF_B + g * 128:OFF_B + (g + 1) * 128], 128)
                    conv_chunk(wv, wk, 0, XT, XK, HALF, True, 8 + g, TAPS_F, BT, kBT, stage_f, tmp_f, tb=tb, left=(hf == 1))
                to_tok(BT, kBT, 16, b_tok, kbt, 0)
                for pr in range(2):
                    for hf in range(2):
                        tb = hf * HALF
                        wv, wk = load_w(wb, w_in[:, OFF_XS + g * 256 + pr * 128:OFF_XS + g * 256 + (pr + 1) * 128], 128)
                        conv_chunk(wv, wk, 0, XT, XK, HALF, True, g * 2 + pr, TAPS_F, xsT, "xsT", stage_f, tmp_f, tb=tb, left=(hf == 1))
                    to_tok(xsT, "xsT", 16, xs_tok, kxs, pr * 128)
                for c in range(16):
                    act(prevFa[:, c, :], HFg, AF.Copy, ["HF"], [kpf])
                    xw = xswF[c % 2]
                    xwk = "xswF%d" % (c % 2)
                    tt(h3(xw), h3(xs_tok[:, c, :]), bch(dF["wgt"], c, g), ALU.mult, [kxs, "lFwgt"], [xwk])
                    mm(pbank[5][:, 0:256], b_tok[:, c, :], xw, True, True, [kbt, xwk], ["ps5"])
                    tt(h3(HFg), h3(HFg), bch(dF["dec"], c, g), ALU.mult, ["HF", "lFdec"], ["HF"])
                    tt(HFg, HFg, pbank[5][:, 0:256], ALU.add, ["HF", "ps5"], ["HF"])
                cp(ccin_s, HFg, ["HF"], ["ccin_s"])
                dma("sp", cc_in[g].ap(), ccin_s, ["ccin_s"], ["cc_in%d" % g])
                S.add("pool", lambda e, g=g: e.collective_compute("AllReduce", ALU.add, replica_groups=[[2 * i, 2 * i + 1] for i in range(ncores // 2)],
                                                                  ins=[cc_in[g].ap().opt()], outs=[cc_out[g].ap().opt()]),
                      ["cc_in%d" % g], ["cc_out%d" % g], cost=25.0)
                dma("sp", ccout_s, cc_out[g].ap(), ["cc_out%d" % g], ["ccout_s"])
                tt(HBg, ccout_s, ccin_s, ALU.subtract, ["ccout_s", "ccin_s"], [kHB])
                for hf in range(2):
                    tb = hf * HALF
                    wv, wk = load_w(wb, w_in[:, OFF_C + g * 128:OFF_C + (g + 1) * 128], 128)
                    conv_chunk(wv, wk, 0, XT, XK, HALF, True, 12 + g, TAPS_F, CT, "CT", stage_f, tmp_f, tb=tb, left=(hf == 1))
                for pr in range(2):
                    wv, wk = load_w(wb, w_in[:, OFF_Z + g * 256 + pr * 128:OFF_Z + g * 256 + (pr + 1) * 128], 128)
                    for (t0, n) in pieces(NT):
                        ps, pk = inproj_T(wv, wk, 0, 128, XT, XK, t0, n)
                        act(szT[:, pr, t0:t0 + n], ps, AF.Silu, [pk], ["szT"])
                for hh in range(4):
                    ts(Dsk[:, hh, :], ident_f, dskb_s[:, g * 4 + hh:g * 4 + hh + 1], None, ALU.mult, None, ["cf", "dskb"], ["Dsk"])
                for c in range(15, -1, -1):
                    act(prevB[:, c, :], HBg, AF.Copy, [kHB], ["prevB"])
                    tt(h3(xsw), h3(xs_tok[:, c, :]), bch(dB["wgt"], c, g), ALU.mult, [kxs, "lBwgt"], ["xsw"])
                    mm(pbank[5][:, 256:512], b_tok[:, c, :], xsw, True, True, [kbt, "xsw"], ["ps5"])
                    tt(h3(HBg), h3(HBg), bch(dB["dec"], c, g), ALU.mult, [kHB, "lBdec"], [kHB])
                    tt(HBg, HBg, pbank[5][:, 256:512], ALU.add, [kHB, "ps5"], [kHB])
                for c in range(16):
                    cs_ = slice(c * 128, (c + 1) * 128)
                    mm(pbank[5][:, 0:128], BT[:, cs_], CT[:, cs_], True, True, [kBT, "CT"], ["ps5"])
                    tt(CBm2, pbank[5][:, 0:128].rearrange("p (o n) -> p o n", o=1).to_broadcast([128, 2, 128]),
                       cf[:, 128:384].rearrange("p (d n) -> p d n", d=2), ALU.mult, ["ps5", "cf"], ["CBm"])
                    for di, (dd_, msk, tri) in enumerate([(dF, SUf, TLEf), (dB, SLf, TGEf)]):
                        pD, pE = 6, 7
                        dk = "lF" if di == 0 else "lB"
                        adt4 = dd_["adt"][:, c, g * 4:(g + 1) * 4]
                        lh = lhs4s[di]
                        if True:
                            tt(lh, msk.rearrange("p (o n) -> p o n", o=1).to_broadcast([128, 4, 128]),
                               adt4.rearrange("p (h o) -> p h o", o=1).to_broadcast([128, 4, 128]), ALU.mult, ["cf", dk + "adt"], ["lhs4%d" % di])
                        else:
                            for hh in range(4):
                                act(lh[:, hh, :], msk, AF.Copy, ["cf", dk + "adt"], ["lhs4%d" % di], scale=adt4[:, hh:hh + 1])
                        for hh in range(4):
                            mm(pbank[pD][:, hh * 128:(hh + 1) * 128], lh[:, hh, :], tri, True, True, ["lhs4%d" % di, "cf"], ["ps%d" % pD])
                        act(L4, pbank[pD][:, :].rearrange("p (h n) -> p h n", h=4), AF.Exp, ["ps%d" % pD], ["L4"])
                        tt(G4[di], L4, CBm2[:, di:di + 1, :].to_broadcast([128, 4, 128]), ALU.mult,
                           ["L4", "CBm"], ["G4%d" % di])
                        for hh in range(4):
                            mm(pbank[pE][:, hh * 128:(hh + 1) * 128], adt4[:, hh:hh + 1].to_broadcast([128, 128]), tri, True, True,
                               [dk + "adt", "cf"], ["ps%d" % pE])
                        act(E4, pbank[pE][:, :].rearrange("p (h n) -> p h n", h=4), AF.Exp, ["ps%d" % pE], ["E4"])
                        tt(GE4[di], E4, CT[:, cs_].rearrange("p (o n) -> p o n", o=1).to_broadcast([128, 4, 128]), ALU.mult,
                           ["E4", "CT"], ["GE4%d" % di])
                        tt(h3(xdt[di]), h3(xs_tok[:, c, :]), bch(dd_["dt"], c, g), ALU.mult, [kxs, dk + "dt"], ["xdt%d" % di])
                    for r in range(2):
                        yo = pbank[3][:, r * 256:(r + 1) * 256]
                        hs = slice(r * 128, (r + 1) * 128)
                        h2 = slice(2 * r, 2 * r + 2)
                        mm(yo, xdt[0][:, hs], G4[0][:, h2, :], True, False, ["xdt0", "G40"], ["ps3"])
                        mm(yo, prevFa[:, c, hs], GE4[0][:, h2, :], False, False, [kpf, "GE40"], ["ps3"])
                        mm(yo, xdt[1][:, hs], G4[1][:, h2, :], False, False, ["xdt1", "G41"], ["ps3"])
                        mm(yo, prevB[:, c, hs], GE4[1][:, h2, :], False, False, ["prevB", "GE41"], ["ps3"])
                        mm(yo, xs_tok[:, c, hs], Dsk[:, h2, :], False, True, [kxs, "Dsk"], ["ps3"])
                    yk = "ystg%d" % (c % 2)
                    for b_ in range(2):
                        rw = slice(b_ * 64, (b_ + 1) * 64)
                        yv = pbank[3][rw, :].rearrange("p (r b n) -> p r b n", r=2, b=2)[:, :, b_, :]
                        tt(ystg[c % 2][rw, :, :], yv, szT[rw, :, cs_], ALU.mult, ["ps3", "szT"], [yk])
                    dma("sp", h_scr[c][:, 2 * g:2 * g + 2, :], ystg[c % 2], [yk], ["h_scr"])
            A.reset(m3)

            if stage >= 6:
                agT = A.take(8 * NT * 2, BF16).rearrange("p (c n) -> p c n", c=8)
                m4 = A.mark()
                A.off = m2
                wo = A.take(16 * D * 2, BF16).rearrange("p (k n) -> p k n", k=16)
                assert A.off <= m3
                A.off = m4
                wb = [A.take(16 * 128 * 2, BF16).rearrange("p (k n) -> p k n", k=16) for _ in range(2)]
                kTs = [A.take(NTH * 2, BF16) for _ in range(2)]
                kTBs = [A.take(NTH * 2, BF16) for _ in range(2)]
                qTs = [A.take(2 * NT * 2, BF16).rearrange("p (a n) -> p a n", a=2) for _ in range(2)]
                sgTs = [A.take(2 * NT * 2, BF16).rearrange("p (a n) -> p a n", a=2) for _ in range(2)]
                VA = A.take(19 * 128 * 2, BF16).rearrange("p (t n) -> p t n", n=128)
                VB = A.take(19 * 128 * 2, BF16).rearrange("p (t n) -> p t n", n=128)
                cos_b = A.take(NTH * 2, BF16)
                ssin_b = A.take(NTH * 2, BF16)
                qraw = A.take(512 * 2, BF16)
                rt1 = Hst[:, 1024:1536]
                rt2 = Hst[:, 1536:2048]
                PT = A.take(5 * 512 * 2, BF16).rearrange("p (c n) -> p c n", c=5)
                lnd = [Hst[:, 0:256], Hst[:, 256:512]]
                rd = lnd
                t1 = [Hst[:, 512:768], Hst[:, 768:1024]]
                for par in range(2):
                    memset(kTs[par][64:128, :], 0.0, ["kT%d" % par])
                    memset(kTBs[par][0:64, :], 0.0, ["kTB%d" % par])
                memset(VA[:, :, 64:128], 1.0, ["VA"])
                memset(VB[:, :, 0:64], 1.0, ["VB"])
                dma("pool", cos_b, cosT, writes=["cos_b"])
                dma("pool", ssin_b, ssinT, writes=["ssin_b"])

                def rope_proj(wv2, wk, ntok, dsts):
                    for (t0, n) in pieces(ntok):
                        ps, pk = inproj_T(wv2, wk, 0, 128, XT, XK, t0, n)
                        act(qraw[:, 0:n], ps, AF.Copy, [pk], ["qraw"])
                        mm(pbank[5][:, 0:n], perm_b, qraw[:, 0:n], True, True, ["cb16", "qraw"], ["ps5"])
                        tt(rt1[:, 0:n], qraw[:, 0:n], cos_b[:, t0:t0 + n], ALU.mult, ["qraw", "cos_b"], ["rt1"])
                        tt(rt2[:, 0:n], pbank[5][:, 0:n], ssin_b[:, t0:t0 + n], ALU.mult, ["ps5", "ssin_b"], ["rt2"])
                        for (dst, rw, dstk) in dsts:
                            tt(dst[rw, t0:t0 + n], rt1[rw, 0:n], rt2[rw, 0:n], ALU.add, ["rt1", "rt2"], [dstk])

                for g in range(4):
                    par = g % 2
                    kT, kTB, qT, sgT = kTs[par], kTBs[par], qTs[par], sgTs[par]
                    kkT, kkTB, kqT, ksg = "kT%d" % par, "kTB%d" % par, "qT%d" % par, "sgT%d" % par
                    wv, wk = load_w(wb, w_in[:, OFF_K + g * 64:OFF_K + (g + 1) * 64], 64)
                    wi = (wslot[0] - 1) % 2
                    dma("pool", wb[wi][:, :, 64:128], w_in[:, OFF_K + g * 64:OFF_K + (g + 1) * 64].rearrange("(k p) n -> p k n", p=128),
                        writes=[wk])
                    rope_proj(wb[wi][:, :, 0:128], wk, NTH, [(kT, slice(0, 64), kkT), (kTB, slice(64, 128), kkTB)])
                    wv, wk = load_w(wb, w_in[:, OFF_V + g * 64:OFF_V + (g + 1) * 64], 64)
                    for t4 in range(0, 17, 4):
                        nt_ = min(4, 17 - t4)
                        for i in range(nt_):
                            inproj_tok(wv, wk, 0, 64, XT, XK, (t4 + i) * 128, pbank[5][:, i * 64:(i + 1) * 64], "ps5")
                        pv = pbank[5][:, 0:nt_ * 64].rearrange("p (t n) -> p t n", n=64)
                        cp(VA[:, t4:t4 + nt_, 0:64], pv, ["ps5"], ["VA"])
                        act(VB[:, t4:t4 + nt_, 64:128], pv, AF.Copy, ["ps5"], ["VB"])
                    vc3 = vctx[:, :].rearrange("p (t n) -> p t n", t=2)
                    cp(VA[:, 17:19, 0:64], vc3[:, :, g * 64:(g + 1) * 64], ["vctx"], ["VA"])
                    cp(VB[:, 17:19, 64:128], vc3[:, :, g * 64:(g + 1) * 64], ["vctx"], ["VB"])
                    for a in range(2):
                        wv, wk = load_w(wb, w_in[:, OFF_Q + (4 * g + 2 * a) * 64:OFF_Q + (4 * g + 2 * a + 2) * 64], 128)
                        rope_proj(wv, wk, NT, [(qT[:, a, :], slice(0, 128), kqT)])
                        wv, wk = load_w(wb, w_in[:, OFF_G + (4 * g + 2 * a) * 64:OFF_G + (4 * g + 2 * a + 2) * 64], 128)
                        for (t0, n) in pieces(NT):
                            ps, pk = inproj_T(wv, wk, 0, 128, XT, XK, t0, n)
                            act(sgT[:, a, t0:t0 + n], ps, AF.Silu, [pk], [ksg])
                    if g == 3:
                        xall = ["XTl%d" % t for t in range(17)]
                        for jb in range(4):
                            dma("pool", wo[:, :, jb * 512:(jb + 1) * 512],
                                w_out[:, jb * 512:(jb + 1) * 512].rearrange("(k p) n -> p k n", p=128), writes=["wo%d" % jb] + xall)
                        for jb in range(4):
                            for ch in range(16):
                                gv = attg_s[:, ch:ch + 1] if ch < 8 else ssdg_s[:, ch - 8:ch - 7]
                                ts(wo[:, ch, jb * 512:(jb + 1) * 512], wo[:, ch, jb * 512:(jb + 1) * 512], gv, None, ALU.mult, None,
                                   ["wo%d" % jb, "attg", "ssdg"], ["wo%d" % jb])
                    for n in range(16):
                        chunks = []
                        if n > 0:
                            chunks.append((n - 1, "k", TGEb))
                        chunks.append((n, "k", None))
                        chunks.append((n + 1, "k", TLEb))
                        chunks.append((17, "c", None))
                        chunks.append((18, "c", None))
                        qs = slice(n * 128, (n + 1) * 128)
                        for ci, (tile, kind, msk) in enumerate(chunks):
                            pb_ = 6 + ci % 2
                            for b_ in range(2):
                                if kind == "k":
                                    lk = (kT if b_ == 0 else kTB)[:, tile * 128:(tile + 1) * 128]
                                    lkk = kkT if b_ == 0 else kkTB
                                else:
                                    lk = (kctxT if b_ == 0 else kctxB)[:, g * 256 + (tile - 17) * 128:g * 256 + (tile - 16) * 128]
                                    lkk = "kctxT" if b_ == 0 else "kctxB"
                                mm(pbank[pb_][:, b_ * 256:(b_ + 1) * 256], lk, qT[:, :, qs], True, True, [lkk, kqT], ["ps%d" % pb_])
                            act(PT[:, ci, :], pbank[pb_][:, :], AF.Exp, ["ps%d" % pb_], ["PT%d" % ci], scale=0.125)
                            if msk is not None:
                                p4 = PT[:, ci, :].rearrange("p (h n) -> p h n", h=4)
                                tt(p4, p4, msk.rearrange("p (o n) -> p o n", o=1).to_broadcast([128, 4, 128]), ALU.mult,
                                   ["PT%d" % ci, "cb16"], ["PT%d" % ci])
                        nci = len(chunks)
                        for ci, (tile, kind, msk) in enumerate(chunks):
                            mm(pbank[2][:, 0:256], VA[:, tile, :], PT[:, ci, 0:256], ci == 0, ci == nci - 1, ["VA", "PT%d" % ci], ["ps2"])
                        for ci, (tile, kind, msk) in enumerate(chunks):
                            mm(pbank[4][:, 0:256], VB[:, tile, :], PT[:, ci, 256:512], ci == 0, ci == nci - 1, ["VB", "PT%d" % ci], ["ps4"])
                        for b_ in range(2):
                            po = pbank[2] if b_ == 0 else pbank[4]
                            pk = "ps2" if b_ == 0 else "ps4"
                            nr = slice(b_ * 64, (b_ + 1) * 64)
                            dr = slice((1 - b_) * 64, (2 - b_) * 64)
                            for a_ in range(2):
                                h = 4 * g + 2 * a_ + b_
                                act(lnd[b_][dr, a_ * 128:(a_ + 1) * 128], po[dr, a_ * 128:(a_ + 1) * 128], AF.Ln, [pk, "esink"], ["lnd%d" % b_],
                                    bias=esink[dr, h:h + 1])
                            act(rd[b_][dr, :], lnd[b_][dr, :], AF.Exp, ["lnd%d" % b_], ["rd%d" % b_], scale=-1.0)
                            tt(t1[b_][nr, :], po[nr, 0:256], rd[b_][dr, :], ALU.mult, [pk, "rd%d" % b_], ["t1%d" % b_])
                            tt(agT[nr, 2 * g:2 * g + 2, qs], t1[b_][nr, :].rearrange("p (a n) -> p a n", a=2), sgT[nr, :, qs], ALU.mult,
                               ["t1%d" % b_, ksg], ["agT"])
                A.reset(m4)
                if "ag" in dbg_out:
                    for ch in range(4):
                        t_ = A.take(NT * 4, F32)
                        cp(t_, agT[:, ch, :], ["agT"], ["t_ag%d" % ch])
                        dma("sp", dbg_out["ag"][ch], t_, ["t_ag%d" % ch], ["dbg_ag"])
                    A.reset(m4)
                    for ch in range(4, 8):
                        t_ = A.take(NT * 4, F32)
                        cp(t_, agT[:, ch, :], ["agT"], ["t_ag%d" % ch])
                        dma("sp", dbg_out["ag"][ch], t_, ["t_ag%d" % ch], ["dbg_ag"])
                    A.reset(m4)

            if stage >= 7:
                S.barrier()
                A.off = m2
                A.off = m4
                gate_bc = A.take(D * 4, F32)
                fng_bc = A.take(D * 4, F32)
                xt_ = [A.take(D * 4, F32) for _ in range(2)]
                res = A.take(D * 4, F32)
                o1 = [A.take(512 * 4, F32) for _ in range(2)]
                sq = [A.take(128 * 2, BF16) for _ in range(2)]
                st2 = [A.take(64, F32) for _ in range(2)]
                ytile = [A.take(8 * 128 * 2, BF16) for _ in range(2)]
                dma("sp", gate_bc, gate_scr.to_broadcast([128, D]), ["gate_scr"], ["gate_bc"])
                dma("sp", fng_bc, fng.to_broadcast([128, D]), writes=["fng_bc"])
                ones2 = TLEb[:, 126:128]
                for t in range(16):
                    b_ = t % 2
                    tsl = slice(t * 128, (t + 1) * 128)
                    xk_ = "xt%d" % b_
                    sk_ = "st2%d" % b_
                    dma("sp", xt_[b_], xloc[tsl, :], writes=[xk_])
                    dma("sp", ytile[b_], h_scr[t].rearrange("p c n -> p (c n)"), ["h_scr"], ["yt%d" % b_])
                    ygT = ytile[b_].rearrange("p (c n) -> p c n", c=8)
                    for br, (src, srck, sl_) in enumerate([(agT, "agT", tsl), (ygT, "yt%d" % b_, slice(0, 128))]):
                        for ch in range(8):
                            act(sq[ch % 2], src[:, ch, sl_], AF.Square, [srck], ["sq%d" % (ch % 2)])
                            mm(pbank[3][:, br * 2:br * 2 + 2], sq[ch % 2], ones2, ch == 0, ch == 7, ["sq%d" % (ch % 2), "cb16"], ["ps3"])
                    s_ = st2[b_]
                    ts(s_[:, 0:4], pbank[3][:, 0:4], 1.0 / 1024.0, EPS, ALU.mult, ALU.add, ["ps3"], [sk_])
                    act(s_[:, 0:4], s_[:, 0:4], AF.Sqrt, [sk_], [sk_])
                    S.dve(lambda e, o=s_[:, 4:8], i=s_[:, 0:4]: e.reciprocal(out=o, in_=i), [sk_], [sk_])
                    for cb in range(4):
                        cs_ = slice(cb * 512, (cb + 1) * 512)
                        pa, pka = (pbank[0], "ps0") if cb % 2 == 0 else (pbank[6], "ps6")
                        pss, pks = (pbank[1], "ps1") if cb % 2 == 0 else (pbank[7], "ps7")
                        for ch in range(8):
                            mm(pa[:, :], agT[:, ch, tsl], wo[:, ch, cs_], ch == 0, ch == 7, ["agT", "wo%d" % cb], [pka])
                        for ch in range(8):
                            mm(pss[:, :], ygT[:, ch, :], wo[:, 8 + ch, cs_], ch == 0, ch == 7, ["yt%d" % b_, "wo%d" % cb], [pks])
                        ok_ = "o1%d" % (cb % 2)
                        oo = o1[cb % 2]
                        act(oo, pa[:, :], AF.Identity, [pka, sk_], [ok_], scale=s_[:, 5:6])
                        stt(oo, pss[:, :], s_[:, 7:8], oo, ALU.mult, ALU.add, [pks, sk_, ok_], [ok_])
                        tt(oo, oo, gate_bc[:, cs_], ALU.mult, [ok_, "gate_bc"], [ok_])
                        tt(res[:, cs_], oo, xt_[b_][:, cs_], ALU.add, [ok_, xk_], ["res"])
                    act(xt_[b_], res, AF.Square, ["res"], [xk_, sk_], accum=s_[:, 8:9])
                    ts(s_[:, 9:10], s_[:, 8:9], 1.0 / D, EPS, ALU.mult, ALU.add, [sk_], [sk_])
                    act(s_[:, 10:11], s_[:, 9:10], AF.Sqrt, [sk_], [sk_])
                    S.dve(lambda e, o=s_[:, 11:12], i=s_[:, 10:11]: e.reciprocal(out=o, in_=i), [sk_], [sk_])
                    stt(xt_[b_], res, s_[:, 11:12], fng_bc, ALU.mult, ALU.mult, ["res", sk_, "fng_bc"], [xk_])
                    dma("sp", out[tsl, :], xt_[b_], [xk_], ["out"])

        fw = [(q, i) for q in ("sp", "pool") for i, op in enumerate(S.ops[q]) if op["dma"]]
        S.emit(final_waits=fw)
    return nc


def _fm(v, nchunk):
    return np.ascontiguousarray(np.asarray(v, np.float32).reshape(nchunk, 128).T)


def _consts():
    k = np.arange(128)[:, None]
    l = np.arange(128)[None, :]
    ident = (k == l).astype(np.float32)
    tle = (k <= l).astype(np.float32)
    tge = (k >= l).astype(np.float32)
    su = (k > l).astype(np.float32)
    sl = (k < l).astype(np.float32)
    d = np.arange(128)
    sw = np.where((d % 32) < 16, d + 16, d - 16)
    perm = np.zeros((128, 128), np.float32)
    perm[sw, d] = 1.0
    return np.ascontiguousarray(np.concatenate([ident, tle, tge, su, sl, perm], axis=1))


def _rope_tables(pos):
    pos = np.asarray(pos)
    row = (pos // 64).astype(np.float32)
    col = (pos % 64).astype(np.float32)
    quarter = 16
    freq = (1.0 / (np.float32(10000.0) ** (np.arange(quarter, dtype=np.float32) / np.float32(quarter)))).astype(np.float32)
    cosT = np.zeros((128, len(pos)), np.float32)
    ssinT = np.zeros((128, len(pos)), np.float32)
    for p in range(128):
        dd = p % 64
        i = dd % 16
        pp = row if dd < 32 else col
        ang = (pp * freq[i]).astype(np.float32)
        cosT[p] = np.cos(ang)
        s = np.sin(ang)
        ssinT[p] = -s if (dd % 32) < 16 else s
    return cosT, ssinT


def prep_inputs(inp):
    g = lambda k: np.asarray(inp[k], np.float32)
    x, c, ctx, c_ctx = g("x"), g("c"), g("ctx"), g("c_ctx")
    w_mod, b_mod, norm_g, w_in = g("w_mod")[0], g("b_mod")[0], g("norm_g")[0], g("w_in")[0]
    conv_w, conv_b = g("conv_w")[0], g("conv_b")[0]
    a_f, a_b, db_f, db_b = g("a_log_f")[0], g("a_log_b")[0], g("dt_bias_f")[0], g("dt_bias_b")[0]
    d_skip, attg, ssdg, sink = g("d_skip")[0], g("att_norm_g")[0], g("ssd_norm_g")[0], g("sink")[0]
    w_out, fng = g("w_out")[0], g("final_norm_g")
    consts = _consts()
    shared = dict(w_in=np.ascontiguousarray(w_in), w_out=np.ascontiguousarray(w_out),
                  normgT=_fm(norm_g, 16), dskipT=_fm(np.repeat(d_skip, 64), 8), attgT=_fm(attg, 8), ssdgT=_fm(ssdg, 8),
                  sinkb=np.ascontiguousarray(np.broadcast_to(sink[None, :], (128, 16))),
                  dskb=np.ascontiguousarray(np.broadcast_to(d_skip[None, :], (128, 16))),
                  fng=np.ascontiguousarray(fng[None, :]), consts=consts)
    maps = []
    for core in range(8):
        b, h = core // 2, core % 2
        xb = x[b]
        xl = xb if h == 0 else xb[::-1]
        pos = np.arange(SEQ) if h == 0 else np.arange(SEQ)[::-1]
        cl = ctx[b] if h == 0 else ctx[b][::-1]
        cw = conv_w if h == 0 else conv_w[::-1]
        convT = np.stack([_fm(cw[0], 16), _fm(cw[1], 16), _fm(cw[2], 16), _fm(conv_b, 16)], axis=2).reshape(128, 64)
        if h == 0:
            wdt = w_in[:, OFF_DTF:OFF_DTF + 32]
            dtbv = np.concatenate([db_f, db_b])
            alv = np.concatenate([a_f, a_b])
        else:
            wdt = np.concatenate([w_in[:, OFF_DTB:OFF_DTB + 16], w_in[:, OFF_DTF:OFF_DTF + 16]], axis=1)
            dtbv = np.concatenate([db_b, db_f])
            alv = np.concatenate([a_b, a_f])
        cosT, ssinT = _rope_tables(pos[0:NTH])
        cvec = np.stack([_fm(c[b], 16), _fm(c_ctx, 16)], axis=2).reshape(128, 32)
        chs = [2 * i + h for i in range(16)]
        gbs = [2 * gb + h for gb in range(2)]
        wsh = np.concatenate([w_mod[:, ch * 128:(ch + 1) * 128] for ch in chs] +
                             [w_mod[:, 2 * D + k * 512:2 * D + (k + 1) * 512] for k in gbs], axis=1)
        bml = np.stack([b_mod[ch * 128:(ch + 1) * 128] for ch in chs], axis=1)
        bgl = np.concatenate([b_mod[2 * D + k * 512:2 * D + (k + 1) * 512] for k in gbs])[None, :]
        rm = np.zeros((128, 2), np.float32)
        rm[:, h] = 1.0
        m = dict(shared)
        m.update(xloc=np.ascontiguousarray(xl[0:NTH]), ctxf=np.ascontiguousarray(cl),
                 w_mod_sh=np.ascontiguousarray(wsh), bmod_l=np.ascontiguousarray(bml), bgate_l=np.ascontiguousarray(bgl), rmask=rm,
                 cvec=np.ascontiguousarray(cvec), w_dt=np.ascontiguousarray(wdt), convT=np.ascontiguousarray(convT),
                 dtb=np.ascontiguousarray(np.broadcast_to(dtbv[None, :], (128, 32))),
                 alog=np.ascontiguousarray(np.broadcast_to(alv[None, :], (128, 32))),
                 cosT=cosT, ssinT=ssinT)
        maps.append(m)
    return maps


def kernel(**inputs):
    maps = prep_inputs(inputs)
    nc = build()
    res = run_bass_kernel_spmd(nc, maps, core_ids=list(range(8)))
    outp = np.zeros((4, SEQ, D), np.float32)
    for core in range(8):
        b, h = core // 2, core % 2
        o = res.results[core]["out"]
        if h == 0:
            outp[b, 0:NT] = o
        else:
            outp[b, NT:SEQ] = o[::-1]
    return outp
```

```python
import contextlib
import os
import numpy as np
CUT = int(os.environ.get('KCUT', '99'))
CUT2 = int(os.environ.get('KCUT2', '99'))
import concourse.bass as bass
import concourse.mybir as mybir
from concourse.bass_utils import run_bass_kernel_spmd

F32 = mybir.dt.float32
BF16 = mybir.dt.bfloat16
AF = mybir.ActivationFunctionType
ALU = mybir.AluOpType

D = 2048
NT = 2048
NTH = 2176
SEQ = 4096
NCTX = 256
NIN = 5664
OFF_Q, OFF_K, OFF_V, OFF_G, OFF_Z, OFF_XS, OFF_B, OFF_C = 0, 1024, 1280, 1536, 2560, 3584, 4608, 5120
OFF_DTF, OFF_DTB = 5632, 5648
EPS = 1e-6


class Sched:
    ENGS = ["pe", "act", "dve", "pool", "sp"]
    NDMASEM = 4
    WINDOW = 128

    def __init__(self, nc):
        self.nc = nc
        self.ops = {e: [] for e in self.ENGS}
        self.lastw = {}
        self.readers = {}
        self.ndma = {e: 0 for e in self.ENGS}
        self.epoch = 0

    def barrier(self):
        self.epoch += 1
        self.lastw = {}
        self.readers = {}

    def add(self, eng, fn, reads=(), writes=(), dma=False, cost=0.3):
        idx = len(self.ops[eng])
        psr = [k for k in reads if k.startswith("ps")]
        if psr:
            reads = [k for k in reads if not k.startswith("ps")]
            writes = list(writes) + [k for k in psr if k not in writes]
        deps = set()
        for k in reads:
            if k in self.lastw:
                deps.add(self.lastw[k])
        for k in writes:
            if k in self.lastw:
                deps.add(self.lastw[k])
            for r in self.readers.get(k, ()):
                deps.add(r)
        deps.discard((eng, idx))
        odeps = set()
        if eng == "pe":
            odeps = set(d for d in deps if d[0] == "pe")
            deps = deps - odeps
        op = dict(fn=fn, deps=deps, odeps=odeps, dma=dma, marked=False, cost=cost, epoch=self.epoch)
        self.ops[eng].append(op)
        for k in reads:
            self.readers.setdefault(k, []).append((eng, idx))
        for k in writes:
            self.lastw[k] = (eng, idx)
            self.readers[k] = []
        return (eng, idx)

    def pe(self, fn, reads=(), writes=(), cost=0.15):
        return self.add("pe", fn, reads, writes, cost=cost)

    def act(self, fn, reads=(), writes=(), cost=0.4):
        return self.add("act", fn, reads, writes, cost=cost)

    def dve(self, fn, reads=(), writes=(), cost=0.4):
        return self.add("dve", fn, reads, writes, cost=cost)

    def dma(self, q, fn, reads=(), writes=(), cost=4.0):
        return self.add(q, fn, reads, writes, dma=True, cost=cost)

    def schedule(self):
        ops = self.ops
        pending = {e: list(range(len(ops[e]))) for e in self.ENGS}
        done = {e: [None] * len(ops[e]) for e in self.ENGS}
        free = {e: 0.0 for e in self.ENGS}
        order = {e: [] for e in self.ENGS}
        dma_free = [0.0]
        self.fence_pending = {}
        ntot = sum(len(v) for v in pending.values())
        SYNC = 0.08
        epoch = 0
        last_exec = {}
        for _ in range(ntot):
            while not any(pending[e] and ops[e][pending[e][0]]["epoch"] == epoch for e in self.ENGS):
                epoch += 1
                tmax = max(free.values())
                for e in self.ENGS:
                    free[e] = tmax
                    if pending[e]:
                        nxt = [i for i in pending[e] if ops[e][i]["epoch"] == epoch]
                        if nxt:
                            self.fence_pending.setdefault(e, set()).update(last_exec.values())
                for k in [k for k in last_exec if isinstance(k, tuple)]:
                    del last_exec[k]
            best = None
            for e in self.ENGS:
                pl = pending[e]
                fe = free[e]
                for wi in range(min(self.WINDOW, len(pl))):
                    i = pl[wi]
                    op = ops[e][i]
                    if op["epoch"] != epoch:
                        break
                    st = fe
                    ok = True
                    for (de, di) in op["deps"]:
                        t = done[de][di]
                        if t is None:
                            ok = False
                            break
                        if t + SYNC > st:
                            st = t + SYNC
                    if ok:
                        for (de, di) in op["odeps"]:
                            if done[de][di] is None:
                                ok = False
                                break
                    if not ok:
                        continue
                    key = (st, wi)
                    if best is None or key < best[0]:
                        best = (key, e, wi, i, st)
                    if st <= fe:
                        break
            assert best is not None, "scheduler deadlock"
            _, e, wi, i, st = best
            op = ops[e][i]
            if op["dma"]:
                free[e] = st + 0.15
                t0 = max(st + 2.0, dma_free[0])
                dma_free[0] = t0 + op["cost"]
                done[e][i] = dma_free[0]
            else:
                free[e] = st + op["cost"]
                done[e][i] = free[e]
            fp = self.fence_pending.pop(e, None)
            if fp:
                op["deps"] = set(op["deps"]) | set(d for d in fp if d != (e, i))
            if op["dma"]:
                last_exec[(e, i)] = (e, i)
            else:
                last_exec[e] = (e, i)
            order[e].append(i)
            del pending[e][wi]
        self.est_time = max(max([0.0] + [t for t in done[e] if t is not None]) for e in self.ENGS)
        return order

    def emit(self, final_waits=()):
        nc = self.nc
        ops = self.ops
        fin = dict(fn=None, deps=set(final_waits), odeps=set(), dma=False, marked=False, cost=0.0, epoch=self.epoch)
        ops["sp"].append(fin)
        order = self.schedule()
        for e in self.ENGS:
            for op in ops[e]:
                for (de, di) in op["deps"]:
                    ops[de][di]["marked"] = True
        for e in self.ENGS:
            c = 0
            nd = 0
            for i in order[e]:
                op = ops[e][i]
                if op["dma"]:
                    op["dmaslot"] = nd
                    nd += 1
                elif op["marked"]:
                    c += 1
                    op["cnt"] = c
        with contextlib.ExitStack() as st:
            esem = {e: st.enter_context(nc.semaphore("s_" + e)) for e in self.ENGS}
            dsem = {}
            for e in self.ENGS:
                if self.ndma[e] or any(op["dma"] for op in ops[e]):
                    dsem[e] = [st.enter_context(nc.semaphore("d_%s%d" % (e, i)))
                               for i in range(self.NDMASEM)]
            block = st.enter_context(nc.Block())
            engobj = {"pe": "tensor", "act": "scalar", "dve": "vector", "pool": "gpsimd", "sp": "sync"}

            def run(e, eng):
                seen = {}
                seen_dma = set()
                for i in order[e]:
                    op = ops[e][i]
                    waits = []
                    if op["dma"]:
                        slot = op["dmaslot"]
                        if slot >= self.NDMASEM:
                            waits.append((dsem[e][slot % self.NDMASEM], 16 * (slot // self.NDMASEM)))
                    cneed = {}
                    for (de, di) in op["deps"]:
                        dop = ops[de][di]
                        if dop["dma"]:
                            if (de, di) in seen_dma:
                                continue
                            seen_dma.add((de, di))
                            s = dop["dmaslot"]
                            waits.append((dsem[de][s % self.NDMASEM], 16 * (s // self.NDMASEM + 1)))
                        else:
                            c = dop["cnt"]
                            if seen.get(de, 0) >= c:
                                continue
                            cneed[de] = max(cneed.get(de, 0), c)
                    for de, c in cneed.items():
                        seen[de] = c
                        waits.append((esem[de], c))
                    for (s, v) in waits:
                        eng.wait_ge(s, v)
                    if op["fn"] is None:
                        continue
                    ins = op["fn"](eng)
                    if op["dma"]:
                        ins.then_inc(dsem[e][op["dmaslot"] % self.NDMASEM], 16)
                    elif op["marked"]:
                        ins.then_inc(esem[e], 1)

            for e in self.ENGS:
                if not ops[e]:
                    continue

                def mk(e):
                    def _f(eng):
                        run(e, eng)
                    return _f
                getattr(block, engobj[e])(mk(e))


def build(stage=99, dbg=(), ncores=8):
    nc = bass.Bass("TRN2", target_bir_lowering=False)

    def din(name, shape):
        return nc.dram_tensor(name, list(shape), F32, kind="ExternalInput").ap()

    xloc = din("xloc", [NTH, D])
    ctxf = din("ctxf", [NCTX, D])
    cvec = din("cvec", [128, 32])
    w_mod_sh = din("w_mod_sh", [D, 3 * D // 2])
    bmod_l = din("bmod_l", [128, 16])
    bgate_l = din("bgate_l", [1, D // 2])
    rmask = din("rmask", [128, 2])
    normgT = din("normgT", [128, 16])
    w_in = din("w_in", [D, NIN])
    w_dt = din("w_dt", [D, 32])
    convT = din("convT", [128, 16 * 4])
    dtb = din("dtb", [128, 32])
    alog = din("alog", [128, 32])
    dskipT = din("dskipT", [128, 8])
    attgT = din("attgT", [128, 8])
    ssdgT = din("ssdgT", [128, 8])
    sinkb = din("sinkb", [128, 16])
    dskb = din("dskb", [128, 16])
    w_out = din("w_out", [D, D])
    fng = din("fng", [1, D])
    cosT = din("cosT", [128, NTH])
    ssinT = din("ssinT", [128, NTH])
    consts = din("consts", [128, 6 * 128])
    out = nc.dram_tensor("out", [NT, D], F32, kind="ExternalOutput").ap()
    gate_scr = nc.dram_tensor("gate_scr", [1, D], F32).ap()
    ccg_in = nc.dram_tensor("ccg_in", [1, D], F32)
    ccm_in = nc.dram_tensor("ccm_in", [128, 64], F32)
    ccm_out = nc.dram_tensor("ccm_out", [128, 64], F32)
    h_scr = nc.dram_tensor("h_scr", [16, 128, 8, 128], BF16).ap()
    cc_in = [nc.dram_tensor("cc_in%d" % g, [128, 256], F32) for g in range(4)]
    cc_out = [nc.dram_tensor("cc_out%d" % g, [128, 256], F32) for g in range(4)]
    dbg_out = {}
    for (nm, shp) in dbg:
        dbg_out[nm] = nc.dram_tensor("dbg_" + nm, list(shp), F32, kind="ExternalOutput").ap()

    st = contextlib.ExitStack()
    with st:
        S = Sched(nc)
        ARENA = 189 * 1024
        arena = st.enter_context(nc.sbuf_tensor("arena", [128, ARENA // 2], BF16))
        cst = st.enter_context(nc.sbuf_tensor("cst", [128, 1472], F32))
        pbank = [st.enter_context(nc.psum_tensor("pb%d" % i, [128, 512], F32)) for i in range(8)]

        class Ar:
            def __init__(self):
                self.off = 0

            def take(self, nbytes, dt, parts=128):
                nbytes = (nbytes + 63) // 64 * 64
                o = self.off
                self.off += nbytes
                assert self.off <= ARENA, ("arena overflow", self.off)
                v = arena[0:parts, o // 2:(o + nbytes) // 2]
                if dt == F32:
                    v = v.bitcast(F32)
                return v

            def mark(self):
                return self.off

            def reset(self, m):
                self.off = m
                S.barrier()

        A = Ar()
        coff = [0]

        def ctake(n):
            o = coff[0]
            coff[0] += n
            assert coff[0] <= 1472
            return cst[:, o:o + n]

        def fsz(ap):
            n = 1
            for d in ap.shape[1:]:
                n *= d
            return n

        def dma(q, o, i, reads=(), writes=()):
            esz = 4 if i.dtype == F32 else 2
            nbytes = fsz(i) * i.shape[0] * esz
            return S.dma(q, lambda e: e.dma_start(out=o, in_=i), reads, writes, cost=nbytes / 250e3)

        def mm(o, lhsT, rhs, start, stop, reads, writes):
            c = max(0.11, 0.006 + fsz(rhs) / 2400.0)
            if rhs.dtype == F32:
                c = 0.05 + fsz(rhs) / 800.0
            return S.pe(lambda e: e.matmul(o, lhsT=lhsT, rhs=rhs, start=start, stop=stop), reads, writes, cost=c)

        def tr(o, i, reads, writes):
            return S.pe(lambda e: e.transpose(o, i, ident_b), list(reads) + ["cb16"], writes, cost=0.12)

        def act(o, i, func, reads, writes, bias=None, scale=None, accum=None):
            kw = {}
            c = 0.22 + fsz(o) / 1400.0
            if bias is not None:
                kw["bias"] = bias
            if scale is not None:
                kw["scale"] = scale
            if accum is not None:
                kw["accum_out"] = accum
                c += 0.1
            return S.act(lambda e: e.activation(out=o, in_=i, func=func, **kw), reads, writes, cost=c)

        def dcost(o, two=False):
            n = fsz(o)
            if o.dtype == F32 or two:
                return 0.08 + n / 960.0
            return 0.08 + n / 1200.0

        def tt(o, a, b, op, reads, writes, eng="dve"):
            return S.add(eng, lambda e: e.tensor_tensor(out=o, in0=a, in1=b, op=op), reads, writes, cost=dcost(o, True))

        def ts(o, a, s1, s2, op0, op1, reads, writes, eng="dve"):
            if op1 is None:
                return S.add(eng, lambda e: e.tensor_scalar(out=o, in0=a, scalar1=s1, scalar2=None, op0=op0), reads, writes,
                             cost=dcost(o))
            return S.add(eng, lambda e: e.tensor_scalar(out=o, in0=a, scalar1=s1, scalar2=s2, op0=op0, op1=op1), reads, writes,
                         cost=dcost(o))

        def stt(o, a, s, b, op0, op1, reads, writes):
            return S.dve(lambda e: e.scalar_tensor_tensor(out=o, in0=a, scalar=s, in1=b, op0=op0, op1=op1), reads, writes,
                         cost=dcost(o, True))

        def cp(o, i, reads, writes, eng="dve"):
            return S.add(eng, lambda e: e.tensor_copy(out=o, in_=i), reads, writes, cost=dcost(o))

        def memset(o, v, writes, eng="dve"):
            return S.add(eng, lambda e: e.memset(o, v), (), writes, cost=dcost(o))

        def pbf(i):
            return pbank[i][:, :].bitcast(BF16)

        Hst = st.enter_context(nc.sbuf_tensor("Hst", [128, 2048], F32))
        cf = ctake(768)
        dma("sp", cf, consts, writes=["cf"])
        ident_f = cf[:, 0:128]
        TLEf = cf[:, 128:256]
        TGEf = cf[:, 256:384]
        SUf = cf[:, 384:512]
        SLf = cf[:, 512:640]
        cb16 = A.take(6 * 128 * 2, BF16)
        cp(cb16, cf, ["cf"], ["cb16"])
        ident_b = cb16[:, 0:128]
        TLEb = cb16[:, 128:256]
        TGEb = cb16[:, 256:384]
        perm_b = cb16[:, 640:768]
        onesf = ctake(128)
        memset(onesf, 1.0, ["onesf"])
        small = ctake(32 + 32 + 16 + 64 + 32 + 32 + 8 + 8 + 8 + 16 + 16)
        rmask_s = ctake(2)
        cv = small[:, 0:32]
        bmod_s = small[:, 32:48]
        normg_s = small[:, 64:80]
        conv_s = small[:, 80:144]
        dtb_s = small[:, 144:176]
        alog_s = small[:, 176:208]
        dskip_s = small[:, 208:216]
        attg_s = small[:, 216:224]
        ssdg_s = small[:, 224:232]
        sink_s = small[:, 232:248]
        dskb_s = small[:, 248:264]
        for (dst, src, k) in [(cv, cvec, "cv"), (bmod_s, bmod_l, "bmod"), (rmask_s, rmask, "rmask"), (normg_s, normgT, "normg"), (conv_s, convT, "conv"),
                              (dtb_s, dtb, "dtb"), (alog_s, alog, "alog"), (dskip_s, dskipT, "dskip"), (attg_s, attgT, "attg"),
                              (ssdg_s, ssdgT, "ssdg"), (sink_s, sinkb, "sink"), (dskb_s, dskb, "dskb")]:
            dma("sp", dst, src, writes=[k])
        a_s = ctake(32)
        act(a_s, alog_s, AF.Exp, ["alog"], ["a_s"])
        ts(a_s, a_s, -1.0, None, ALU.mult, None, ["a_s"], ["a_s"])
        esink = ctake(16)
        act(esink, sink_s, AF.Exp, ["sink"], ["esink"])
        scb = A.take(32 * 2, BF16)
        act(scb, cv, AF.Silu, ["cv"], ["scb"])

        modT = ctake(64)
        PAIRS = [[2 * i, 2 * i + 1] for i in range(ncores // 2)]
        top = (ARENA - 41984) // 2
        wmb = [arena[:, top + i * 8192:top + (i + 1) * 8192].rearrange("p (k n) -> p k n", k=16) for i in range(2)]
        G2 = arena[:, top + 16384:top + 16384 + 2048].bitcast(F32)
        bgl = arena[:, top + 18432:top + 18432 + 2048].bitcast(F32)
        ml = arena[:, top + 20480:top + 20480 + 64].bitcast(F32)
        modl = arena[:, top + 20544:top + 20544 + 128].bitcast(F32)
        dma("sp", bgl[0:1, :], bgate_l, writes=["bgl"])
        psmod = pbank[3][:, 0:32]
        for lb in range(4):
            wb_ = wmb[lb % 2]
            dma("pool", wb_, w_mod_sh[:, lb * 512:(lb + 1) * 512].rearrange("(k p) n -> p k n", p=128), writes=["wmb%d" % (lb % 2)])
            for c4 in range(4):
                i = lb * 4 + c4
                for kc in range(16):
                    mm(psmod[:, i * 2:i * 2 + 2], wb_[:, kc, c4 * 128:(c4 + 1) * 128], scb[:, kc * 2:kc * 2 + 2],
                       kc == 0, kc == 15, ["wmb%d" % (lb % 2), "scb"], ["ps3"])
        tt(ml.rearrange("p (c j) -> p c j", j=2), psmod.rearrange("p (c j) -> p c j", j=2),
           bmod_s.rearrange("p (c o) -> p c o", o=1).to_broadcast([128, 16, 2]), ALU.add, ["ps3", "bmod"], ["ml"])
        tt(modl.rearrange("p (c r j) -> p c r j", r=2, j=2),
           ml.rearrange("p (c o j) -> p c o j", o=1, j=2).to_broadcast([128, 16, 2, 2]),
           rmask_s.rearrange("p (o r q) -> p o r q", o=1, q=1).to_broadcast([128, 16, 2, 2]), ALU.mult, ["ml", "rmask"], ["modl"])
        dma("sp", ccm_in.ap(), modl, ["modl"], ["ccm_in"])
        S.add("pool", lambda e: e.collective_compute("AllReduce", ALU.add, replica_groups=PAIRS,
                                                     ins=[ccm_in.ap().opt()], outs=[ccm_out.ap().opt()]),
              ["ccm_in"], ["ccm_out"], cost=12.0)
        dma("sp", modT, ccm_out.ap(), ["ccm_out"], ["modT"])
        for gb in range(2):
            wb_ = wmb[gb % 2]
            dma("pool", wb_, w_mod_sh[:, 2048 + gb * 512:2048 + (gb + 1) * 512].rearrange("(k p) n -> p k n", p=128),
                writes=["wmb%d" % (gb % 2)])
            for kc in range(16):
                mm(pbank[5][:, :], scb[:, kc * 2:kc * 2 + 1].to_broadcast([128, 128]), wb_[:, kc, :], kc == 0, kc == 15,
                   ["scb", "wmb%d" % (gb % 2)], ["ps5"])
            tt(G2[0:1, gb * 512:(gb + 1) * 512], pbank[5][0:1, :], bgl[0:1, gb * 512:(gb + 1) * 512], ALU.add, ["ps5", "bgl"], ["G2"])
        tt(Hst[0:1, :].rearrange("p (g r n) -> p g r n", g=2, r=2),
           G2[0:1, :].rearrange("p (g o n) -> p g o n", g=2, o=1).to_broadcast([1, 2, 2, 512]),
           rmask_s[0:1, :].rearrange("p (o r q) -> p o r q", o=1, q=1).to_broadcast([1, 2, 2, 512]), ALU.mult,
           ["G2", "rmask"], ["HF", "HB"])
        dma("sp", ccg_in.ap(), Hst[0:1, :], ["HF", "HB"], ["ccg_in"])
        S.add("pool", lambda e: e.collective_compute("AllReduce", ALU.add, replica_groups=PAIRS,
                                                     ins=[ccg_in.ap().opt()], outs=[gate_scr.tensor.ap().opt() if hasattr(gate_scr, "tensor") else gate_scr.opt()]),
              ["ccg_in"], ["gate_scr"], cost=12.0)
        modv = modT.rearrange("p (c j) -> p c j", j=2)
        gs = [ctake(16), ctake(16)]
        sh = [ctake(16), ctake(16)]
        for j in range(2):
            stt(gs[j], modv[:, 16:32, j], 1.0, normg_s, ALU.add, ALU.mult, ["modT", "normg"], ["gs%d" % j])
            cp(sh[j], modv[:, 0:16, j], ["modT"], ["sh%d" % j])

        if "mod" in dbg_out:
            dma("sp", dbg_out["mod"], modT, ["modT"], ["dbg_mod"])

        def alloc_build_set():
            return dict(xb=[A.take(D * 4, F32) for _ in range(2)], xn=[A.take(D * 2, BF16) for _ in range(2)],
                        junk=A.take(D * 2, BF16), stat=[A.take(64, F32) for _ in range(2)])

        def xkeys(pfx):
            def f(t0, n):
                return [pfx + "%d" % t for t in range(t0 // 128, (t0 + n - 1) // 128 + 1)]
            return f

        def build_xmT(XT, xkp, src, ntiles, j, bs):
            xb, xn, junk, stat = bs["xb"], bs["xn"], bs["junk"], bs["stat"]
            for t in range(ntiles):
                b_ = t % 2
                kx, kn, ks = "b_xb%d" % b_, "b_xn%d" % b_, "b_st%d" % b_
                dma("sp", xb[b_], src[t * 128:(t + 1) * 128, :], writes=[kx])
                act(junk, xb[b_], AF.Square, [kx], ["b_junk", ks], accum=stat[b_][:, 0:1])
                ts(stat[b_][:, 1:2], stat[b_][:, 0:1], 1.0 / D, EPS, ALU.mult, ALU.add, [ks], [ks])
                act(stat[b_][:, 2:3], stat[b_][:, 1:2], AF.Sqrt, [ks], [ks])
                S.dve(lambda e, o=stat[b_][:, 3:4], i=stat[b_][:, 2:3]: e.reciprocal(out=o, in_=i), [ks], [ks], cost=0.1)
                ts(xn[b_], xb[b_], stat[b_][:, 3:4], None, ALU.mult, None, [kx, ks], [kn])
                for q4 in range(4):
                    pt = pbf([2, 4][q4 % 2])[:, 0:512]
                    pk = "ps%d" % [2, 4][q4 % 2]
                    for i4 in range(4):
                        kc = q4 * 4 + i4
                        tr(pt[:, i4 * 128:(i4 + 1) * 128], xn[b_][:, kc * 128:(kc + 1) * 128], [kn], [pk])
                    o = XT[:, q4 * 4:q4 * 4 + 4, t * 128:(t + 1) * 128]
                    pv = pt.rearrange("p (c n) -> p c n", c=4)
                    if q4 % 2 == 0:
                        cp(o, pv, [pk], [xkp + "%d" % t])
                    else:
                        act(o, pv, AF.Copy, [pk], [xkp + "%d" % t])

        def modulate(XT, xkp, ntiles, j):
            keys = [xkp + "%d" % t for t in range(ntiles)]
            for kc in range(16):
                o = XT[:, kc, 0:ntiles * 128]
                if kc % 2 == 0:
                    ts(o, o, gs[j][:, kc:kc + 1], sh[j][:, kc:kc + 1], ALU.mult, ALU.add, keys + ["gs%d" % j, "sh%d" % j], keys)
                else:
                    act(o, o, AF.Identity, keys + ["gs%d" % j, "sh%d" % j], keys, bias=sh[j][:, kc:kc + 1], scale=gs[j][:, kc:kc + 1])

        wslot = [0]

        def load_w(wbufs, src_cols, ncols):
            i = wslot[0] % len(wbufs)
            wslot[0] += 1
            wv = wbufs[i][:, :, 0:ncols]
            dma("pool", wv, src_cols.rearrange("(k p) n -> p k n", p=128), writes=["wbuf%d" % i])
            return wv, "wbuf%d" % i

        pacc = [0]

        def inproj_T(wv, wk, c0, nco, XT, xk, t0, nt):
            b_ = pacc[0] % 2
            pacc[0] += 1
            ps = pbank[b_][0:nco, 0:nt]
            for kc in range(16):
                mm(ps, wv[:, kc, c0:c0 + nco], XT[:, kc, t0:t0 + nt], kc == 0, kc == 15, [wk] + xk(t0, nt), ["ps%d" % b_])
            return ps, "ps%d" % b_

        def inproj_tok(wv, wk, c0, nco, XT, xk, t0, ps, pk):
            for kc in range(16):
                mm(ps, XT[:, kc, t0:t0 + 128], wv[:, kc, c0:c0 + nco], kc == 0, kc == 15, [wk] + xk(t0, 128), [pk])

        def pieces(ntok):
            npc = (ntok + 511) // 512
            sz = ((ntok + npc - 1) // npc + 1) // 2 * 2
            r = []
            t = 0
            while t < ntok:
                n = min(sz, ntok - t)
                r.append((t, n))
                t += n
            return r

        def conv_chunk(wv, wk, c0, XT, xk, ntok, has_halo, cc, taps, outT, outk, stage_f, tmp_f, tb=0, left=False):
            if not left:
                memset(stage_f[:, 0:1], 0.0, ["stage"])
            if not has_halo:
                memset(stage_f[:, ntok + 1:ntok + 2], 0.0, ["stage"])
            lo = tb - (1 if left else 0)
            ntot = ntok + (1 if has_halo else 0) + (1 if left else 0)
            so = 0 if left else 1
            for (t0, n) in pieces(ntot):
                ps, pk = inproj_T(wv, wk, c0, 128, XT, xk, lo + t0, n)
                act(stage_f[:, so + t0:so + t0 + n], ps, AF.Copy, [pk], ["stage"])
            cw = conv_s.rearrange("p (c f) -> p c f", f=4)
            wA = cw[:, cc, taps[0]:taps[0] + 1]
            wB = cw[:, cc, taps[1]:taps[1] + 1]
            wC = cw[:, cc, taps[2]:taps[2] + 1]
            bb = cw[:, cc, 3:4]
            ts(tmp_f[:, 0:ntok], stage_f[:, 0:ntok], wA, None, ALU.mult, None, ["stage", "conv"], ["ctmp"])
            stt(tmp_f[:, 0:ntok], stage_f[:, 1:ntok + 1], wB, tmp_f[:, 0:ntok], ALU.mult, ALU.add, ["stage", "conv", "ctmp"], ["ctmp"])
            stt(tmp_f[:, 0:ntok], stage_f[:, 2:ntok + 2], wC, tmp_f[:, 0:ntok], ALU.mult, ALU.add, ["stage", "conv", "ctmp"], ["ctmp"])
            act(outT[:, tb:tb + ntok], tmp_f[:, 0:ntok], AF.Silu, ["ctmp", "conv"], [outk], bias=bb)

        def to_tok(srcT, srck, nchunks, dst, dstk, c_off):
            for c0 in range(0, nchunks, 4):
                n = min(4, nchunks - c0)
                hb = (c0 // 4) % 2
                pt = pbf([2, 4][hb])[:, 0:n * 128]
                pk = "ps%d" % [2, 4][hb]
                for i in range(n):
                    tr(pt[:, i * 128:(i + 1) * 128], srcT[:, (c0 + i) * 128:(c0 + i + 1) * 128], [srck], [pk])
                cp(dst[:, c0:c0 + n, c_off:c_off + 128], pt.rearrange("p (c n) -> p c n", n=128), [pk], [dstk], eng="dve")

        def alloc_dt_set(nch):
            n = nch * 16
            return [A.take(n * 4, F32) for _ in range(5)]

        def dt_prep(wdt_v, wdtk, dcol, XT, xk, nch, dirn, tg, bufs):
            n = nch * 16
            dtv, adt, wgt, dec, tmp = [b[:, 0:n] for b in bufs]
            ps = pbank[3][:, 0:n]
            for c in range(nch):
                inproj_tok(wdt_v, wdtk, dcol, 16, XT, xk, c * 128, ps[:, c * 16:(c + 1) * 16], "ps3")
            v3 = lambda a: a.rearrange("p (c h) -> p c h", h=16)
            if CUT2 < 1:
                cp(tmp, ps, ["ps3"], [tg + "tmp"])
                return None
            tt(v3(tmp), v3(ps), dtb_s[:, dcol:dcol + 16].rearrange("p (o h) -> p o h", o=1).to_broadcast([128, nch, 16]), ALU.add,
               ["ps3", "dtb"], [tg + "tmp"])
            if CUT2 < 2:
                return None
            act(tmp, tmp, AF.Exp, [tg + "tmp"], [tg + "tmp"])
            if CUT2 < 3:
                return None
            act(dtv, tmp, AF.Ln, [tg + "tmp"], [tg + "dt"], bias=1.0)
            if CUT2 < 4:
                return None
            tt(v3(adt), v3(dtv), a_s[:, dcol:dcol + 16].rearrange("p (o h) -> p o h", o=1).to_broadcast([128, nch, 16]), ALU.mult,
               [tg + "dt", "a_s"], [tg + "adt"])
            if CUT2 < 5:
                return None
            pcs = pbank[3][:, 0:n]
            mm(pcs, TLEf if dirn == "F" else TGEf, adt, True, True, ["cf", tg + "adt"], ["ps3"])
            cp(tmp, pcs, ["ps3"], [tg + "tmp"])
            if CUT2 < 6:
                return None
            ptot = pbank[3][:, 0:n]
            mm(ptot, onesf, adt, True, True, ["onesf", tg + "adt"], ["ps3"])
            if CUT2 < 7:
                cp(tmp, ptot, ["ps3"], [tg + "tmp"])
                return None
            act(dec, ptot, AF.Exp, ["ps3"], [tg + "dec"])
            tt(tmp, ptot, tmp, ALU.subtract, ["ps3", tg + "tmp"], [tg + "tmp"])
            act(tmp, tmp, AF.Exp, [tg + "tmp"], [tg + "tmp"])
            tt(wgt, tmp, dtv, ALU.mult, [tg + "tmp", tg + "dt"], [tg + "wgt"])
            return dict(dt=v3(dtv), adt=v3(adt), wgt=v3(wgt), dec=v3(dec))

        def alloc_scan_set(ntok):
            nch = ntok // 128
            return dict(wdt=A.take(16 * 32 * 2, BF16).rearrange("p (k n) -> p k n", k=16), dt=alloc_dt_set(nch),
                        wb=[A.take(16 * 128 * 2, BF16).rearrange("p (k n) -> p k n", k=16) for _ in range(2)],
                        stage=[A.take((ntok + 2) * 4, F32)], tmp=[A.take(ntok * 4, F32)],
                        cT=A.take(ntok * 2, BF16),
                        xs_tok=A.take(nch * 256 * 2, BF16).rearrange("p (c n) -> p c n", n=256),
                        b_tok=A.take(nch * 128 * 2, BF16).rearrange("p (c n) -> p c n", n=128),
                        xsw=[A.take(256 * 2, BF16) for _ in range(2)])

        def scanF(XT, xk, ntok, has_halo, taps, dcol, Hst, Hk, ss):
            nch = ntok // 128
            wdt_v = ss["wdt"]
            dma("pool", wdt_v, w_dt.rearrange("(k p) n -> p k n", p=128), writes=["s_wdt"])
            dd = dt_prep(wdt_v, "s_wdt", dcol, XT, xk, nch, "F", "s_d", ss["dt"])
            wb = ss["wb"]
            stage_f, tmp_f, cT = ss["stage"][0], ss["tmp"][0], ss["cT"]
            xs_tok = ss["xs_tok"][:, 0:nch, :]
            b_tok = ss["b_tok"][:, 0:nch, :]
            for g in range(4):
                wv, wk = load_w(wb, w_in[:, OFF_B + g * 128:OFF_B + (g + 1) * 128], 128)
                conv_chunk(wv, wk, 0, XT, xk, ntok, has_halo, 8 + g, taps, cT, "s_cT", stage_f, tmp_f)
                to_tok(cT, "s_cT", nch, b_tok, "s_btok", 0)
                for pr in range(2):
                    wv, wk = load_w(wb, w_in[:, OFF_XS + g * 256 + pr * 128:OFF_XS + g * 256 + (pr + 1) * 128], 128)
                    conv_chunk(wv, wk, 0, XT, xk, ntok, has_halo, g * 2 + pr, taps, cT, "s_cT", stage_f, tmp_f)
                    to_tok(cT, "s_cT", nch, xs_tok, "s_xstok", pr * 128)
                Hg = Hst[:, g * 256:(g + 1) * 256]
                for c in range(nch):
                    xsw = ss["xsw"][c % 2]
                    xswk = "s_xsw%d" % (c % 2)
                    tt(xsw.rearrange("p (h d) -> p h d", d=64), xs_tok[:, c, :].rearrange("p (h d) -> p h d", d=64),
                       dd["wgt"][:, c, g * 4:(g + 1) * 4].rearrange("p (h o) -> p h o", o=1).to_broadcast([128, 4, 64]), ALU.mult,
                       ["s_xstok", "s_dwgt"], [xswk])
                    pst = pbank[7][:, 0:256]
                    mm(pst, b_tok[:, c, :], xsw, True, True, ["s_btok", xswk], ["ps7"])
                    tt(Hg.rearrange("p (h d) -> p h d", d=64), Hg.rearrange("p (h d) -> p h d", d=64),
                       dd["dec"][:, c, g * 4:(g + 1) * 4].rearrange("p (h o) -> p h o", o=1).to_broadcast([128, 4, 64]), ALU.mult,
                       [Hk, "s_ddec"], [Hk])
                    tt(Hg, Hg, pst, ALU.add, [Hk, "ps7"], [Hk])

        HF = Hst[:, 0:1024]
        HB = Hst[:, 1024:2048]
        memset(HF, 0.0, ["HF"])
        kctxT = st.enter_context(nc.sbuf_tensor("kctxT", [128, 4 * 256], BF16))
        kctxB = st.enter_context(nc.sbuf_tensor("kctxB", [128, 4 * 256], BF16))
        memset(kctxT[64:128, :], 0.0, ["kctxT"])
        memset(kctxB[0:64, :], 0.0, ["kctxB"])
        vctx = st.enter_context(nc.sbuf_tensor("vctx", [128, 2 * 256], BF16))
        TAPS_F = (0, 1, 2)
        TAPS_R = (2, 1, 0)
        m2 = A.mark()
        XT = A.take(16 * NTH * 2, BF16).rearrange("p (k n) -> p k n", k=16)
        XK = xkeys("XTl")
        m3 = A.mark()
        bset = alloc_build_set()
        XTc = A.take(16 * NCTX * 2, BF16).rearrange("p (k n) -> p k n", k=16)
        sset = alloc_scan_set(NCTX)
        assert A.off <= ARENA - 41984
        xkc = xkeys("XTc")
        build_xmT(XTc, "XTc", ctxf, 2, 1, bset)
        build_xmT(XT, "XTl", xloc, 17, 0, bset)
        modulate(XTc, "XTc", 2, 1)
        modulate(XT, "XTl", 17, 0)
        scanF(XTc, xkc, NCTX, False, TAPS_F, 0, HF, "HF", sset)
        wkv = sset["wb"]
        for g in range(4):
            wv, wk = load_w(wkv, w_in[:, OFF_K + g * 64:OFF_K + (g + 1) * 64], 64)
            wi = (wslot[0] - 1) % 2
            dma("pool", wkv[wi][:, :, 64:128], w_in[:, OFF_K + g * 64:OFF_K + (g + 1) * 64].rearrange("(k p) n -> p k n", p=128),
                writes=[wk])
            ps, pk = inproj_T(wkv[wi][:, :, 0:128], wk, 0, 128, XTc, xkc, 0, NCTX)
            cp(kctxT[0:64, g * 256:(g + 1) * 256], ps[0:64, :], [pk], ["kctxT"])
            cp(kctxB[64:128, g * 256:(g + 1) * 256], ps[64:128, :], [pk], ["kctxB"])
        for hv in range(2):
            wv, wk = load_w(wkv, w_in[:, OFF_V + hv * 128:OFF_V + (hv + 1) * 128], 128)
            for t in range(2):
                psv = pbank[7][:, 0:128]
                inproj_tok(wv, wk, 0, 128, XTc, xkc, t * 128, psv, "ps7")
                cp(vctx[:, t * 256 + hv * 128:t * 256 + (hv + 1) * 128], psv, ["ps7"], ["vctx"])
        A.reset(m3)

        if "H" in dbg_out:
            dma("sp", dbg_out["H"], Hst[:, :], ["HF", "HB"], ["dbg_H"])
        if "kctx" in dbg_out and stage >= 2:
            t_ = A.take(1024 * 4, F32)
            cp(t_, kctxT[:, :], ["kctxT"], ["t_"])
            dma("sp", dbg_out["kctx"], t_, ["t_"], ["dbg_kctx"])

        if stage >= 5:
            wdt_v = A.take(16 * 32 * 2, BF16).rearrange("p (k n) -> p k n", k=16)
            dma("pool", wdt_v, w_dt.rearrange("(k p) n -> p k n", p=128), writes=["l_wdt"])
            dF = dt_prep(wdt_v, "l_wdt", 0, XT, XK, 16, "F", "lF", alloc_dt_set(16))
            dB = dt_prep(wdt_v, "l_wdt", 16, XT, XK, 16, "B", "lB", alloc_dt_set(16))
            wb = [A.take(16 * 128 * 2, BF16).rearrange("p (k n) -> p k n", k=16) for _ in range(2)]
            BTs = [A.take(NT * 2, BF16) for _ in range(2)]
            CT = A.take(NT * 2, BF16)
            xs_toks = [A.take(16 * 256 * 2, BF16).rearrange("p (c n) -> p c n", n=256) for _ in range(2)]
            b_toks = [A.take(16 * 128 * 2, BF16).rearrange("p (c n) -> p c n", n=128) for _ in range(2)]
            prevFas = [A.take(16 * 256 * 2, BF16).rearrange("p (c n) -> p c n", n=256) for _ in range(2)]
            szT = A.take(2 * NT * 2, BF16).rearrange("p (c n) -> p c n", c=2)
            prevB = A.take(16 * 256 * 2, BF16).rearrange("p (c n) -> p c n", n=256)
            xswF = [A.take(256 * 2, BF16) for _ in range(2)]
            ccin_s = A.take(256 * 4, F32)
            ccout_s = A.take(256 * 4, F32)
            Dsk = A.take(4 * 128 * 2, BF16).rearrange("p (h n) -> p h n", h=4)
            HALF = 1024
            stage_f = A.take((HALF + 2) * 4, F32)
            tmp_f = A.take(HALF * 4, F32)
            xsT = A.take(NT * 2, BF16)
            lhs4 = A.take(512 * 4, F32).rearrange("p (h n) -> p h n", h=4)
            L4 = A.take(512 * 2, BF16).rearrange("p (h n) -> p h n", h=4)
            E4 = A.take(512 * 2, BF16).rearrange("p (h n) -> p h n", h=4)
            G4 = [A.take(512 * 2, BF16).rearrange("p (h n) -> p h n", h=4) for _ in range(2)]
            GE4 = [A.take(512 * 2, BF16).rearrange("p (h n) -> p h n", h=4) for _ in range(2)]
            CBm2 = A.take(256 * 2, BF16).rearrange("p (d n) -> p d n", d=2)
            lhs4s = [lhs4, A.take(512 * 4, F32).rearrange("p (h n) -> p h n", h=4)]
            xdt = [A.take(256 * 2, BF16) for _ in range(2)]
            xsw = A.take(256 * 2, BF16)
            ystg = [A.take(256 * 2, BF16).rearrange("p (c n) -> p c n", c=2) for _ in range(2)]
            h3 = lambda a: a.rearrange("p (h d) -> p h d", d=64)
            bch = lambda a, c, g: a[:, c, g * 4:(g + 1) * 4].rearrange("p (h o) -> p h o", o=1).to_broadcast([128, 4, 64])
            for g in range(4):
                par = g % 2
                BT, xs_tok, b_tok, prevFa = BTs[par], xs_toks[par], b_toks[par], prevFas[par]
                kBT, kxs, kbt, kpf, kHB = "BT%d" % par, "xstok%d" % par, "btok%d" % par, "prevFa%d" % par, "HB%d" % par
                HFg = HF[:, g * 256:(g + 1) * 256]
                HBg = HB[:, par * 256:(par + 1) * 256]
                for hf in range(2):
                    tb = hf * HALF
                    wv, wk = load_w(wb, w_in[:, OFF_B + g * 128:OFF_B + (g + 1) * 128], 128)
                    conv_chunk(wv, wk, 0, XT, XK, HALF, True, 8 + g, TAPS_F, BT, kBT, stage_f, tmp_f, tb=tb, left=(hf == 1))
                to_tok(BT, kBT, 16, b_tok, kbt, 0)
                for pr in range(2):
                    for hf in range(2):
                        tb = hf * HALF
                        wv, wk = load_w(wb, w_in[:, OFF_XS + g * 256 + pr * 128:OFF_XS + g * 256 + (pr + 1) * 128], 128)
                        conv_chunk(wv, wk, 0, XT, XK, HALF, True, g * 2 + pr, TAPS_F, xsT, "xsT", stage_f, tmp_f, tb=tb, left=(hf == 1))
                    to_tok(xsT, "xsT", 16, xs_tok, kxs, pr * 128)
                for c in range(16):
                    act(prevFa[:, c, :], HFg, AF.Copy, ["HF"], [kpf])
                    xw = xswF[c % 2]
                    xwk = "xswF%d" % (c % 2)
                    tt(h3(xw), h3(xs_tok[:, c, :]), bch(dF["wgt"], c, g), ALU.mult, [kxs, "lFwgt"], [xwk])
                    mm(pbank[5][:, 0:256], b_tok[:, c, :], xw, True, True, [kbt, xwk], ["ps5"])
                    tt(h3(HFg), h3(HFg), bch(dF["dec"], c, g), ALU.mult, ["HF", "lFdec"], ["HF"])
                    tt(HFg, HFg, pbank[5][:, 0:256], ALU.add, ["HF", "ps5"], ["HF"])
                cp(ccin_s, HFg, ["HF"], ["ccin_s"])
                dma("sp", cc_in[g].ap(), ccin_s, ["ccin_s"], ["cc_in%d" % g])
                S.add("pool", lambda e, g=g: e.collective_compute("AllReduce", ALU.add, replica_groups=[[2 * i, 2 * i + 1] for i in range(ncores // 2)],
                                                                  ins=[cc_in[g].ap().opt()], outs=[cc_out[g].ap().opt()]),
                      ["cc_in%d" % g], ["cc_out%d" % g], cost=25.0)
                dma("sp", ccout_s, cc_out[g].ap(), ["cc_out%d" % g], ["ccout_s"])
                tt(HBg, ccout_s, ccin_s, ALU.subtract, ["ccout_s", "ccin_s"], [kHB])
                for hf in range(2):
                    tb = hf * HALF
                    wv, wk = load_w(wb, w_in[:, OFF_C + g * 128:OFF_C + (g + 1) * 128], 128)
                    conv_chunk(wv, wk, 0, XT, XK, HALF, True, 12 + g, TAPS_F, CT, "CT", stage_f, tmp_f, tb=tb, left=(hf == 1))
                for pr in range(2):
                    wv, wk = load_w(wb, w_in[:, OFF_Z + g * 256 + pr * 128:OFF_Z + g * 256 + (pr + 1) * 128], 128)
                    for (t0, n) in pieces(NT):
                        ps, pk = inproj_T(wv, wk, 0, 128, XT, XK, t0, n)
                        act(szT[:, pr, t0:t0 + n], ps, AF.Silu, [pk], ["szT"])
                for hh in range(4):
                    ts(Dsk[:, hh, :], ident_f, dskb_s[:, g * 4 + hh:g * 4 + hh + 1], None, ALU.mult, None, ["cf", "dskb"], ["Dsk"])
                for c in range(15, -1, -1):
                    act(prevB[:, c, :], HBg, AF.Copy, [kHB], ["prevB"])
                    tt(h3(xsw), h3(xs_tok[:, c, :]), bch(dB["wgt"], c, g), ALU.mult, [kxs, "lBwgt"], ["xsw"])
                    mm(pbank[5][:, 256:512], b_tok[:, c, :], xsw, True, True, [kbt, "xsw"], ["ps5"])
                    tt(h3(HBg), h3(HBg), bch(dB["dec"], c, g), ALU.mult, [kHB, "lBdec"], [kHB])
                    tt(HBg, HBg, pbank[5][:, 256:512], ALU.add, [kHB, "ps5"], [kHB])
                for c in range(16):
                    cs_ = slice(c * 128, (c + 1) * 128)
                    mm(pbank[5][:, 0:128], BT[:, cs_], CT[:, cs_], True, True, [kBT, "CT"], ["ps5"])
                    tt(CBm2, pbank[5][:, 0:128].rearrange("p (o n) -> p o n", o=1).to_broadcast([128, 2, 128]),
                       cf[:, 128:384].rearrange("p (d n) -> p d n", d=2), ALU.mult, ["ps5", "cf"], ["CBm"])
                    for di, (dd_, msk, tri) in enumerate([(dF, SUf, TLEf), (dB, SLf, TGEf)]):
                        pD, pE = 6, 7
                        dk = "lF" if di == 0 else "lB"
                        adt4 = dd_["adt"][:, c, g * 4:(g + 1) * 4]
                        lh = lhs4s[di]
                        if True:
                            tt(lh, msk.rearrange("p (o n) -> p o n", o=1).to_broadcast([128, 4, 128]),
                               adt4.rearrange("p (h o) -> p h o", o=1).to_broadcast([128, 4, 128]), ALU.mult, ["cf", dk + "adt"], ["lhs4%d" % di])
                        else:
                            for hh in range(4):
                                act(lh[:, hh, :], msk, AF.Copy, ["cf", dk + "adt"], ["lhs4%d" % di], scale=adt4[:, hh:hh + 1])
                        for hh in range(4):
                            mm(pbank[pD][:, hh * 128:(hh + 1) * 128], lh[:, hh, :], tri, True, True, ["lhs4%d" % di, "cf"], ["ps%d" % pD])
                        act(L4, pbank[pD][:, :].rearrange("p (h n) -> p h n", h=4), AF.Exp, ["ps%d" % pD], ["L4"])
                        tt(G4[di], L4, CBm2[:, di:di + 1, :].to_broadcast([128, 4, 128]), ALU.mult,
                           ["L4", "CBm"], ["G4%d" % di])
                        for hh in range(4):
                            mm(pbank[pE][:, hh * 128:(hh + 1) * 128], adt4[:, hh:hh + 1].to_broadcast([128, 128]), tri, True, True,
                               [dk + "adt", "cf"], ["ps%d" % pE])
                        act(E4, pbank[pE][:, :].rearrange("p (h n) -> p h n", h=4), AF.Exp, ["ps%d" % pE], ["E4"])
                        tt(GE4[di], E4, CT[:, cs_].rearrange("p (o n) -> p o n", o=1).to_broadcast([128, 4, 128]), ALU.mult,
                           ["E4", "CT"], ["GE4%d" % di])
                        tt(h3(xdt[di]), h3(xs_tok[:, c, :]), bch(dd_["dt"], c, g), ALU.mult, [kxs, dk + "dt"], ["xdt%d" % di])
                    for r in range(2):
                        yo = pbank[3][:, r * 256:(r + 1) * 256]
                        hs = slice(r * 128, (r + 1) * 128)
                        h2 = slice(2 * r, 2 * r + 2)
                        mm(yo, xdt[0][:, hs], G4[0][:, h2, :], True, False, ["xdt0", "G40"], ["ps3"])
                        mm(yo, prevFa[:, c, hs], GE4[0][:, h2, :], False, False, [kpf, "GE40"], ["ps3"])
                        mm(yo, xdt[1][:, hs], G4[1][:, h2, :], False, False, ["xdt1", "G41"], ["ps3"])
                        mm(yo, prevB[:, c, hs], GE4[1][:, h2, :], False, False, ["prevB", "GE41"], ["ps3"])
                        mm(yo, xs_tok[:, c, hs], Dsk[:, h2, :], False, True, [kxs, "Dsk"], ["ps3"])
                    yk = "ystg%d" % (c % 2)
                    for b_ in range(2):
                        rw = slice(b_ * 64, (b_ + 1) * 64)
                        yv = pbank[3][rw, :].rearrange("p (r b n) -> p r b n", r=2, b=2)[:, :, b_, :]
                        tt(ystg[c % 2][rw, :, :], yv, szT[rw, :, cs_], ALU.mult, ["ps3", "szT"], [yk])
                    dma("sp", h_scr[c][:, 2 * g:2 * g + 2, :], ystg[c % 2], [yk], ["h_scr"])
            A.reset(m3)

            if stage >= 6:
                agT = A.take(8 * NT * 2, BF16).rearrange("p (c n) -> p c n", c=8)
                m4 = A.mark()
                A.off = m2
                wo = A.take(16 * D * 2, BF16).rearrange("p (k n) -> p k n", k=16)
                assert A.off <= m3
                A.off = m4
                wb = [A.take(16 * 128 * 2, BF16).rearrange("p (k n) -> p k n", k=16) for _ in range(2)]
                kTs = [A.take(NTH * 2, BF16) for _ in range(2)]
                kTBs = [A.take(NTH * 2, BF16) for _ in range(2)]
                qTs = [A.take(2 * NT * 2, BF16).rearrange("p (a n) -> p a n", a=2) for _ in range(2)]
                sgTs = [A.take(2 * NT * 2, BF16).rearrange("p (a n) -> p a n", a=2) for _ in range(2)]
                VA = A.take(19 * 128 * 2, BF16).rearrange("p (t n) -> p t n", n=128)
                VB = A.take(19 * 128 * 2, BF16).rearrange("p (t n) -> p t n", n=128)
                cos_b = A.take(NTH * 2, BF16)
                ssin_b = A.take(NTH * 2, BF16)
                qraw = A.take(512 * 2, BF16)
                rt1 = Hst[:, 1024:1536]
                rt2 = Hst[:, 1536:2048]
                PT = A.take(5 * 512 * 2, BF16).rearrange("p (c n) -> p c n", c=5)
                lnd = [Hst[:, 0:256], Hst[:, 256:512]]
                rd = lnd
                t1 = [Hst[:, 512:768], Hst[:, 768:1024]]
                for par in range(2):
                    memset(kTs[par][64:128, :], 0.0, ["kT%d" % par])
                    memset(kTBs[par][0:64, :], 0.0, ["kTB%d" % par])
                memset(VA[:, :, 64:128], 1.0, ["VA"])
                memset(VB[:, :, 0:64], 1.0, ["VB"])
                dma("pool", cos_b, cosT, writes=["cos_b"])
                dma("pool", ssin_b, ssinT, writes=["ssin_b"])

                def rope_proj(wv2, wk, ntok, dsts):
                    for (t0, n) in pieces(ntok):
                        ps, pk = inproj_T(wv2, wk, 0, 128, XT, XK, t0, n)
                        act(qraw[:, 0:n], ps, AF.Copy, [pk], ["qraw"])
                        mm(pbank[5][:, 0:n], perm_b, qraw[:, 0:n], True, True, ["cb16", "qraw"], ["ps5"])
                        tt(rt1[:, 0:n], qraw[:, 0:n], cos_b[:, t0:t0 + n], ALU.mult, ["qraw", "cos_b"], ["rt1"])
                        tt(rt2[:, 0:n], pbank[5][:, 0:n], ssin_b[:, t0:t0 + n], ALU.mult, ["ps5", "ssin_b"], ["rt2"])
                        for (dst, rw, dstk) in dsts:
                            tt(dst[rw, t0:t0 + n], rt1[rw, 0:n], rt2[rw, 0:n], ALU.add, ["rt1", "rt2"], [dstk])

                for g in range(4):
                    par = g % 2
                    kT, kTB, qT, sgT = kTs[par], kTBs[par], qTs[par], sgTs[par]
                    kkT, kkTB, kqT, ksg = "kT%d" % par, "kTB%d" % par, "qT%d" % par, "sgT%d" % par
                    wv, wk = load_w(wb, w_in[:, OFF_K + g * 64:OFF_K + (g + 1) * 64], 64)
                    wi = (wslot[0] - 1) % 2
                    dma("pool", wb[wi][:, :, 64:128], w_in[:, OFF_K + g * 64:OFF_K + (g + 1) * 64].rearrange("(k p) n -> p k n", p=128),
                        writes=[wk])
                    rope_proj(wb[wi][:, :, 0:128], wk, NTH, [(kT, slice(0, 64), kkT), (kTB, slice(64, 128), kkTB)])
                    wv, wk = load_w(wb, w_in[:, OFF_V + g * 64:OFF_V + (g + 1) * 64], 64)
                    for t4 in range(0, 17, 4):
                        nt_ = min(4, 17 - t4)
                        for i in range(nt_):
                            inproj_tok(wv, wk, 0, 64, XT, XK, (t4 + i) * 128, pbank[5][:, i * 64:(i + 1) * 64], "ps5")
                        pv = pbank[5][:, 0:nt_ * 64].rearrange("p (t n) -> p t n", n=64)
                        cp(VA[:, t4:t4 + nt_, 0:64], pv, ["ps5"], ["VA"])
                        act(VB[:, t4:t4 + nt_, 64:128], pv, AF.Copy, ["ps5"], ["VB"])
                    vc3 = vctx[:, :].rearrange("p (t n) -> p t n", t=2)
                    cp(VA[:, 17:19, 0:64], vc3[:, :, g * 64:(g + 1) * 64], ["vctx"], ["VA"])
                    cp(VB[:, 17:19, 64:128], vc3[:, :, g * 64:(g + 1) * 64], ["vctx"], ["VB"])
                    for a in range(2):
                        wv, wk = load_w(wb, w_in[:, OFF_Q + (4 * g + 2 * a) * 64:OFF_Q + (4 * g + 2 * a + 2) * 64], 128)
                        rope_proj(wv, wk, NT, [(qT[:, a, :], slice(0, 128), kqT)])
                        wv, wk = load_w(wb, w_in[:, OFF_G + (4 * g + 2 * a) * 64:OFF_G + (4 * g + 2 * a + 2) * 64], 128)
                        for (t0, n) in pieces(NT):
                            ps, pk = inproj_T(wv, wk, 0, 128, XT, XK, t0, n)
                            act(sgT[:, a, t0:t0 + n], ps, AF.Silu, [pk], [ksg])
                    if g == 3:
                        xall = ["XTl%d" % t for t in range(17)]
                        for jb in range(4):
                            dma("pool", wo[:, :, jb * 512:(jb + 1) * 512],
                                w_out[:, jb * 512:(jb + 1) * 512].rearrange("(k p) n -> p k n", p=128), writes=["wo%d" % jb] + xall)
                        for jb in range(4):
                            for ch in range(16):
                                gv = attg_s[:, ch:ch + 1] if ch < 8 else ssdg_s[:, ch - 8:ch - 7]
                                ts(wo[:, ch, jb * 512:(jb + 1) * 512], wo[:, ch, jb * 512:(jb + 1) * 512], gv, None, ALU.mult, None,
                                   ["wo%d" % jb, "attg", "ssdg"], ["wo%d" % jb])
                    for n in range(16):
                        chunks = []
                        if n > 0:
                            chunks.append((n - 1, "k", TGEb))
                        chunks.append((n, "k", None))
                        chunks.append((n + 1, "k", TLEb))
                        chunks.append((17, "c", None))
                        chunks.append((18, "c", None))
                        qs = slice(n * 128, (n + 1) * 128)
                        for ci, (tile, kind, msk) in enumerate(chunks):
                            pb_ = 6 + ci % 2
                            for b_ in range(2):
                                if kind == "k":
                                    lk = (kT if b_ == 0 else kTB)[:, tile * 128:(tile + 1) * 128]
                                    lkk = kkT if b_ == 0 else kkTB
                                else:
                                    lk = (kctxT if b_ == 0 else kctxB)[:, g * 256 + (tile - 17) * 128:g * 256 + (tile - 16) * 128]
                                    lkk = "kctxT" if b_ == 0 else "kctxB"
                                mm(pbank[pb_][:, b_ * 256:(b_ + 1) * 256], lk, qT[:, :, qs], True, True, [lkk, kqT], ["ps%d" % pb_])
                            act(PT[:, ci, :], pbank[pb_][:, :], AF.Exp, ["ps%d" % pb_], ["PT%d" % ci], scale=0.125)
                            if msk is not None:
                                p4 = PT[:, ci, :].rearrange("p (h n) -> p h n", h=4)
                                tt(p4, p4, msk.rearrange("p (o n) -> p o n", o=1).to_broadcast([128, 4, 128]), ALU.mult,
                                   ["PT%d" % ci, "cb16"], ["PT%d" % ci])
                        nci = len(chunks)
                        for ci, (tile, kind, msk) in enumerate(chunks):
                            mm(pbank[2][:, 0:256], VA[:, tile, :], PT[:, ci, 0:256], ci == 0, ci == nci - 1, ["VA", "PT%d" % ci], ["ps2"])
                        for ci, (tile, kind, msk) in enumerate(chunks):
                            mm(pbank[4][:, 0:256], VB[:, tile, :], PT[:, ci, 256:512], ci == 0, ci == nci - 1, ["VB", "PT%d" % ci], ["ps4"])
                        for b_ in range(2):
                            po = pbank[2] if b_ == 0 else pbank[4]
                            pk = "ps2" if b_ == 0 else "ps4"
                            nr = slice(b_ * 64, (b_ + 1) * 64)
                            dr = slice((1 - b_) * 64, (2 - b_) * 64)
                            for a_ in range(2):
                                h = 4 * g + 2 * a_ + b_
                                act(lnd[b_][dr, a_ * 128:(a_ + 1) * 128], po[dr, a_ * 128:(a_ + 1) * 128], AF.Ln, [pk, "esink"], ["lnd%d" % b_],
                                    bias=esink[dr, h:h + 1])
                            act(rd[b_][dr, :], lnd[b_][dr, :], AF.Exp, ["lnd%d" % b_], ["rd%d" % b_], scale=-1.0)
                            tt(t1[b_][nr, :], po[nr, 0:256], rd[b_][dr, :], ALU.mult, [pk, "rd%d" % b_], ["t1%d" % b_])
                            tt(agT[nr, 2 * g:2 * g + 2, qs], t1[b_][nr, :].rearrange("p (a n) -> p a n", a=2), sgT[nr, :, qs], ALU.mult,
                               ["t1%d" % b_, ksg], ["agT"])
                A.reset(m4)
                if "ag" in dbg_out:
                    for ch in range(4):
                        t_ = A.take(NT * 4, F32)
                        cp(t_, agT[:, ch, :], ["agT"], ["t_ag%d" % ch])
                        dma("sp", dbg_out["ag"][ch], t_, ["t_ag%d" % ch], ["dbg_ag"])
                    A.reset(m4)
                    for ch in range(4, 8):
                        t_ = A.take(NT * 4, F32)
                        cp(t_, agT[:, ch, :], ["agT"], ["t_ag%d" % ch])
                        dma("sp", dbg_out["ag"][ch], t_, ["t_ag%d" % ch], ["dbg_ag"])
                    A.reset(m4)

            if stage >= 7:
                S.barrier()
                A.off = m2
                A.off = m4
                gate_bc = A.take(D * 4, F32)
                fng_bc = A.take(D * 4, F32)
                xt_ = [A.take(D * 4, F32) for _ in range(2)]
                res = A.take(D * 4, F32)
                o1 = [A.take(512 * 4, F32) for _ in range(2)]
                sq = [A.take(128 * 2, BF16) for _ in range(2)]
                st2 = [A.take(64, F32) for _ in range(2)]
                ytile = [A.take(8 * 128 * 2, BF16) for _ in range(2)]
                dma("sp", gate_bc, gate_scr.to_broadcast([128, D]), ["gate_scr"], ["gate_bc"])
                dma("sp", fng_bc, fng.to_broadcast([128, D]), writes=["fng_bc"])
                ones2 = TLEb[:, 126:128]
                for t in range(16):
                    b_ = t % 2
                    tsl = slice(t * 128, (t + 1) * 128)
                    xk_ = "xt%d" % b_
                    sk_ = "st2%d" % b_
                    dma("sp", xt_[b_], xloc[tsl, :], writes=[xk_])
                    dma("sp", ytile[b_], h_scr[t].rearrange("p c n -> p (c n)"), ["h_scr"], ["yt%d" % b_])
                    ygT = ytile[b_].rearrange("p (c n) -> p c n", c=8)
                    for br, (src, srck, sl_) in enumerate([(agT, "agT", tsl), (ygT, "yt%d" % b_, slice(0, 128))]):
                        for ch in range(8):
                            act(sq[ch % 2], src[:, ch, sl_], AF.Square, [srck], ["sq%d" % (ch % 2)])
                            mm(pbank[3][:, br * 2:br * 2 + 2], sq[ch % 2], ones2, ch == 0, ch == 7, ["sq%d" % (ch % 2), "cb16"], ["ps3"])
                    s_ = st2[b_]
                    ts(s_[:, 0:4], pbank[3][:, 0:4], 1.0 / 1024.0, EPS, ALU.mult, ALU.add, ["ps3"], [sk_])
                    act(s_[:, 0:4], s_[:, 0:4], AF.Sqrt, [sk_], [sk_])
                    S.dve(lambda e, o=s_[:, 4:8], i=s_[:, 0:4]: e.reciprocal(out=o, in_=i), [sk_], [sk_])
                    for cb in range(4):
                        cs_ = slice(cb * 512, (cb + 1) * 512)
                        pa, pka = (pbank[0], "ps0") if cb % 2 == 0 else (pbank[6], "ps6")
                        pss, pks = (pbank[1], "ps1") if cb % 2 == 0 else (pbank[7], "ps7")
                        for ch in range(8):
                            mm(pa[:, :], agT[:, ch, tsl], wo[:, ch, cs_], ch == 0, ch == 7, ["agT", "wo%d" % cb], [pka])
                        for ch in range(8):
                            mm(pss[:, :], ygT[:, ch, :], wo[:, 8 + ch, cs_], ch == 0, ch == 7, ["yt%d" % b_, "wo%d" % cb], [pks])
                        ok_ = "o1%d" % (cb % 2)
                        oo = o1[cb % 2]
                        act(oo, pa[:, :], AF.Identity, [pka, sk_], [ok_], scale=s_[:, 5:6])
                        stt(oo, pss[:, :], s_[:, 7:8], oo, ALU.mult, ALU.add, [pks, sk_, ok_], [ok_])
                        tt(oo, oo, gate_bc[:, cs_], ALU.mult, [ok_, "gate_bc"], [ok_])
                        tt(res[:, cs_], oo, xt_[b_][:, cs_], ALU.add, [ok_, xk_], ["res"])
                    act(xt_[b_], res, AF.Square, ["res"], [xk_, sk_], accum=s_[:, 8:9])
                    ts(s_[:, 9:10], s_[:, 8:9], 1.0 / D, EPS, ALU.mult, ALU.add, [sk_], [sk_])
                    act(s_[:, 10:11], s_[:, 9:10], AF.Sqrt, [sk_], [sk_])
                    S.dve(lambda e, o=s_[:, 11:12], i=s_[:, 10:11]: e.reciprocal(out=o, in_=i), [sk_], [sk_])
                    stt(xt_[b_], res, s_[:, 11:12], fng_bc, ALU.mult, ALU.mult, ["res", sk_, "fng_bc"], [xk_])
                    dma("sp", out[tsl, :], xt_[b_], [xk_], ["out"])

        fw = [(q, i) for q in ("sp", "pool") for i, op in enumerate(S.ops[q]) if op["dma"]]
        S.emit(final_waits=fw)
    return nc


def _fm(v, nchunk):
    return np.ascontiguousarray(np.asarray(v, np.float32).reshape(nchunk, 128).T)


def _consts():
    k = np.arange(128)[:, None]
    l = np.arange(128)[None, :]
    ident = (k == l).astype(np.float32)
    tle = (k <= l).astype(np.float32)
    tge = (k >= l).astype(np.float32)
    su = (k > l).astype(np.float32)
    sl = (k < l).astype(np.float32)
    d = np.arange(128)
    sw = np.where((d % 32) < 16, d + 16, d - 16)
    perm = np.zeros((128, 128), np.float32)
    perm[sw, d] = 1.0
    return np.ascontiguousarray(np.concatenate([ident, tle, tge, su, sl, perm], axis=1))


def _rope_tables(pos):
    pos = np.asarray(pos)
    row = (pos // 64).astype(np.float32)
    col = (pos % 64).astype(np.float32)
    quarter = 16
    freq = (1.0 / (np.float32(10000.0) ** (np.arange(quarter, dtype=np.float32) / np.float32(quarter)))).astype(np.float32)
    cosT = np.zeros((128, len(pos)), np.float32)
    ssinT = np.zeros((128, len(pos)), np.float32)
    for p in range(128):
        dd = p % 64
        i = dd % 16
        pp = row if dd < 32 else col
        ang = (pp * freq[i]).astype(np.float32)
        cosT[p] = np.cos(ang)
        s = np.sin(ang)
        ssinT[p] = -s if (dd % 32) < 16 else s
    return cosT, ssinT


def prep_inputs(inp):
    g = lambda k: np.asarray(inp[k], np.float32)
    x, c, ctx, c_ctx = g("x"), g("c"), g("ctx"), g("c_ctx")
    w_mod, b_mod, norm_g, w_in = g("w_mod")[0], g("b_mod")[0], g("norm_g")[0], g("w_in")[0]
    conv_w, conv_b = g("conv_w")[0], g("conv_b")[0]
    a_f, a_b, db_f, db_b = g("a_log_f")[0], g("a_log_b")[0], g("dt_bias_f")[0], g("dt_bias_b")[0]
    d_skip, attg, ssdg, sink = g("d_skip")[0], g("att_norm_g")[0], g("ssd_norm_g")[0], g("sink")[0]
    w_out, fng = g("w_out")[0], g("final_norm_g")
    consts = _consts()
    shared = dict(w_in=np.ascontiguousarray(w_in), w_out=np.ascontiguousarray(w_out),
                  normgT=_fm(norm_g, 16), dskipT=_fm(np.repeat(d_skip, 64), 8), attgT=_fm(attg, 8), ssdgT=_fm(ssdg, 8),
                  sinkb=np.ascontiguousarray(np.broadcast_to(sink[None, :], (128, 16))),
                  dskb=np.ascontiguousarray(np.broadcast_to(d_skip[None, :], (128, 16))),
                  fng=np.ascontiguousarray(fng[None, :]), consts=consts)
    maps = []
    for core in range(8):
        b, h = core // 2, core % 2
        xb = x[b]
        xl = xb if h == 0 else xb[::-1]
        pos = np.arange(SEQ) if h == 0 else np.arange(SEQ)[::-1]
        cl = ctx[b] if h == 0 else ctx[b][::-1]
        cw = conv_w if h == 0 else conv_w[::-1]
        convT = np.stack([_fm(cw[0], 16), _fm(cw[1], 16), _fm(cw[2], 16), _fm(conv_b, 16)], axis=2).reshape(128, 64)
        if h == 0:
            wdt = w_in[:, OFF_DTF:OFF_DTF + 32]
            dtbv = np.concatenate([db_f, db_b])
            alv = np.concatenate([a_f, a_b])
        else:
            wdt = np.concatenate([w_in[:, OFF_DTB:OFF_DTB + 16], w_in[:, OFF_DTF:OFF_DTF + 16]], axis=1)
            dtbv = np.concatenate([db_b, db_f])
            alv = np.concatenate([a_b, a_f])
        cosT, ssinT = _rope_tables(pos[0:NTH])
        cvec = np.stack([_fm(c[b], 16), _fm(c_ctx, 16)], axis=2).reshape(128, 32)
        chs = [2 * i + h for i in range(16)]
        gbs = [2 * gb + h for gb in range(2)]
        wsh = np.concatenate([w_mod[:, ch * 128:(ch + 1) * 128] for ch in chs] +
                             [w_mod[:, 2 * D + k * 512:2 * D + (k + 1) * 512] for k in gbs], axis=1)
        bml = np.stack([b_mod[ch * 128:(ch + 1) * 128] for ch in chs], axis=1)
        bgl = np.concatenate([b_mod[2 * D + k * 512:2 * D + (k + 1) * 512] for k in gbs])[None, :]
        rm = np.zeros((128, 2), np.float32)
        rm[:, h] = 1.0
        m = dict(shared)
        m.update(xloc=np.ascontiguousarray(xl[0:NTH]), ctxf=np.ascontiguousarray(cl),
                 w_mod_sh=np.ascontiguousarray(wsh), bmod_l=np.ascontiguousarray(bml), bgate_l=np.ascontiguousarray(bgl), rmask=rm,
                 cvec=np.ascontiguousarray(cvec), w_dt=np.ascontiguousarray(wdt), convT=np.ascontiguousarray(convT),
                 dtb=np.ascontiguousarray(np.broadcast_to(dtbv[None, :], (128, 32))),
                 alog=np.ascontiguousarray(np.broadcast_to(alv[None, :], (128, 32))),
                 cosT=cosT, ssinT=ssinT)
        maps.append(m)
    return maps


def kernel(**inputs):
    maps = prep_inputs(inputs)
    nc = build()
    res = run_bass_kernel_spmd(nc, maps, core_ids=list(range(8)))
    outp = np.zeros((4, SEQ, D), np.float32)
    for core in range(8):
        b, h = core // 2, core % 2
        o = res.results[core]["out"]
        if h == 0:
            outp[b, 0:NT] = o
        else:
            outp[b, NT:SEQ] = o[::-1]
    return outp
```

```python
import contextlib
import os
import numpy as np
CUT = int(os.environ.get('KCUT', '99'))
CUT2 = int(os.environ.get('KCUT2', '99'))
import concourse.bass as bass
import concourse.mybir as mybir
from concourse.bass_utils import run_bass_kernel_spmd

F32 = mybir.dt.float32
BF16 = mybir.dt.bfloat16
AF = mybir.ActivationFunctionType
ALU = mybir.AluOpType

D = 2048
NT = 2048
NTH = 2176
SEQ = 4096
NCTX = 256
NIN = 5664
OFF_Q, OFF_K, OFF_V, OFF_G, OFF_Z, OFF_XS, OFF_B, OFF_C = 0, 1024, 1280, 1536, 2560, 3584, 4608, 5120
OFF_DTF, OFF_DTB = 5632, 5648
EPS = 1e-6


class Sched:
    ENGS = ["pe", "act", "dve", "pool", "sp"]
    NDMASEM = 4
    WINDOW = 128

    def __init__(self, nc):
        self.nc = nc
        self.ops = {e: [] for e in self.ENGS}
        self.lastw = {}
        self.readers = {}
        self.ndma = {e: 0 for e in self.ENGS}
        self.epoch = 0

    def barrier(self):
        self.epoch += 1
        self.lastw = {}
        self.readers = {}

    def add(self, eng, fn, reads=(), writes=(), dma=False, cost=0.3):
        idx = len(self.ops[eng])
        psr = [k for k in reads if k.startswith("ps")]
        if psr:
            reads = [k for k in reads if not k.startswith("ps")]
            writes = list(writes) + [k for k in psr if k not in writes]
        deps = set()
        for k in reads:
            if k in self.lastw:
                deps.add(self.lastw[k])
        for k in writes:
            if k in self.lastw:
                deps.add(self.lastw[k])
            for r in self.readers.get(k, ()):
                deps.add(r)
        deps.discard((eng, idx))
        odeps = set()
        if eng == "pe":
            odeps = set(d for d in deps if d[0] == "pe")
            deps = deps - odeps
        op = dict(fn=fn, deps=deps, odeps=odeps, dma=dma, marked=False, cost=cost, epoch=self.epoch)
        self.ops[eng].append(op)
        for k in reads:
            self.readers.setdefault(k, []).append((eng, idx))
        for k in writes:
            self.lastw[k] = (eng, idx)
            self.readers[k] = []
        return (eng, idx)

    def pe(self, fn, reads=(), writes=(), cost=0.15):
        return self.add("pe", fn, reads, writes, cost=cost)

    def act(self, fn, reads=(), writes=(), cost=0.4):
        return self.add("act", fn, reads, writes, cost=cost)

    def dve(self, fn, reads=(), writes=(), cost=0.4):
        return self.add("dve", fn, reads, writes, cost=cost)

    def dma(self, q, fn, reads=(), writes=(), cost=4.0):
        return self.add(q, fn, reads, writes, dma=True, cost=cost)

    def schedule(self):
        ops = self.ops
        pending = {e: list(range(len(ops[e]))) for e in self.ENGS}
        done = {e: [None] * len(ops[e]) for e in self.ENGS}
        free = {e: 0.0 for e in self.ENGS}
        order = {e: [] for e in self.ENGS}
        dma_free = [0.0]
        self.fence_pending = {}
        ntot = sum(len(v) for v in pending.values())
        SYNC = 0.2
        epoch = 0
        last_exec = {}
        for _ in range(ntot):
            while not any(pending[e] and ops[e][pending[e][0]]["epoch"] == epoch for e in self.ENGS):
                epoch += 1
                tmax = max(free.values())
                for e in self.ENGS:
                    free[e] = tmax
                    if pending[e]:
                        nxt = [i for i in pending[e] if ops[e][i]["epoch"] == epoch]
                        if nxt:
                            self.fence_pending.setdefault(e, set()).update(last_exec.values())
                for k in [k for k in last_exec if isinstance(k, tuple)]:
                    del last_exec[k]
            best = None
            for e in self.ENGS:
                pl = pending[e]
                fe = free[e]
                for wi in range(min(self.WINDOW, len(pl))):
                    i = pl[wi]
                    op = ops[e][i]
                    if op["epoch"] != epoch:
                        break
                    st = fe
                    ok = True
                    for (de, di) in op["deps"]:
                        t = done[de][di]
                        if t is None:
                            ok = False
                            break
                        if t + SYNC > st:
                            st = t + SYNC
                    if ok:
                        for (de, di) in op["odeps"]:
                            if done[de][di] is None:
                                ok = False
                                break
                    if not ok:
                        continue
                    key = (st, wi)
                    if best is None or key < best[0]:
                        best = (key, e, wi, i, st)
                    if st <= fe:
                        break
            assert best is not None, "scheduler deadlock"
            _, e, wi, i, st = best
            op = ops[e][i]
            if op["dma"]:
                free[e] = st + 0.15
                t0 = max(st + 2.0, dma_free[0])
                dma_free[0] = t0 + op["cost"]
                done[e][i] = dma_free[0]
            else:
                free[e] = st + op["cost"]
                done[e][i] = free[e]
            fp = self.fence_pending.pop(e, None)
            if fp:
                op["deps"] = set(op["deps"]) | set(d for d in fp if d != (e, i))
            if op["dma"]:
                last_exec[(e, i)] = (e, i)
            else:
                last_exec[e] = (e, i)
            order[e].append(i)
            del pending[e][wi]
        self.est_time = max(max([0.0] + [t for t in done[e] if t is not None]) for e in self.ENGS)
        return order

    def emit(self, final_waits=()):
        nc = self.nc
        ops = self.ops
        fin = dict(fn=None, deps=set(final_waits), odeps=set(), dma=False, marked=False, cost=0.0, epoch=self.epoch)
        ops["sp"].append(fin)
        order = self.schedule()
        for e in self.ENGS:
            for op in ops[e]:
                for (de, di) in op["deps"]:
                    ops[de][di]["marked"] = True
        for e in self.ENGS:
            c = 0
            nd = 0
            for i in order[e]:
                op = ops[e][i]
                if op["dma"]:
                    op["dmaslot"] = nd
                    nd += 1
                elif op["marked"]:
                    c += 1
                    op["cnt"] = c
        with contextlib.ExitStack() as st:
            esem = {e: st.enter_context(nc.semaphore("s_" + e)) for e in self.ENGS}
            dsem = {}
            for e in self.ENGS:
                if self.ndma[e] or any(op["dma"] for op in ops[e]):
                    dsem[e] = [st.enter_context(nc.semaphore("d_%s%d" % (e, i)))
                               for i in range(self.NDMASEM)]
            block = st.enter_context(nc.Block())
            engobj = {"pe": "tensor", "act": "scalar", "dve": "vector", "pool": "gpsimd", "sp": "sync"}

            def run(e, eng):
                seen = {}
                seen_dma = set()
                for i in order[e]:
                    op = ops[e][i]
                    waits = []
                    if op["dma"]:
                        slot = op["dmaslot"]
                        if slot >= self.NDMASEM:
                            waits.append((dsem[e][slot % self.NDMASEM], 16 * (slot // self.NDMASEM)))
                    cneed = {}
                    for (de, di) in op["deps"]:
                        dop = ops[de][di]
                        if dop["dma"]:
                            if (de, di) in seen_dma:
                                continue
                            seen_dma.add((de, di))
                            s = dop["dmaslot"]
                            waits.append((dsem[de][s % self.NDMASEM], 16 * (s // self.NDMASEM + 1)))
                        else:
                            c = dop["cnt"]
                            if seen.get(de, 0) >= c:
                                continue
                            cneed[de] = max(cneed.get(de, 0), c)
                    for de, c in cneed.items():
                        seen[de] = c
                        waits.append((esem[de], c))
                    for (s, v) in waits:
                        eng.wait_ge(s, v)
                    if op["fn"] is None:
                        continue
                    ins = op["fn"](eng)
                    if op["dma"]:
                        ins.then_inc(dsem[e][op["dmaslot"] % self.NDMASEM], 16)
                    elif op["marked"]:
                        ins.then_inc(esem[e], 1)

            for e in self.ENGS:
                if not ops[e]:
                    continue

                def mk(e):
                    def _f(eng):
                        run(e, eng)
                    return _f
                getattr(block, engobj[e])(mk(e))


def build(stage=99, dbg=(), ncores=8):
    nc = bass.Bass("TRN2", target_bir_lowering=False)

    def din(name, shape):
        return nc.dram_tensor(name, list(shape), F32, kind="ExternalInput").ap()

    xloc = din("xloc", [NTH, D])
    ctxf = din("ctxf", [NCTX, D])
    cvec = din("cvec", [128, 32])
    w_mod_sh = din("w_mod_sh", [D, 3 * D // 2])
    bmod_l = din("bmod_l", [128, 16])
    bgate_l = din("bgate_l", [1, D // 2])
    rmask = din("rmask", [128, 2])
    normgT = din("normgT", [128, 16])
    w_in = din("w_in", [D, NIN])
    w_dt = din("w_dt", [D, 32])
    convT = din("convT", [128, 16 * 4])
    dtb = din("dtb", [128, 32])
    alog = din("alog", [128, 32])
    dskipT = din("dskipT", [128, 8])
    attgT = din("attgT", [128, 8])
    ssdgT = din("ssdgT", [128, 8])
    sinkb = din("sinkb", [128, 16])
    dskb = din("dskb", [128, 16])
    w_out = din("w_out", [D, D])
    fng = din("fng", [1, D])
    cosT = din("cosT", [128, NTH])
    ssinT = din("ssinT", [128, NTH])
    consts = din("consts", [128, 6 * 128])
    out = nc.dram_tensor("out", [NT, D], F32, kind="ExternalOutput").ap()
    gate_scr = nc.dram_tensor("gate_scr", [1, D], F32).ap()
    ccg_in = nc.dram_tensor("ccg_in", [1, D], F32)
    ccm_in = nc.dram_tensor("ccm_in", [128, 64], F32)
    ccm_out = nc.dram_tensor("ccm_out", [128, 64], F32)
    h_scr = nc.dram_tensor("h_scr", [16, 128, 8, 128], BF16).ap()
    cc_in = [nc.dram_tensor("cc_in%d" % g, [128, 256], F32) for g in range(4)]
    cc_out = [nc.dram_tensor("cc_out%d" % g, [128, 256], F32) for g in range(4)]
    dbg_out = {}
    for (nm, shp) in dbg:
        dbg_out[nm] = nc.dram_tensor("dbg_" + nm, list(shp), F32, kind="ExternalOutput").ap()

    st = contextlib.ExitStack()
    with st:
        S = Sched(nc)
        ARENA = 189 * 1024
        arena = st.enter_context(nc.sbuf_tensor("arena", [128, ARENA // 2], BF16))
        cst = st.enter_context(nc.sbuf_tensor("cst", [128, 1472], F32))
        pbank = [st.enter_context(nc.psum_tensor("pb%d" % i, [128, 512], F32)) for i in range(8)]

        class Ar:
            def __init__(self):
                self.off = 0

            def take(self, nbytes, dt, parts=128):
                nbytes = (nbytes + 63) // 64 * 64
                o = self.off
                self.off += nbytes
                assert self.off <= ARENA, ("arena overflow", self.off)
                v = arena[0:parts, o // 2:(o + nbytes) // 2]
                if dt == F32:
                    v = v.bitcast(F32)
                return v

            def mark(self):
                return self.off

            def reset(self, m):
                self.off = m
                S.barrier()

        A = Ar()
        coff = [0]

        def ctake(n):
            o = coff[0]
            coff[0] += n
            assert coff[0] <= 1472
            return cst[:, o:o + n]

        def fsz(ap):
            n = 1
            for d in ap.shape[1:]:
                n *= d
            return n

        def dma(q, o, i, reads=(), writes=()):
            esz = 4 if i.dtype == F32 else 2
            nbytes = fsz(i) * i.shape[0] * esz
            return S.dma(q, lambda e: e.dma_start(out=o, in_=i), reads, writes, cost=nbytes / 250e3)

        def mm(o, lhsT, rhs, start, stop, reads, writes):
            c = max(0.11, 0.006 + fsz(rhs) / 2400.0)
            if rhs.dtype == F32:
                c = 0.05 + fsz(rhs) / 800.0
            return S.pe(lambda e: e.matmul(o, lhsT=lhsT, rhs=rhs, start=start, stop=stop), reads, writes, cost=c)

        def tr(o, i, reads, writes):
            return S.pe(lambda e: e.transpose(o, i, ident_b), list(reads) + ["cb16"], writes, cost=0.12)

        def act(o, i, func, reads, writes, bias=None, scale=None, accum=None):
            kw = {}
            c = 0.22 + fsz(o) / 1400.0
            if bias is not None:
                kw["bias"] = bias
            if scale is not None:
                kw["scale"] = scale
            if accum is not None:
                kw["accum_out"] = accum
                c += 0.1
            return S.act(lambda e: e.activation(out=o, in_=i, func=func, **kw), reads, writes, cost=c)

        def dcost(o, two=False):
            n = fsz(o)
            if o.dtype == F32 or two:
                return 0.08 + n / 960.0
            return 0.08 + n / 1200.0

        def tt(o, a, b, op, reads, writes, eng="dve"):
            return S.add(eng, lambda e: e.tensor_tensor(out=o, in0=a, in1=b, op=op), reads, writes, cost=dcost(o, True))

        def ts(o, a, s1, s2, op0, op1, reads, writes, eng="dve"):
            if op1 is None:
                return S.add(eng, lambda e: e.tensor_scalar(out=o, in0=a, scalar1=s1, scalar2=None, op0=op0), reads, writes,
                             cost=dcost(o))
            return S.add(eng, lambda e: e.tensor_scalar(out=o, in0=a, scalar1=s1, scalar2=s2, op0=op0, op1=op1), reads, writes,
                         cost=dcost(o))

        def stt(o, a, s, b, op0, op1, reads, writes):
            return S.dve(lambda e: e.scalar_tensor_tensor(out=o, in0=a, scalar=s, in1=b, op0=op0, op1=op1), reads, writes,
                         cost=dcost(o, True))

        def cp(o, i, reads, writes, eng="dve"):
            return S.add(eng, lambda e: e.tensor_copy(out=o, in_=i), reads, writes, cost=dcost(o))

        def memset(o, v, writes, eng="dve"):
            return S.add(eng, lambda e: e.memset(o, v), (), writes, cost=dcost(o))

        def pbf(i):
            return pbank[i][:, :].bitcast(BF16)

        Hst = st.enter_context(nc.sbuf_tensor("Hst", [128, 2048], F32))
        cf = ctake(768)
        dma("sp", cf, consts, writes=["cf"])
        ident_f = cf[:, 0:128]
        TLEf = cf[:, 128:256]
        TGEf = cf[:, 256:384]
        SUf = cf[:, 384:512]
        SLf = cf[:, 512:640]
        cb16 = A.take(6 * 128 * 2, BF16)
        cp(cb16, cf, ["cf"], ["cb16"])
        ident_b = cb16[:, 0:128]
        TLEb = cb16[:, 128:256]
        TGEb = cb16[:, 256:384]
        perm_b = cb16[:, 640:768]
        onesf = ctake(128)
        memset(onesf, 1.0, ["onesf"])
        small = ctake(32 + 32 + 16 + 64 + 32 + 32 + 8 + 8 + 8 + 16 + 16)
        rmask_s = ctake(2)
        cv = small[:, 0:32]
        bmod_s = small[:, 32:48]
        normg_s = small[:, 64:80]
        conv_s = small[:, 80:144]
        dtb_s = small[:, 144:176]
        alog_s = small[:, 176:208]
        dskip_s = small[:, 208:216]
        attg_s = small[:, 216:224]
        ssdg_s = small[:, 224:232]
        sink_s = small[:, 232:248]
        dskb_s = small[:, 248:264]
        for (dst, src, k) in [(cv, cvec, "cv"), (bmod_s, bmod_l, "bmod"), (rmask_s, rmask, "rmask"), (normg_s, normgT, "normg"), (conv_s, convT, "conv"),
                              (dtb_s, dtb, "dtb"), (alog_s, alog, "alog"), (dskip_s, dskipT, "dskip"), (attg_s, attgT, "attg"),
                              (ssdg_s, ssdgT, "ssdg"), (sink_s, sinkb, "sink"), (dskb_s, dskb, "dskb")]:
            dma("sp", dst, src, writes=[k])
        a_s = ctake(32)
        act(a_s, alog_s, AF.Exp, ["alog"], ["a_s"])
        ts(a_s, a_s, -1.0, None, ALU.mult, None, ["a_s"], ["a_s"])
        esink = ctake(16)
        act(esink, sink_s, AF.Exp, ["sink"], ["esink"])
        scb = A.take(32 * 2, BF16)
        act(scb, cv, AF.Silu, ["cv"], ["scb"])

        modT = ctake(64)
        PAIRS = [[2 * i, 2 * i + 1] for i in range(ncores // 2)]
        top = (ARENA - 41984) // 2
        wmb = [arena[:, top + i * 8192:top + (i + 1) * 8192].rearrange("p (k n) -> p k n", k=16) for i in range(2)]
        G2 = arena[:, top + 16384:top + 16384 + 2048].bitcast(F32)
        bgl = arena[:, top + 18432:top + 18432 + 2048].bitcast(F32)
        ml = arena[:, top + 20480:top + 20480 + 64].bitcast(F32)
        modl = arena[:, top + 20544:top + 20544 + 128].bitcast(F32)
        dma("sp", bgl[0:1, :], bgate_l, writes=["bgl"])
        psmod = pbank[3][:, 0:32]
        for lb in range(4):
            wb_ = wmb[lb % 2]
            dma("pool", wb_, w_mod_sh[:, lb * 512:(lb + 1) * 512].rearrange("(k p) n -> p k n", p=128), writes=["wmb%d" % (lb % 2)])
            for c4 in range(4):
                i = lb * 4 + c4
                for kc in range(16):
                    mm(psmod[:, i * 2:i * 2 + 2], wb_[:, kc, c4 * 128:(c4 + 1) * 128], scb[:, kc * 2:kc * 2 + 2],
                       kc == 0, kc == 15, ["wmb%d" % (lb % 2), "scb"], ["ps3"])
        tt(ml.rearrange("p (c j) -> p c j", j=2), psmod.rearrange("p (c j) -> p c j", j=2),
           bmod_s.rearrange("p (c o) -> p c o", o=1).to_broadcast([128, 16, 2]), ALU.add, ["ps3", "bmod"], ["ml"])
        tt(modl.rearrange("p (c r j) -> p c r j", r=2, j=2),
           ml.rearrange("p (c o j) -> p c o j", o=1, j=2).to_broadcast([128, 16, 2, 2]),
           rmask_s.rearrange("p (o r q) -> p o r q", o=1, q=1).to_broadcast([128, 16, 2, 2]), ALU.mult, ["ml", "rmask"], ["modl"])
        dma("sp", ccm_in.ap(), modl, ["modl"], ["ccm_in"])
        S.add("pool", lambda e: e.collective_compute("AllReduce", ALU.add, replica_groups=PAIRS,
                                                     ins=[ccm_in.ap().opt()], outs=[ccm_out.ap().opt()]),
              ["ccm_in"], ["ccm_out"], cost=12.0)
        dma("sp", modT, ccm_out.ap(), ["ccm_out"], ["modT"])
        for gb in range(2):
            wb_ = wmb[gb % 2]
            dma("pool", wb_, w_mod_sh[:, 2048 + gb * 512:2048 + (gb + 1) * 512].rearrange("(k p) n -> p k n", p=128),
                writes=["wmb%d" % (gb % 2)])
            for kc in range(16):
                mm(pbank[5][:, :], scb[:, kc * 2:kc * 2 + 1].to_broadcast([128, 128]), wb_[:, kc, :], kc == 0, kc == 15,
                   ["scb", "wmb%d" % (gb % 2)], ["ps5"])
            tt(G2[0:1, gb * 512:(gb + 1) * 512], pbank[5][0:1, :], bgl[0:1, gb * 512:(gb + 1) * 512], ALU.add, ["ps5", "bgl"], ["G2"])
        tt(Hst[0:1, :].rearrange("p (g r n) -> p g r n", g=2, r=2),
           G2[0:1, :].rearrange("p (g o n) -> p g o n", g=2, o=1).to_broadcast([1, 2, 2, 512]),
           rmask_s[0:1, :].rearrange("p (o r q) -> p o r q", o=1, q=1).to_broadcast([1, 2, 2, 512]), ALU.mult,
           ["G2", "rmask"], ["HF", "HB"])
        dma("sp", ccg_in.ap(), Hst[0:1, :], ["HF", "HB"], ["ccg_in"])
        S.add("pool", lambda e: e.collective_compute("AllReduce", ALU.add, replica_groups=PAIRS,
                                                     ins=[ccg_in.ap().opt()], outs=[gate_scr.tensor.ap().opt() if hasattr(gate_scr, "tensor") else gate_scr.opt()]),
              ["ccg_in"], ["gate_scr"], cost=12.0)
        modv = modT.rearrange("p (c j) -> p c j", j=2)
        gs = [ctake(16), ctake(16)]
        sh = [ctake(16), ctake(16)]
        for j in range(2):
            stt(gs[j], modv[:, 16:32, j], 1.0, normg_s, ALU.add, ALU.mult, ["modT", "normg"], ["gs%d" % j])
            cp(sh[j], modv[:, 0:16, j], ["modT"], ["sh%d" % j])

        if "mod" in dbg_out:
            dma("sp", dbg_out["mod"], modT, ["modT"], ["dbg_mod"])

        def alloc_build_set():
            return dict(xb=[A.take(D * 4, F32) for _ in range(2)], xn=[A.take(D * 2, BF16) for _ in range(2)],
                        junk=A.take(D * 2, BF16), stat=[A.take(64, F32) for _ in range(2)])

        def xkeys(pfx):
            def f(t0, n):
                return [pfx + "%d" % t for t in range(t0 // 128, (t0 + n - 1) // 128 + 1)]
            return f

        def build_xmT(XT, xkp, src, ntiles, j, bs):
            xb, xn, junk, stat = bs["xb"], bs["xn"], bs["junk"], bs["stat"]
            for t in range(ntiles):
                b_ = t % 2
                kx, kn, ks = "b_xb%d" % b_, "b_xn%d" % b_, "b_st%d" % b_
                dma("sp", xb[b_], src[t * 128:(t + 1) * 128, :], writes=[kx])
                act(junk, xb[b_], AF.Square, [kx], ["b_junk", ks], accum=stat[b_][:, 0:1])
                ts(stat[b_][:, 1:2], stat[b_][:, 0:1], 1.0 / D, EPS, ALU.mult, ALU.add, [ks], [ks])
                act(stat[b_][:, 2:3], stat[b_][:, 1:2], AF.Sqrt, [ks], [ks])
                S.dve(lambda e, o=stat[b_][:, 3:4], i=stat[b_][:, 2:3]: e.reciprocal(out=o, in_=i), [ks], [ks], cost=0.1)
                ts(xn[b_], xb[b_], stat[b_][:, 3:4], None, ALU.mult, None, [kx, ks], [kn])
                for q4 in range(4):
                    pt = pbf([2, 4][q4 % 2])[:, 0:512]
                    pk = "ps%d" % [2, 4][q4 % 2]
                    for i4 in range(4):
                        kc = q4 * 4 + i4
                        tr(pt[:, i4 * 128:(i4 + 1) * 128], xn[b_][:, kc * 128:(kc + 1) * 128], [kn], [pk])
                    o = XT[:, q4 * 4:q4 * 4 + 4, t * 128:(t + 1) * 128]
                    pv = pt.rearrange("p (c n) -> p c n", c=4)
                    if q4 % 2 == 0:
                        cp(o, pv, [pk], [xkp + "%d" % t])
                    else:
                        act(o, pv, AF.Copy, [pk], [xkp + "%d" % t])

        def modulate(XT, xkp, ntiles, j):
            keys = [xkp + "%d" % t for t in range(ntiles)]
            for kc in range(16):
                o = XT[:, kc, 0:ntiles * 128]
                if kc % 2 == 0:
                    ts(o, o, gs[j][:, kc:kc + 1], sh[j][:, kc:kc + 1], ALU.mult, ALU.add, keys + ["gs%d" % j, "sh%d" % j], keys)
                else:
                    act(o, o, AF.Identity, keys + ["gs%d" % j, "sh%d" % j], keys, bias=sh[j][:, kc:kc + 1], scale=gs[j][:, kc:kc + 1])

        wslot = [0]

        def load_w(wbufs, src_cols, ncols):
            i = wslot[0] % len(wbufs)
            wslot[0] += 1
            wv = wbufs[i][:, :, 0:ncols]
            dma("pool", wv, src_cols.rearrange("(k p) n -> p k n", p=128), writes=["wbuf%d" % i])
            return wv, "wbuf%d" % i

        pacc = [0]

        def inproj_T(wv, wk, c0, nco, XT, xk, t0, nt):
            b_ = pacc[0] % 2
            pacc[0] += 1
            ps = pbank[b_][0:nco, 0:nt]
            for kc in range(16):
                mm(ps, wv[:, kc, c0:c0 + nco], XT[:, kc, t0:t0 + nt], kc == 0, kc == 15, [wk] + xk(t0, nt), ["ps%d" % b_])
            return ps, "ps%d" % b_

        def inproj_tok(wv, wk, c0, nco, XT, xk, t0, ps, pk):
            for kc in range(16):
                mm(ps, XT[:, kc, t0:t0 + 128], wv[:, kc, c0:c0 + nco], kc == 0, kc == 15, [wk] + xk(t0, 128), [pk])

        def pieces(ntok):
            npc = (ntok + 511) // 512
            sz = ((ntok + npc - 1) // npc + 1) // 2 * 2
            r = []
            t = 0
            while t < ntok:
                n = min(sz, ntok - t)
                r.append((t, n))
                t += n
            return r

        def conv_chunk(wv, wk, c0, XT, xk, ntok, has_halo, cc, taps, outT, outk, stage_f, tmp_f, tb=0, left=False):
            if not left:
                memset(stage_f[:, 0:1], 0.0, ["stage"])
            if not has_halo:
                memset(stage_f[:, ntok + 1:ntok + 2], 0.0, ["stage"])
            lo = tb - (1 if left else 0)
            ntot = ntok + (1 if has_halo else 0) + (1 if left else 0)
            so = 0 if left else 1
            for (t0, n) in pieces(ntot):
                ps, pk = inproj_T(wv, wk, c0, 128, XT, xk, lo + t0, n)
                act(stage_f[:, so + t0:so + t0 + n], ps, AF.Copy, [pk], ["stage"])
            cw = conv_s.rearrange("p (c f) -> p c f", f=4)
            wA = cw[:, cc, taps[0]:taps[0] + 1]
            wB = cw[:, cc, taps[1]:taps[1] + 1]
            wC = cw[:, cc, taps[2]:taps[2] + 1]
            bb = cw[:, cc, 3:4]
            ts(tmp_f[:, 0:ntok], stage_f[:, 0:ntok], wA, None, ALU.mult, None, ["stage", "conv"], ["ctmp"])
            stt(tmp_f[:, 0:ntok], stage_f[:, 1:ntok + 1], wB, tmp_f[:, 0:ntok], ALU.mult, ALU.add, ["stage", "conv", "ctmp"], ["ctmp"])
            stt(tmp_f[:, 0:ntok], stage_f[:, 2:ntok + 2], wC, tmp_f[:, 0:ntok], ALU.mult, ALU.add, ["stage", "conv", "ctmp"], ["ctmp"])
            act(outT[:, tb:tb + ntok], tmp_f[:, 0:ntok], AF.Silu, ["ctmp", "conv"], [outk], bias=bb)

        def to_tok(srcT, srck, nchunks, dst, dstk, c_off):
            for c0 in range(0, nchunks, 4):
                n = min(4, nchunks - c0)
                hb = (c0 // 4) % 2
                pt = pbf([2, 4][hb])[:, 0:n * 128]
                pk = "ps%d" % [2, 4][hb]
                for i in range(n):
                    tr(pt[:, i * 128:(i + 1) * 128], srcT[:, (c0 + i) * 128:(c0 + i + 1) * 128], [srck], [pk])
                cp(dst[:, c0:c0 + n, c_off:c_off + 128], pt.rearrange("p (c n) -> p c n", n=128), [pk], [dstk], eng="dve")

        def alloc_dt_set(nch):
            n = nch * 16
            return [A.take(n * 4, F32) for _ in range(5)]

        def dt_prep(wdt_v, wdtk, dcol, XT, xk, nch, dirn, tg, bufs):
            n = nch * 16
            dtv, adt, wgt, dec, tmp = [b[:, 0:n] for b in bufs]
            ps = pbank[3][:, 0:n]
            for c in range(nch):
                inproj_tok(wdt_v, wdtk, dcol, 16, XT, xk, c * 128, ps[:, c * 16:(c + 1) * 16], "ps3")
            v3 = lambda a: a.rearrange("p (c h) -> p c h", h=16)
            if CUT2 < 1:
                cp(tmp, ps, ["ps3"], [tg + "tmp"])
                return None
            tt(v3(tmp), v3(ps), dtb_s[:, dcol:dcol + 16].rearrange("p (o h) -> p o h", o=1).to_broadcast([128, nch, 16]), ALU.add,
               ["ps3", "dtb"], [tg + "tmp"])
            if CUT2 < 2:
                return None
            act(tmp, tmp, AF.Exp, [tg + "tmp"], [tg + "tmp"])
            if CUT2 < 3:
                return None
            act(dtv, tmp, AF.Ln, [tg + "tmp"], [tg + "dt"], bias=1.0)
            if CUT2 < 4:
                return None
            tt(v3(adt), v3(dtv), a_s[:, dcol:dcol + 16].rearrange("p (o h) -> p o h", o=1).to_broadcast([128, nch, 16]), ALU.mult,
               [tg + "dt", "a_s"], [tg + "adt"])
            if CUT2 < 5:
                return None
            pcs = pbank[3][:, 0:n]
            mm(pcs, TLEf if dirn == "F" else TGEf, adt, True, True, ["cf", tg + "adt"], ["ps3"])
            cp(tmp, pcs, ["ps3"], [tg + "tmp"])
            if CUT2 < 6:
                return None
            ptot = pbank[3][:, 0:n]
            mm(ptot, onesf, adt, True, True, ["onesf", tg + "adt"], ["ps3"])
            if CUT2 < 7:
                cp(tmp, ptot, ["ps3"], [tg + "tmp"])
                return None
            act(dec, ptot, AF.Exp, ["ps3"], [tg + "dec"])
            tt(tmp, ptot, tmp, ALU.subtract, ["ps3", tg + "tmp"], [tg + "tmp"])
            act(tmp, tmp, AF.Exp, [tg + "tmp"], [tg + "tmp"])
            tt(wgt, tmp, dtv, ALU.mult, [tg + "tmp", tg + "dt"], [tg + "wgt"])
            return dict(dt=v3(dtv), adt=v3(adt), wgt=v3(wgt), dec=v3(dec))

        def alloc_scan_set(ntok):
            nch = ntok // 128
            return dict(wdt=A.take(16 * 32 * 2, BF16).rearrange("p (k n) -> p k n", k=16), dt=alloc_dt_set(nch),
                        wb=[A.take(16 * 128 * 2, BF16).rearrange("p (k n) -> p k n", k=16) for _ in range(2)],
                        stage=[A.take((ntok + 2) * 4, F32)], tmp=[A.take(ntok * 4, F32)],
                        cT=A.take(ntok * 2, BF16),
                        xs_tok=A.take(nch * 256 * 2, BF16).rearrange("p (c n) -> p c n", n=256),
                        b_tok=A.take(nch * 128 * 2, BF16).rearrange("p (c n) -> p c n", n=128),
                        xsw=[A.take(256 * 2, BF16) for _ in range(2)])

        def scanF(XT, xk, ntok, has_halo, taps, dcol, Hst, Hk, ss):
            nch = ntok // 128
            wdt_v = ss["wdt"]
            dma("pool", wdt_v, w_dt.rearrange("(k p) n -> p k n", p=128), writes=["s_wdt"])
            dd = dt_prep(wdt_v, "s_wdt", dcol, XT, xk, nch, "F", "s_d", ss["dt"])
            wb = ss["wb"]
            stage_f, tmp_f, cT = ss["stage"][0], ss["tmp"][0], ss["cT"]
            xs_tok = ss["xs_tok"][:, 0:nch, :]
            b_tok = ss["b_tok"][:, 0:nch, :]
            for g in range(4):
                wv, wk = load_w(wb, w_in[:, OFF_B + g * 128:OFF_B + (g + 1) * 128], 128)
                conv_chunk(wv, wk, 0, XT, xk, ntok, has_halo, 8 + g, taps, cT, "s_cT", stage_f, tmp_f)
                to_tok(cT, "s_cT", nch, b_tok, "s_btok", 0)
                for pr in range(2):
                    wv, wk = load_w(wb, w_in[:, OFF_XS + g * 256 + pr * 128:OFF_XS + g * 256 + (pr + 1) * 128], 128)
                    conv_chunk(wv, wk, 0, XT, xk, ntok, has_halo, g * 2 + pr, taps, cT, "s_cT", stage_f, tmp_f)
                    to_tok(cT, "s_cT", nch, xs_tok, "s_xstok", pr * 128)
                Hg = Hst[:, g * 256:(g + 1) * 256]
                for c in range(nch):
                    xsw = ss["xsw"][c % 2]
                    xswk = "s_xsw%d" % (c % 2)
                    tt(xsw.rearrange("p (h d) -> p h d", d=64), xs_tok[:, c, :].rearrange("p (h d) -> p h d", d=64),
                       dd["wgt"][:, c, g * 4:(g + 1) * 4].rearrange("p (h o) -> p h o", o=1).to_broadcast([128, 4, 64]), ALU.mult,
                       ["s_xstok", "s_dwgt"], [xswk])
                    pst = pbank[7][:, 0:256]
                    mm(pst, b_tok[:, c, :], xsw, True, True, ["s_btok", xswk], ["ps7"])
                    tt(Hg.rearrange("p (h d) -> p h d", d=64), Hg.rearrange("p (h d) -> p h d", d=64),
                       dd["dec"][:, c, g * 4:(g + 1) * 4].rearrange("p (h o) -> p h o", o=1).to_broadcast([128, 4, 64]), ALU.mult,
                       [Hk, "s_ddec"], [Hk])
                    tt(Hg, Hg, pst, ALU.add, [Hk, "ps7"], [Hk])

        HF = Hst[:, 0:1024]
        HB = Hst[:, 1024:2048]
        memset(HF, 0.0, ["HF"])
        kctxT = st.enter_context(nc.sbuf_tensor("kctxT", [128, 4 * 256], BF16))
        kctxB = st.enter_context(nc.sbuf_tensor("kctxB", [128, 4 * 256], BF16))
        memset(kctxT[64:128, :], 0.0, ["kctxT"])
        memset(kctxB[0:64, :], 0.0, ["kctxB"])
        vctx = st.enter_context(nc.sbuf_tensor("vctx", [128, 2 * 256], BF16))
        TAPS_F = (0, 1, 2)
        TAPS_R = (2, 1, 0)
        m2 = A.mark()
        XT = A.take(16 * NTH * 2, BF16).rearrange("p (k n) -> p k n", k=16)
        XK = xkeys("XTl")
        m3 = A.mark()
        bset = alloc_build_set()
        XTc = A.take(16 * NCTX * 2, BF16).rearrange("p (k n) -> p k n", k=16)
        sset = alloc_scan_set(NCTX)
        assert A.off <= ARENA - 41984
        xkc = xkeys("XTc")
        build_xmT(XTc, "XTc", ctxf, 2, 1, bset)
        build_xmT(XT, "XTl", xloc, 17, 0, bset)
        modulate(XTc, "XTc", 2, 1)
        modulate(XT, "XTl", 17, 0)
        scanF(XTc, xkc, NCTX, False, TAPS_F, 0, HF, "HF", sset)
        wkv = sset["wb"]
        for g in range(4):
            wv, wk = load_w(wkv, w_in[:, OFF_K + g * 64:OFF_K + (g + 1) * 64], 64)
            wi = (wslot[0] - 1) % 2
            dma("pool", wkv[wi][:, :, 64:128], w_in[:, OFF_K + g * 64:OFF_K + (g + 1) * 64].rearrange("(k p) n -> p k n", p=128),
                writes=[wk])
            ps, pk = inproj_T(wkv[wi][:, :, 0:128], wk, 0, 128, XTc, xkc, 0, NCTX)
            cp(kctxT[0:64, g * 256:(g + 1) * 256], ps[0:64, :], [pk], ["kctxT"])
            cp(kctxB[64:128, g * 256:(g + 1) * 256], ps[64:128, :], [pk], ["kctxB"])
        for hv in range(2):
            wv, wk = load_w(wkv, w_in[:, OFF_V + hv * 128:OFF_V + (hv + 1) * 128], 128)
            for t in range(2):
                psv = pbank[7][:, 0:128]
                inproj_tok(wv, wk, 0, 128, XTc, xkc, t * 128, psv, "ps7")
                cp(vctx[:, t * 256 + hv * 128:t * 256 + (hv + 1) * 128], psv, ["ps7"], ["vctx"])
        A.reset(m3)

        if "H" in dbg_out:
            dma("sp", dbg_out["H"], Hst[:, :], ["HF", "HB"], ["dbg_H"])
        if "kctx" in dbg_out and stage >= 2:
            t_ = A.take(1024 * 4, F32)
            cp(t_, kctxT[:, :], ["kctxT"], ["t_"])
            dma("sp", dbg_out["kctx"], t_, ["t_"], ["dbg_kctx"])

        if stage >= 5:
            wdt_v = A.take(16 * 32 * 2, BF16).rearrange("p (k n) -> p k n", k=16)
            dma("pool", wdt_v, w_dt.rearrange("(k p) n -> p k n", p=128), writes=["l_wdt"])
            dF = dt_prep(wdt_v, "l_wdt", 0, XT, XK, 16, "F", "lF", alloc_dt_set(16))
            dB = dt_prep(wdt_v, "l_wdt", 16, XT, XK, 16, "B", "lB", alloc_dt_set(16))
            wb = [A.take(16 * 128 * 2, BF16).rearrange("p (k n) -> p k n", k=16) for _ in range(2)]
            BTs = [A.take(NT * 2, BF16) for _ in range(2)]
            CT = A.take(NT * 2, BF16)
            xs_toks = [A.take(16 * 256 * 2, BF16).rearrange("p (c n) -> p c n", n=256) for _ in range(2)]
            b_toks = [A.take(16 * 128 * 2, BF16).rearrange("p (c n) -> p c n", n=128) for _ in range(2)]
            prevFas = [A.take(16 * 256 * 2, BF16).rearrange("p (c n) -> p c n", n=256) for _ in range(2)]
            szT = A.take(2 * NT * 2, BF16).rearrange("p (c n) -> p c n", c=2)
            prevB = A.take(16 * 256 * 2, BF16).rearrange("p (c n) -> p c n", n=256)
            xswF = [A.take(256 * 2, BF16) for _ in range(2)]
            ccin_s = A.take(256 * 4, F32)
            ccout_s = A.take(256 * 4, F32)
            Dsk = A.take(4 * 128 * 2, BF16).rearrange("p (h n) -> p h n", h=4)
            HALF = 1024
            stage_f = A.take((HALF + 2) * 4, F32)
            tmp_f = A.take(HALF * 4, F32)
            xsT = A.take(NT * 2, BF16)
            lhs4 = A.take(512 * 4, F32).rearrange("p (h n) -> p h n", h=4)
            L4 = A.take(512 * 2, BF16).rearrange("p (h n) -> p h n", h=4)
            E4 = A.take(512 * 2, BF16).rearrange("p (h n) -> p h n", h=4)
            G4 = [A.take(512 * 2, BF16).rearrange("p (h n) -> p h n", h=4) for _ in range(2)]
            GE4 = [A.take(512 * 2, BF16).rearrange("p (h n) -> p h n", h=4) for _ in range(2)]
            CBm2 = A.take(256 * 2, BF16).rearrange("p (d n) -> p d n", d=2)
            lhs4s = [lhs4, A.take(512 * 4, F32).rearrange("p (h n) -> p h n", h=4)]
            xdt = [A.take(256 * 2, BF16) for _ in range(2)]
            xsw = A.take(256 * 2, BF16)
            ystg = [A.take(256 * 2, BF16).rearrange("p (c n) -> p c n", c=2) for _ in range(2)]
            h3 = lambda a: a.rearrange("p (h d) -> p h d", d=64)
            bch = lambda a, c, g: a[:, c, g * 4:(g + 1) * 4].rearrange("p (h o) -> p h o", o=1).to_broadcast([128, 4, 64])
            for g in range(4):
                par = g % 2
                BT, xs_tok, b_tok, prevFa = BTs[par], xs_toks[par], b_toks[par], prevFas[par]
                kBT, kxs, kbt, kpf, kHB = "BT%d" % par, "xstok%d" % par, "btok%d" % par, "prevFa%d" % par, "HB%d" % par
                HFg = HF[:, g * 256:(g + 1) * 256]
                HBg = HB[:, par * 256:(par + 1) * 256]
                for hf in range(2):
                    tb = hf * HALF
                    wv, wk = load_w(wb, w_in[:, OFF_B + g * 128:OFF_B + (g + 1) * 128], 128)
                    conv_chunk(wv, wk, 0, XT, XK, HALF, True, 8 + g, TAPS_F, BT, kBT, stage_f, tmp_f, tb=tb, left=(hf == 1))
                to_tok(BT, kBT, 16, b_tok, kbt, 0)
                for pr in range(2):
                    for hf in range(2):
                        tb = hf * HALF
                        wv, wk = load_w(wb, w_in[:, OFF_XS + g * 256 + pr * 128:OFF_XS + g * 256 + (pr + 1) * 128], 128)
                        conv_chunk(wv, wk, 0, XT, XK, HALF, True, g * 2 + pr, TAPS_F, xsT, "xsT", stage_f, tmp_f, tb=tb, left=(hf == 1))
                    to_tok(xsT, "xsT", 16, xs_tok, kxs, pr * 128)
                for c in range(16):
                    act(prevFa[:, c, :], HFg, AF.Copy, ["HF"], [kpf])
                    xw = xswF[c % 2]
                    xwk = "xswF%d" % (c % 2)
                    tt(h3(xw), h3(xs_tok[:, c, :]), bch(dF["wgt"], c, g), ALU.mult, [kxs, "lFwgt"], [xwk])
                    mm(pbank[5][:, 0:256], b_tok[:, c, :], xw, True, True, [kbt, xwk], ["ps5"])
                    tt(h3(HFg), h3(HFg), bch(dF["dec"], c, g), ALU.mult, ["HF", "lFdec"], ["HF"])
                    tt(HFg, HFg, pbank[5][:, 0:256], ALU.add, ["HF", "ps5"], ["HF"])
                cp(ccin_s, HFg, ["HF"], ["ccin_s"])
                dma("sp", cc_in[g].ap(), ccin_s, ["ccin_s"], ["cc_in%d" % g])
                S.add("pool", lambda e, g=g: e.collective_compute("AllReduce", ALU.add, replica_groups=[[2 * i, 2 * i + 1] for i in range(ncores // 2)],
                                                                  ins=[cc_in[g].ap().opt()], outs=[cc_out[g].ap().opt()]),
                      ["cc_in%d" % g], ["cc_out%d" % g], cost=25.0)
                dma("sp", ccout_s, cc_out[g].ap(), ["cc_out%d" % g], ["ccout_s"])
                tt(HBg, ccout_s, ccin_s, ALU.subtract, ["ccout_s", "ccin_s"], [kHB])
                for hf in range(2):
                    tb = hf * HALF
                    wv, wk = load_w(wb, w_in[:, OFF_C + g * 128:OFF_C + (g + 1) * 128], 128)
                    conv_chunk(wv, wk, 0, XT, XK, HALF, True, 12 + g, TAPS_F, CT, "CT", stage_f, tmp_f, tb=tb, left=(hf == 1))
                for pr in range(2):
                    wv, wk = load_w(wb, w_in[:, OFF_Z + g * 256 + pr * 128:OFF_Z + g * 256 + (pr + 1) * 128], 128)
                    for (t0, n) in pieces(NT):
                        ps, pk = inproj_T(wv, wk, 0, 128, XT, XK, t0, n)
                        act(szT[:, pr, t0:t0 + n], ps, AF.Silu, [pk], ["szT"])
                for hh in range(4):
                    ts(Dsk[:, hh, :], ident_f, dskb_s[:, g * 4 + hh:g * 4 + hh + 1], None, ALU.mult, None, ["cf", "dskb"], ["Dsk"])
                for c in range(15, -1, -1):
                    act(prevB[:, c, :], HBg, AF.Copy, [kHB], ["prevB"])
                    tt(h3(xsw), h3(xs_tok[:, c, :]), bch(dB["wgt"], c, g), ALU.mult, [kxs, "lBwgt"], ["xsw"])
                    mm(pbank[5][:, 256:512], b_tok[:, c, :], xsw, True, True, [kbt, "xsw"], ["ps5"])
                    tt(h3(HBg), h3(HBg), bch(dB["dec"], c, g), ALU.mult, [kHB, "lBdec"], [kHB])
                    tt(HBg, HBg, pbank[5][:, 256:512], ALU.add, [kHB, "ps5"], [kHB])
                for c in range(16):
                    cs_ = slice(c * 128, (c + 1) * 128)
                    mm(pbank[5][:, 0:128], BT[:, cs_], CT[:, cs_], True, True, [kBT, "CT"], ["ps5"])
                    tt(CBm2, pbank[5][:, 0:128].rearrange("p (o n) -> p o n", o=1).to_broadcast([128, 2, 128]),
                       cf[:, 128:384].rearrange("p (d n) -> p d n", d=2), ALU.mult, ["ps5", "cf"], ["CBm"])
                    for di, (dd_, msk, tri) in enumerate([(dF, SUf, TLEf), (dB, SLf, TGEf)]):
                        pD, pE = 6, 7
                        dk = "lF" if di == 0 else "lB"
                        adt4 = dd_["adt"][:, c, g * 4:(g + 1) * 4]
                        lh = lhs4s[di]
                        if True:
                            tt(lh, msk.rearrange("p (o n) -> p o n", o=1).to_broadcast([128, 4, 128]),
                               adt4.rearrange("p (h o) -> p h o", o=1).to_broadcast([128, 4, 128]), ALU.mult, ["cf", dk + "adt"], ["lhs4%d" % di])
                        else:
                            for hh in range(4):
                                act(lh[:, hh, :], msk, AF.Copy, ["cf", dk + "adt"], ["lhs4%d" % di], scale=adt4[:, hh:hh + 1])
                        for hh in range(4):
                            mm(pbank[pD][:, hh * 128:(hh + 1) * 128], lh[:, hh, :], tri, True, True, ["lhs4%d" % di, "cf"], ["ps%d" % pD])
                        act(L4, pbank[pD][:, :].rearrange("p (h n) -> p h n", h=4), AF.Exp, ["ps%d" % pD], ["L4"])
                        tt(G4[di], L4, CBm2[:, di:di + 1, :].to_broadcast([128, 4, 128]), ALU.mult,
                           ["L4", "CBm"], ["G4%d" % di])
                        for hh in range(4):
                            mm(pbank[pE][:, hh * 128:(hh + 1) * 128], adt4[:, hh:hh + 1].to_broadcast([128, 128]), tri, True, True,
                               [dk + "adt", "cf"], ["ps%d" % pE])
                        act(E4, pbank[pE][:, :].rearrange("p (h n) -> p h n", h=4), AF.Exp, ["ps%d" % pE], ["E4"])
                        tt(GE4[di], E4, CT[:, cs_].rearrange("p (o n) -> p o n", o=1).to_broadcast([128, 4, 128]), ALU.mult,
                           ["E4", "CT"], ["GE4%d" % di])
                        tt(h3(xdt[di]), h3(xs_tok[:, c, :]), bch(dd_["dt"], c, g), ALU.mult, [kxs, dk + "dt"], ["xdt%d" % di])
                    for r in range(2):
                        yo = pbank[3][:, r * 256:(r + 1) * 256]
                        hs = slice(r * 128, (r + 1) * 128)
                        h2 = slice(2 * r, 2 * r + 2)
                        mm(yo, xdt[0][:, hs], G4[0][:, h2, :], True, False, ["xdt0", "G40"], ["ps3"])
                        mm(yo, prevFa[:, c, hs], GE4[0][:, h2, :], False, False, [kpf, "GE40"], ["ps3"])
                        mm(yo, xdt[1][:, hs], G4[1][:, h2, :], False, False, ["xdt1", "G41"], ["ps3"])
                        mm(yo, prevB[:, c, hs], GE4[1][:, h2, :], False, False, ["prevB", "GE41"], ["ps3"])
                        mm(yo, xs_tok[:, c, hs], Dsk[:, h2, :], False, True, [kxs, "Dsk"], ["ps3"])
                    yk = "ystg%d" % (c % 2)
                    for b_ in range(2):
                        rw = slice(b_ * 64, (b_ + 1) * 64)
                        yv = pbank[3][rw, :].rearrange("p (r b n) -> p r b n", r=2, b=2)[:, :, b_, :]
                        tt(ystg[c % 2][rw, :, :], yv, szT[rw, :, cs_], ALU.mult, ["ps3", "szT"], [yk])
                    dma("sp", h_scr[c][:, 2 * g:2 * g + 2, :], ystg[c % 2], [yk], ["h_scr"])
            A.reset(m3)

            if stage >= 6:
                agT = A.take(8 * NT * 2, BF16).rearrange("p (c n) -> p c n", c=8)
                m4 = A.mark()
                A.off = m2
                wo = A.take(16 * D * 2, BF16).rearrange("p (k n) -> p k n", k=16)
                assert A.off <= m3
                A.off = m4
                wb = [A.take(16 * 128 * 2, BF16).rearrange("p (k n) -> p k n", k=16) for _ in range(2)]
                kTs = [A.take(NTH * 2, BF16) for _ in range(2)]
                kTBs = [A.take(NTH * 2, BF16) for _ in range(2)]
                qTs = [A.take(2 * NT * 2, BF16).rearrange("p (a n) -> p a n", a=2) for _ in range(2)]
                sgTs = [A.take(2 * NT * 2, BF16).rearrange("p (a n) -> p a n", a=2) for _ in range(2)]
                VA = A.take(19 * 128 * 2, BF16).rearrange("p (t n) -> p t n", n=128)
                VB = A.take(19 * 128 * 2, BF16).rearrange("p (t n) -> p t n", n=128)
                cos_b = A.take(NTH * 2, BF16)
                ssin_b = A.take(NTH * 2, BF16)
                qraw = A.take(512 * 2, BF16)
                rt1 = Hst[:, 1024:1536]
                rt2 = Hst[:, 1536:2048]
                PT = A.take(5 * 512 * 2, BF16).rearrange("p (c n) -> p c n", c=5)
                lnd = [Hst[:, 0:256], Hst[:, 256:512]]
                rd = lnd
                t1 = [Hst[:, 512:768], Hst[:, 768:1024]]
                for par in range(2):
                    memset(kTs[par][64:128, :], 0.0, ["kT%d" % par])
                    memset(kTBs[par][0:64, :], 0.0, ["kTB%d" % par])
                memset(VA[:, :, 64:128], 1.0, ["VA"])
                memset(VB[:, :, 0:64], 1.0, ["VB"])
                dma("pool", cos_b, cosT, writes=["cos_b"])
                dma("pool", ssin_b, ssinT, writes=["ssin_b"])

                def rope_proj(wv2, wk, ntok, dsts):
                    for (t0, n) in pieces(ntok):
                        ps, pk = inproj_T(wv2, wk, 0, 128, XT, XK, t0, n)
                        act(qraw[:, 0:n], ps, AF.Copy, [pk], ["qraw"])
                        mm(pbank[5][:, 0:n], perm_b, qraw[:, 0:n], True, True, ["cb16", "qraw"], ["ps5"])
                        tt(rt1[:, 0:n], qraw[:, 0:n], cos_b[:, t0:t0 + n], ALU.mult, ["qraw", "cos_b"], ["rt1"])
                        tt(rt2[:, 0:n], pbank[5][:, 0:n], ssin_b[:, t0:t0 + n], ALU.mult, ["ps5", "ssin_b"], ["rt2"])
                        for (dst, rw, dstk) in dsts:
                            tt(dst[rw, t0:t0 + n], rt1[rw, 0:n], rt2[rw, 0:n], ALU.add, ["rt1", "rt2"], [dstk])

                for g in range(4):
                    par = g % 2
                    kT, kTB, qT, sgT = kTs[par], kTBs[par], qTs[par], sgTs[par]
                    kkT, kkTB, kqT, ksg = "kT%d" % par, "kTB%d" % par, "qT%d" % par, "sgT%d" % par
                    wv, wk = load_w(wb, w_in[:, OFF_K + g * 64:OFF_K + (g + 1) * 64], 64)
                    wi = (wslot[0] - 1) % 2
                    dma("pool", wb[wi][:, :, 64:128], w_in[:, OFF_K + g * 64:OFF_K + (g + 1) * 64].rearrange("(k p) n -> p k n", p=128),
                        writes=[wk])
                    rope_proj(wb[wi][:, :, 0:128], wk, NTH, [(kT, slice(0, 64), kkT), (kTB, slice(64, 128), kkTB)])
                    wv, wk = load_w(wb, w_in[:, OFF_V + g * 64:OFF_V + (g + 1) * 64], 64)
                    for t4 in range(0, 17, 4):
                        nt_ = min(4, 17 - t4)
                        for i in range(nt_):
                            inproj_tok(wv, wk, 0, 64, XT, XK, (t4 + i) * 128, pbank[5][:, i * 64:(i + 1) * 64], "ps5")
                        pv = pbank[5][:, 0:nt_ * 64].rearrange("p (t n) -> p t n", n=64)
                        cp(VA[:, t4:t4 + nt_, 0:64], pv, ["ps5"], ["VA"])
                        act(VB[:, t4:t4 + nt_, 64:128], pv, AF.Copy, ["ps5"], ["VB"])
                    vc3 = vctx[:, :].rearrange("p (t n) -> p t n", t=2)
                    cp(VA[:, 17:19, 0:64], vc3[:, :, g * 64:(g + 1) * 64], ["vctx"], ["VA"])
                    cp(VB[:, 17:19, 64:128], vc3[:, :, g * 64:(g + 1) * 64], ["vctx"], ["VB"])
                    for a in range(2):
                        wv, wk = load_w(wb, w_in[:, OFF_Q + (4 * g + 2 * a) * 64:OFF_Q + (4 * g + 2 * a + 2) * 64], 128)
                        rope_proj(wv, wk, NT, [(qT[:, a, :], slice(0, 128), kqT)])
                        wv, wk = load_w(wb, w_in[:, OFF_G + (4 * g + 2 * a) * 64:OFF_G + (4 * g + 2 * a + 2) * 64], 128)
                        for (t0, n) in pieces(NT):
                            ps, pk = inproj_T(wv, wk, 0, 128, XT, XK, t0, n)
                            act(sgT[:, a, t0:t0 + n], ps, AF.Silu, [pk], [ksg])
                    if g == 3:
                        xall = ["XTl%d" % t for t in range(17)]
                        for jb in range(4):
                            dma("pool", wo[:, :, jb * 512:(jb + 1) * 512],
                                w_out[:, jb * 512:(jb + 1) * 512].rearrange("(k p) n -> p k n", p=128), writes=["wo%d" % jb] + xall)
                        for jb in range(4):
                            for ch in range(16):
                                gv = attg_s[:, ch:ch + 1] if ch < 8 else ssdg_s[:, ch - 8:ch - 7]
                                ts(wo[:, ch, jb * 512:(jb + 1) * 512], wo[:, ch, jb * 512:(jb + 1) * 512], gv, None, ALU.mult, None,
                                   ["wo%d" % jb, "attg", "ssdg"], ["wo%d" % jb])
                    for n in range(16):
                        chunks = []
                        if n > 0:
                            chunks.append((n - 1, "k", TGEb))
                        chunks.append((n, "k", None))
                        chunks.append((n + 1, "k", TLEb))
                        chunks.append((17, "c", None))
                        chunks.append((18, "c", None))
                        qs = slice(n * 128, (n + 1) * 128)
                        for ci, (tile, kind, msk) in enumerate(chunks):
                            pb_ = 6 + ci % 2
                            for b_ in range(2):
                                if kind == "k":
                                    lk = (kT if b_ == 0 else kTB)[:, tile * 128:(tile + 1) * 128]
                                    lkk = kkT if b_ == 0 else kkTB
                                else:
                                    lk = (kctxT if b_ == 0 else kctxB)[:, g * 256 + (tile - 17) * 128:g * 256 + (tile - 16) * 128]
                                    lkk = "kctxT" if b_ == 0 else "kctxB"
                                mm(pbank[pb_][:, b_ * 256:(b_ + 1) * 256], lk, qT[:, :, qs], True, True, [lkk, kqT], ["ps%d" % pb_])
                            act(PT[:, ci, :], pbank[pb_][:, :], AF.Exp, ["ps%d" % pb_], ["PT%d" % ci], scale=0.125)
                            if msk is not None:
                                p4 = PT[:, ci, :].rearrange("p (h n) -> p h n", h=4)
                                tt(p4, p4, msk.rearrange("p (o n) -> p o n", o=1).to_broadcast([128, 4, 128]), ALU.mult,
                                   ["PT%d" % ci, "cb16"], ["PT%d" % ci])
                        nci = len(chunks)
                        for ci, (tile, kind, msk) in enumerate(chunks):
                            mm(pbank[2][:, 0:256], VA[:, tile, :], PT[:, ci, 0:256], ci == 0, ci == nci - 1, ["VA", "PT%d" % ci], ["ps2"])
                        for ci, (tile, kind, msk) in enumerate(chunks):
                            mm(pbank[4][:, 0:256], VB[:, tile, :], PT[:, ci, 256:512], ci == 0, ci == nci - 1, ["VB", "PT%d" % ci], ["ps4"])
                        for b_ in range(2):
                            po = pbank[2] if b_ == 0 else pbank[4]
                            pk = "ps2" if b_ == 0 else "ps4"
                            nr = slice(b_ * 64, (b_ + 1) * 64)
                            dr = slice((1 - b_) * 64, (2 - b_) * 64)
                            for a_ in range(2):
                                h = 4 * g + 2 * a_ + b_
                                act(lnd[b_][dr, a_ * 128:(a_ + 1) * 128], po[dr, a_ * 128:(a_ + 1) * 128], AF.Ln, [pk, "esink"], ["lnd%d" % b_],
                                    bias=esink[dr, h:h + 1])
                            act(rd[b_][dr, :], lnd[b_][dr, :], AF.Exp, ["lnd%d" % b_], ["rd%d" % b_], scale=-1.0)
                            tt(t1[b_][nr, :], po[nr, 0:256], rd[b_][dr, :], ALU.mult, [pk, "rd%d" % b_], ["t1%d" % b_])
                            tt(agT[nr, 2 * g:2 * g + 2, qs], t1[b_][nr, :].rearrange("p (a n) -> p a n", a=2), sgT[nr, :, qs], ALU.mult,
                               ["t1%d" % b_, ksg], ["agT"])
                A.reset(m4)
                if "ag" in dbg_out:
                    for ch in range(4):
                        t_ = A.take(NT * 4, F32)
                        cp(t_, agT[:, ch, :], ["agT"], ["t_ag%d" % ch])
                        dma("sp", dbg_out["ag"][ch], t_, ["t_ag%d" % ch], ["dbg_ag"])
                    A.reset(m4)
                    for ch in range(4, 8):
                        t_ = A.take(NT * 4, F32)
                        cp(t_, agT[:, ch, :], ["agT"], ["t_ag%d" % ch])
                        dma("sp", dbg_out["ag"][ch], t_, ["t_ag%d" % ch], ["dbg_ag"])
                    A.reset(m4)

            if stage >= 7:
                S.barrier()
                A.off = m2
                A.off = m4
                gate_bc = A.take(D * 4, F32)
                fng_bc = A.take(D * 4, F32)
                xt_ = [A.take(D * 4, F32) for _ in range(2)]
                res = A.take(D * 4, F32)
                o1 = [A.take(512 * 4, F32) for _ in range(2)]
                sq = [A.take(128 * 2, BF16) for _ in range(2)]
                st2 = [A.take(64, F32) for _ in range(2)]
                ytile = [A.take(8 * 128 * 2, BF16) for _ in range(2)]
                dma("sp", gate_bc, gate_scr.to_broadcast([128, D]), ["gate_scr"], ["gate_bc"])
                dma("sp", fng_bc, fng.to_broadcast([128, D]), writes=["fng_bc"])
                ones2 = TLEb[:, 126:128]
                for t in range(16):
                    b_ = t % 2
                    tsl = slice(t * 128, (t + 1) * 128)
                    xk_ = "xt%d" % b_
                    sk_ = "st2%d" % b_
                    dma("sp", xt_[b_], xloc[tsl, :], writes=[xk_])
                    dma("sp", ytile[b_], h_scr[t].rearrange("p c n -> p (c n)"), ["h_scr"], ["yt%d" % b_])
                    ygT = ytile[b_].rearrange("p (c n) -> p c n", c=8)
                    for br, (src, srck, sl_) in enumerate([(agT, "agT", tsl), (ygT, "yt%d" % b_, slice(0, 128))]):
                        for ch in range(8):
                            act(sq[ch % 2], src[:, ch, sl_], AF.Square, [srck], ["sq%d" % (ch % 2)])
                            mm(pbank[3][:, br * 2:br * 2 + 2], sq[ch % 2], ones2, ch == 0, ch == 7, ["sq%d" % (ch % 2), "cb16"], ["ps3"])
                    s_ = st2[b_]
                    ts(s_[:, 0:4], pbank[3][:, 0:4], 1.0 / 1024.0, EPS, ALU.mult, ALU.add, ["ps3"], [sk_])
                    act(s_[:, 0:4], s_[:, 0:4], AF.Sqrt, [sk_], [sk_])
                    S.dve(lambda e, o=s_[:, 4:8], i=s_[:, 0:4]: e.reciprocal(out=o, in_=i), [sk_], [sk_])
                    for cb in range(4):
                        cs_ = slice(cb * 512, (cb + 1) * 512)
                        pa, pka = (pbank[0], "ps0") if cb % 2 == 0 else (pbank[6], "ps6")
                        pss, pks = (pbank[1], "ps1") if cb % 2 == 0 else (pbank[7], "ps7")
                        for ch in range(8):
                            mm(pa[:, :], agT[:, ch, tsl], wo[:, ch, cs_], ch == 0, ch == 7, ["agT", "wo%d" % cb], [pka])
                        for ch in range(8):
                            mm(pss[:, :], ygT[:, ch, :], wo[:, 8 + ch, cs_], ch == 0, ch == 7, ["yt%d" % b_, "wo%d" % cb], [pks])
                        ok_ = "o1%d" % (cb % 2)
                        oo = o1[cb % 2]
                        act(oo, pa[:, :], AF.Identity, [pka, sk_], [ok_], scale=s_[:, 5:6])
                        stt(oo, pss[:, :], s_[:, 7:8], oo, ALU.mult, ALU.add, [pks, sk_, ok_], [ok_])
                        tt(oo, oo, gate_bc[:, cs_], ALU.mult, [ok_, "gate_bc"], [ok_])
                        tt(res[:, cs_], oo, xt_[b_][:, cs_], ALU.add, [ok_, xk_], ["res"])
                    act(xt_[b_], res, AF.Square, ["res"], [xk_, sk_], accum=s_[:, 8:9])
                    ts(s_[:, 9:10], s_[:, 8:9], 1.0 / D, EPS, ALU.mult, ALU.add, [sk_], [sk_])
                    act(s_[:, 10:11], s_[:, 9:10], AF.Sqrt, [sk_], [sk_])
                    S.dve(lambda e, o=s_[:, 11:12], i=s_[:, 10:11]: e.reciprocal(out=o, in_=i), [sk_], [sk_])
                    stt(xt_[b_], res, s_[:, 11:12], fng_bc, ALU.mult, ALU.mult, ["res", sk_, "fng_bc"], [xk_])
                    dma("sp", out[tsl, :], xt_[b_], [xk_], ["out"])

        fw = [(q, i) for q in ("sp", "pool") for i, op in enumerate(S.ops[q]) if op["dma"]]
        S.emit(final_waits=fw)
    return nc


def _fm(v, nchunk):
    return np.ascontiguousarray(np.asarray(v, np.float32).reshape(nchunk, 128).T)


def _consts():
    k = np.arange(128)[:, None]
    l = np.arange(128)[None, :]
    ident = (k == l).astype(np.float32)
    tle = (k <= l).astype(np.float32)
    tge = (k >= l).astype(np.float32)
    su = (k > l).astype(np.float32)
    sl = (k < l).astype(np.float32)
    d = np.arange(128)
    sw = np.where((d % 32) < 16, d + 16, d - 16)
    perm = np.zeros((128, 128), np.float32)
    perm[sw, d] = 1.0
    return np.ascontiguousarray(np.concatenate([ident, tle, tge, su, sl, perm], axis=1))


def _rope_tables(pos):
    pos = np.asarray(pos)
    row = (pos // 64).astype(np.float32)
    col = (pos % 64).astype(np.float32)
    quarter = 16
    freq = (1.0 / (np.float32(10000.0) ** (np.arange(quarter, dtype=np.float32) / np.float32(quarter)))).astype(np.float32)
    cosT = np.zeros((128, len(pos)), np.float32)
    ssinT = np.zeros((128, len(pos)), np.float32)
    for p in range(128):
        dd = p % 64
        i = dd % 16
        pp = row if dd < 32 else col
        ang = (pp * freq[i]).astype(np.float32)
        cosT[p] = np.cos(ang)
        s = np.sin(ang)
        ssinT[p] = -s if (dd % 32) < 16 else s
    return cosT, ssinT


def prep_inputs(inp):
    g = lambda k: np.asarray(inp[k], np.float32)
    x, c, ctx, c_ctx = g("x"), g("c"), g("ctx"), g("c_ctx")
    w_mod, b_mod, norm_g, w_in = g("w_mod")[0], g("b_mod")[0], g("norm_g")[0], g("w_in")[0]
    conv_w, conv_b = g("conv_w")[0], g("conv_b")[0]
    a_f, a_b, db_f, db_b = g("a_log_f")[0], g("a_log_b")[0], g("dt_bias_f")[0], g("dt_bias_b")[0]
    d_skip, attg, ssdg, sink = g("d_skip")[0], g("att_norm_g")[0], g("ssd_norm_g")[0], g("sink")[0]
    w_out, fng = g("w_out")[0], g("final_norm_g")
    consts = _consts()
    shared = dict(w_in=np.ascontiguousarray(w_in), w_out=np.ascontiguousarray(w_out),
                  normgT=_fm(norm_g, 16), dskipT=_fm(np.repeat(d_skip, 64), 8), attgT=_fm(attg, 8), ssdgT=_fm(ssdg, 8),
                  sinkb=np.ascontiguousarray(np.broadcast_to(sink[None, :], (128, 16))),
                  dskb=np.ascontiguousarray(np.broadcast_to(d_skip[None, :], (128, 16))),
                  fng=np.ascontiguousarray(fng[None, :]), consts=consts)
    maps = []
    for core in range(8):
        b, h = core // 2, core % 2
        xb = x[b]
        xl = xb if h == 0 else xb[::-1]
        pos = np.arange(SEQ) if h == 0 else np.arange(SEQ)[::-1]
        cl = ctx[b] if h == 0 else ctx[b][::-1]
        cw = conv_w if h == 0 else conv_w[::-1]
        convT = np.stack([_fm(cw[0], 16), _fm(cw[1], 16), _fm(cw[2], 16), _fm(conv_b, 16)], axis=2).reshape(128, 64)
        if h == 0:
            wdt = w_in[:, OFF_DTF:OFF_DTF + 32]
            dtbv = np.concatenate([db_f, db_b])
            alv = np.concatenate([a_f, a_b])
        else:
            wdt = np.concatenate([w_in[:, OFF_DTB:OFF_DTB + 16], w_in[:, OFF_DTF:OFF_DTF + 16]], axis=1)
            dtbv = np.concatenate([db_b, db_f])
            alv = np.concatenate([a_b, a_f])
        cosT, ssinT = _rope_tables(pos[0:NTH])
        cvec = np.stack([_fm(c[b], 16), _fm(c_ctx, 16)], axis=2).reshape(128, 32)
        chs = [2 * i + h for i in range(16)]
        gbs = [2 * gb + h for gb in range(2)]
        wsh = np.concatenate([w_mod[:, ch * 128:(ch + 1) * 128] for ch in chs] +
                             [w_mod[:, 2 * D + k * 512:2 * D + (k + 1) * 512] for k in gbs], axis=1)
        bml = np.stack([b_mod[ch * 128:(ch + 1) * 128] for ch in chs], axis=1)
        bgl = np.concatenate([b_mod[2 * D + k * 512:2 * D + (k + 1) * 512] for k in gbs])[None, :]
        rm = np.zeros((128, 2), np.float32)
        rm[:, h] = 1.0
        m = dict(shared)
        m.update(xloc=np.ascontiguousarray(xl[0:NTH]), ctxf=np.ascontiguousarray(cl),
                 w_mod_sh=np.ascontiguousarray(wsh), bmod_l=np.ascontiguousarray(bml), bgate_l=np.ascontiguousarray(bgl), rmask=rm,
                 cvec=np.ascontiguousarray(cvec), w_dt=np.ascontiguousarray(wdt), convT=np.ascontiguousarray(convT),
                 dtb=np.ascontiguousarray(np.broadcast_to(dtbv[None, :], (128, 32))),
                 alog=np.ascontiguousarray(np.broadcast_to(alv[None, :], (128, 32))),
                 cosT=cosT, ssinT=ssinT)
        maps.append(m)
    return maps


def kernel(**inputs):
    maps = prep_inputs(inputs)
    nc = build()
    res = run_bass_kernel_spmd(nc, maps, core_ids=list(range(8)))
    outp = np.zeros((4, SEQ, D), np.float32)
    for core in range(8):
        b, h = core // 2, core % 2
        o = res.results[core]["out"]
        if h == 0:
            outp[b, 0:NT] = o
        else:
            outp[b, NT:SEQ] = o[::-1]
    return outp
```

```python
import contextlib
import os
import numpy as np
CUT = int(os.environ.get('KCUT', '99'))
CUT2 = int(os.environ.get('KCUT2', '99'))
import concourse.bass as bass
import concourse.mybir as mybir
from concourse.bass_utils import run_bass_kernel_spmd

F32 = mybir.dt.float32
BF16 = mybir.dt.bfloat16
AF = mybir.ActivationFunctionType
ALU = mybir.AluOpType

D = 2048
NT = 2048
NTH = 2176
SEQ = 4096
NCTX = 256
NIN = 5664
OFF_Q, OFF_K, OFF_V, OFF_G, OFF_Z, OFF_XS, OFF_B, OFF_C = 0, 1024, 1280, 1536, 2560, 3584, 4608, 5120
OFF_DTF, OFF_DTB = 5632, 5648
EPS = 1e-6


class Sched:
    ENGS = ["pe", "act", "dve", "pool", "sp"]
    NDMASEM = 4
    WINDOW = 128

    def __init__(self, nc):
        self.nc = nc
        self.ops = {e: [] for e in self.ENGS}
        self.lastw = {}
        self.readers = {}
        self.ndma = {e: 0 for e in self.ENGS}
        self.epoch = 0

    def barrier(self):
        self.epoch += 1
        self.lastw = {}
        self.readers = {}

    def add(self, eng, fn, reads=(), writes=(), dma=False, cost=0.3):
        idx = len(self.ops[eng])
        psr = [k for k in reads if k.startswith("ps")]
        if psr:
            reads = [k for k in reads if not k.startswith("ps")]
            writes = list(writes) + [k for k in psr if k not in writes]
        deps = set()
        for k in reads:
            if k in self.lastw:
                deps.add(self.lastw[k])
        for k in writes:
            if k in self.lastw:
                deps.add(self.lastw[k])
            for r in self.readers.get(k, ()):
                deps.add(r)
        deps.discard((eng, idx))
        odeps = set()
        if eng == "pe":
            odeps = set(d for d in deps if d[0] == "pe")
            deps = deps - odeps
        op = dict(fn=fn, deps=deps, odeps=odeps, dma=dma, marked=False, cost=cost, epoch=self.epoch)
        self.ops[eng].append(op)
        for k in reads:
            self.readers.setdefault(k, []).append((eng, idx))
        for k in writes:
            self.lastw[k] = (eng, idx)
            self.readers[k] = []
        return (eng, idx)

    def pe(self, fn, reads=(), writes=(), cost=0.15):
        return self.add("pe", fn, reads, writes, cost=cost)

    def act(self, fn, reads=(), writes=(), cost=0.4):
        return self.add("act", fn, reads, writes, cost=cost)

    def dve(self, fn, reads=(), writes=(), cost=0.4):
        return self.add("dve", fn, reads, writes, cost=cost)

    def dma(self, q, fn, reads=(), writes=(), cost=4.0):
        return self.add(q, fn, reads, writes, dma=True, cost=cost)

    def schedule(self):
        ops = self.ops
        pending = {e: list(range(len(ops[e]))) for e in self.ENGS}
        done = {e: [None] * len(ops[e]) for e in self.ENGS}
        free = {e: 0.0 for e in self.ENGS}
        order = {e: [] for e in self.ENGS}
        dma_free = [0.0]
        self.fence_pending = {}
        ntot = sum(len(v) for v in pending.values())
        SYNC = 0.4
        epoch = 0
        last_exec = {}
        for _ in range(ntot):
            while not any(pending[e] and ops[e][pending[e][0]]["epoch"] == epoch for e in self.ENGS):
                epoch += 1
                tmax = max(free.values())
                for e in self.ENGS:
                    free[e] = tmax
                    if pending[e]:
                        nxt = [i for i in pending[e] if ops[e][i]["epoch"] == epoch]
                        if nxt:
                            self.fence_pending.setdefault(e, set()).update(last_exec.values())
                for k in [k for k in last_exec if isinstance(k, tuple)]:
                    del last_exec[k]
            best = None
            for e in self.ENGS:
                pl = pending[e]
                fe = free[e]
                for wi in range(min(self.WINDOW, len(pl))):
                    i = pl[wi]
                    op = ops[e][i]
                    if op["epoch"] != epoch:
                        break
                    st = fe
                    ok = True
                    for (de, di) in op["deps"]:
                        t = done[de][di]
                        if t is None:
                            ok = False
                            break
                        if t + SYNC > st:
                            st = t + SYNC
                    if ok:
                        for (de, di) in op["odeps"]:
                            if done[de][di] is None:
                                ok = False
                                break
                    if not ok:
                        continue
                    key = (st, wi)
                    if best is None or key < best[0]:
                        best = (key, e, wi, i, st)
                    if st <= fe:
                        break
            assert best is not None, "scheduler deadlock"
            _, e, wi, i, st = best
            op = ops[e][i]
            if op["dma"]:
                free[e] = st + 0.15
                t0 = max(st + 2.0, dma_free[0])
                dma_free[0] = t0 + op["cost"]
                done[e][i] = dma_free[0]
            else:
                free[e] = st + op["cost"]
                done[e][i] = free[e]
            fp = self.fence_pending.pop(e, None)
            if fp:
                op["deps"] = set(op["deps"]) | set(d for d in fp if d != (e, i))
            if op["dma"]:
                last_exec[(e, i)] = (e, i)
            else:
                last_exec[e] = (e, i)
            order[e].append(i)
            del pending[e][wi]
        self.est_time = max(max([0.0] + [t for t in done[e] if t is not None]) for e in self.ENGS)
        return order

    def emit(self, final_waits=()):
        nc = self.nc
        ops = self.ops
        fin = dict(fn=None, deps=set(final_waits), odeps=set(), dma=False, marked=False, cost=0.0, epoch=self.epoch)
        ops["sp"].append(fin)
        order = self.schedule()
        for e in self.ENGS:
            for op in ops[e]:
                for (de, di) in op["deps"]:
                    ops[de][di]["marked"] = True
        for e in self.ENGS:
            c = 0
            nd = 0
            for i in order[e]:
                op = ops[e][i]
                if op["dma"]:
                    op["dmaslot"] = nd
                    nd += 1
                elif op["marked"]:
                    c += 1
                    op["cnt"] = c
        with contextlib.ExitStack() as st:
            esem = {e: st.enter_context(nc.semaphore("s_" + e)) for e in self.ENGS}
            dsem = {}
            for e in self.ENGS:
                if self.ndma[e] or any(op["dma"] for op in ops[e]):
                    dsem[e] = [st.enter_context(nc.semaphore("d_%s%d" % (e, i)))
                               for i in range(self.NDMASEM)]
            block = st.enter_context(nc.Block())
            engobj = {"pe": "tensor", "act": "scalar", "dve": "vector", "pool": "gpsimd", "sp": "sync"}

            def run(e, eng):
                seen = {}
                seen_dma = set()
                for i in order[e]:
                    op = ops[e][i]
                    waits = []
                    if op["dma"]:
                        slot = op["dmaslot"]
                        if slot >= self.NDMASEM:
                            waits.append((dsem[e][slot % self.NDMASEM], 16 * (slot // self.NDMASEM)))
                    cneed = {}
                    for (de, di) in op["deps"]:
                        dop = ops[de][di]
                        if dop["dma"]:
                            if (de, di) in seen_dma:
                                continue
                            seen_dma.add((de, di))
                            s = dop["dmaslot"]
                            waits.append((dsem[de][s % self.NDMASEM], 16 * (s // self.NDMASEM + 1)))
                        else:
                            c = dop["cnt"]
                            if seen.get(de, 0) >= c:
                                continue
                            cneed[de] = max(cneed.get(de, 0), c)
                    for de, c in cneed.items():
                        seen[de] = c
                        waits.append((esem[de], c))
                    for (s, v) in waits:
                        eng.wait_ge(s, v)
                    if op["fn"] is None:
                        continue
                    ins = op["fn"](eng)
                    if op["dma"]:
                        ins.then_inc(dsem[e][op["dmaslot"] % self.NDMASEM], 16)
                    elif op["marked"]:
                        ins.then_inc(esem[e], 1)

            for e in self.ENGS:
                if not ops[e]:
                    continue

                def mk(e):
                    def _f(eng):
                        run(e, eng)
                    return _f
                getattr(block, engobj[e])(mk(e))


def build(stage=99, dbg=(), ncores=8):
    nc = bass.Bass("TRN2", target_bir_lowering=False)

    def din(name, shape):
        return nc.dram_tensor(name, list(shape), F32, kind="ExternalInput").ap()

    xloc = din("xloc", [NTH, D])
    ctxf = din("ctxf", [NCTX, D])
    cvec = din("cvec", [128, 32])
    w_mod_sh = din("w_mod_sh", [D, 3 * D // 2])
    bmod_l = din("bmod_l", [128, 16])
    bgate_l = din("bgate_l", [1, D // 2])
    rmask = din("rmask", [128, 2])
    normgT = din("normgT", [128, 16])
    w_in = din("w_in", [D, NIN])
    w_dt = din("w_dt", [D, 32])
    convT = din("convT", [128, 16 * 4])
    dtb = din("dtb", [128, 32])
    alog = din("alog", [128, 32])
    dskipT = din("dskipT", [128, 8])
    attgT = din("attgT", [128, 8])
    ssdgT = din("ssdgT", [128, 8])
    sinkb = din("sinkb", [128, 16])
    dskb = din("dskb", [128, 16])
    w_out = din("w_out", [D, D])
    fng = din("fng", [1, D])
    cosT = din("cosT", [128, NTH])
    ssinT = din("ssinT", [128, NTH])
    consts = din("consts", [128, 6 * 128])
    out = nc.dram_tensor("out", [NT, D], F32, kind="ExternalOutput").ap()
    gate_scr = nc.dram_tensor("gate_scr", [1, D], F32).ap()
    ccg_in = nc.dram_tensor("ccg_in", [1, D], F32)
    ccm_in = nc.dram_tensor("ccm_in", [128, 64], F32)
    ccm_out = nc.dram_tensor("ccm_out", [128, 64], F32)
    h_scr = nc.dram_tensor("h_scr", [16, 128, 8, 128], BF16).ap()
    cc_in = [nc.dram_tensor("cc_in%d" % g, [128, 256], F32) for g in range(4)]
    cc_out = [nc.dram_tensor("cc_out%d" % g, [128, 256], F32) for g in range(4)]
    dbg_out = {}
    for (nm, shp) in dbg:
        dbg_out[nm] = nc.dram_tensor("dbg_" + nm, list(shp), F32, kind="ExternalOutput").ap()

    st = contextlib.ExitStack()
    with st:
        S = Sched(nc)
        ARENA = 189 * 1024
        arena = st.enter_context(nc.sbuf_tensor("arena", [128, ARENA // 2], BF16))
        cst = st.enter_context(nc.sbuf_tensor("cst", [128, 1472], F32))
        pbank = [st.enter_context(nc.psum_tensor("pb%d" % i, [128, 512], F32)) for i in range(8)]

        class Ar:
            def __init__(self):
                self.off = 0

            def take(self, nbytes, dt, parts=128):
                nbytes = (nbytes + 63) // 64 * 64
                o = self.off
                self.off += nbytes
                assert self.off <= ARENA, ("arena overflow", self.off)
                v = arena[0:parts, o // 2:(o + nbytes) // 2]
                if dt == F32:
                    v = v.bitcast(F32)
                return v

            def mark(self):
                return self.off

            def reset(self, m):
                self.off = m
                S.barrier()

        A = Ar()
        coff = [0]

        def ctake(n):
            o = coff[0]
            coff[0] += n
            assert coff[0] <= 1472
            return cst[:, o:o + n]

        def fsz(ap):
            n = 1
            for d in ap.shape[1:]:
                n *= d
            return n

        def dma(q, o, i, reads=(), writes=()):
            esz = 4 if i.dtype == F32 else 2
            nbytes = fsz(i) * i.shape[0] * esz
            return S.dma(q, lambda e: e.dma_start(out=o, in_=i), reads, writes, cost=nbytes / 250e3)

        def mm(o, lhsT, rhs, start, stop, reads, writes):
            c = max(0.11, 0.006 + fsz(rhs) / 2400.0)
            if rhs.dtype == F32:
                c = 0.05 + fsz(rhs) / 800.0
            return S.pe(lambda e: e.matmul(o, lhsT=lhsT, rhs=rhs, start=start, stop=stop), reads, writes, cost=c)

        def tr(o, i, reads, writes):
            return S.pe(lambda e: e.transpose(o, i, ident_b), list(reads) + ["cb16"], writes, cost=0.12)

        def act(o, i, func, reads, writes, bias=None, scale=None, accum=None):
            kw = {}
            c = 0.22 + fsz(o) / 1400.0
            if bias is not None:
                kw["bias"] = bias
            if scale is not None:
                kw["scale"] = scale
            if accum is not None:
                kw["accum_out"] = accum
                c += 0.1
            return S.act(lambda e: e.activation(out=o, in_=i, func=func, **kw), reads, writes, cost=c)

        def dcost(o, two=False):
            n = fsz(o)
            if o.dtype == F32 or two:
                return 0.08 + n / 960.0
            return 0.08 + n / 1200.0

        def tt(o, a, b, op, reads, writes, eng="dve"):
            return S.add(eng, lambda e: e.tensor_tensor(out=o, in0=a, in1=b, op=op), reads, writes, cost=dcost(o, True))

        def ts(o, a, s1, s2, op0, op1, reads, writes, eng="dve"):
            if op1 is None:
                return S.add(eng, lambda e: e.tensor_scalar(out=o, in0=a, scalar1=s1, scalar2=None, op0=op0), reads, writes,
                             cost=dcost(o))
            return S.add(eng, lambda e: e.tensor_scalar(out=o, in0=a, scalar1=s1, scalar2=s2, op0=op0, op1=op1), reads, writes,
                         cost=dcost(o))

        def stt(o, a, s, b, op0, op1, reads, writes):
            return S.dve(lambda e: e.scalar_tensor_tensor(out=o, in0=a, scalar=s, in1=b, op0=op0, op1=op1), reads, writes,
                         cost=dcost(o, True))

        def cp(o, i, reads, writes, eng="dve"):
            return S.add(eng, lambda e: e.tensor_copy(out=o, in_=i), reads, writes, cost=dcost(o))

        def memset(o, v, writes, eng="dve"):
            return S.add(eng, lambda e: e.memset(o, v), (), writes, cost=dcost(o))

        def pbf(i):
            return pbank[i][:, :].bitcast(BF16)

        Hst = st.enter_context(nc.sbuf_tensor("Hst", [128, 2048], F32))
        cf = ctake(768)
        dma("sp", cf, consts, writes=["cf"])
        ident_f = cf[:, 0:128]
        TLEf = cf[:, 128:256]
        TGEf = cf[:, 256:384]
        SUf = cf[:, 384:512]
        SLf = cf[:, 512:640]
        cb16 = A.take(6 * 128 * 2, BF16)
        cp(cb16, cf, ["cf"], ["cb16"])
        ident_b = cb16[:, 0:128]
        TLEb = cb16[:, 128:256]
        TGEb = cb16[:, 256:384]
        perm_b = cb16[:, 640:768]
        onesf = ctake(128)
        memset(onesf, 1.0, ["onesf"])
        small = ctake(32 + 32 + 16 + 64 + 32 + 32 + 8 + 8 + 8 + 16 + 16)
        rmask_s = ctake(2)
        cv = small[:, 0:32]
        bmod_s = small[:, 32:48]
        normg_s = small[:, 64:80]
        conv_s = small[:, 80:144]
        dtb_s = small[:, 144:176]
        alog_s = small[:, 176:208]
        dskip_s = small[:, 208:216]
        attg_s = small[:, 216:224]
        ssdg_s = small[:, 224:232]
        sink_s = small[:, 232:248]
        dskb_s = small[:, 248:264]
        for (dst, src, k) in [(cv, cvec, "cv"), (bmod_s, bmod_l, "bmod"), (rmask_s, rmask, "rmask"), (normg_s, normgT, "normg"), (conv_s, convT, "conv"),
                              (dtb_s, dtb, "dtb"), (alog_s, alog, "alog"), (dskip_s, dskipT, "dskip"), (attg_s, attgT, "attg"),
                              (ssdg_s, ssdgT, "ssdg"), (sink_s, sinkb, "sink"), (dskb_s, dskb, "dskb")]:
            dma("sp", dst, src, writes=[k])
        a_s = ctake(32)
        act(a_s, alog_s, AF.Exp, ["alog"], ["a_s"])
        ts(a_s, a_s, -1.0, None, ALU.mult, None, ["a_s"], ["a_s"])
        esink = ctake(16)
        act(esink, sink_s, AF.Exp, ["sink"], ["esink"])
        scb = A.take(32 * 2, BF16)
        act(scb, cv, AF.Silu, ["cv"], ["scb"])

        modT = ctake(64)
        PAIRS = [[2 * i, 2 * i + 1] for i in range(ncores // 2)]
        top = (ARENA - 41984) // 2
        wmb = [arena[:, top + i * 8192:top + (i + 1) * 8192].rearrange("p (k n) -> p k n", k=16) for i in range(2)]
        G2 = arena[:, top + 16384:top + 16384 + 2048].bitcast(F32)
        bgl = arena[:, top + 18432:top + 18432 + 2048].bitcast(F32)
        ml = arena[:, top + 20480:top + 20480 + 64].bitcast(F32)
        modl = arena[:, top + 20544:top + 20544 + 128].bitcast(F32)
        dma("sp", bgl[0:1, :], bgate_l, writes=["bgl"])
        psmod = pbank[3][:, 0:32]
        for lb in range(4):
            wb_ = wmb[lb % 2]
            dma("pool", wb_, w_mod_sh[:, lb * 512:(lb + 1) * 512].rearrange("(k p) n -> p k n", p=128), writes=["wmb%d" % (lb % 2)])
            for c4 in range(4):
                i = lb * 4 + c4
                for kc in range(16):
                    mm(psmod[:, i * 2:i * 2 + 2], wb_[:, kc, c4 * 128:(c4 + 1) * 128], scb[:, kc * 2:kc * 2 + 2],
                       kc == 0, kc == 15, ["wmb%d" % (lb % 2), "scb"], ["ps3"])
        tt(ml.rearrange("p (c j) -> p c j", j=2), psmod.rearrange("p (c j) -> p c j", j=2),
           bmod_s.rearrange("p (c o) -> p c o", o=1).to_broadcast([128, 16, 2]), ALU.add, ["ps3", "bmod"], ["ml"])
        tt(modl.rearrange("p (c r j) -> p c r j", r=2, j=2),
           ml.rearrange("p (c o j) -> p c o j", o=1, j=2).to_broadcast([128, 16, 2, 2]),
           rmask_s.rearrange("p (o r q) -> p o r q", o=1, q=1).to_broadcast([128, 16, 2, 2]), ALU.mult, ["ml", "rmask"], ["modl"])
        dma("sp", ccm_in.ap(), modl, ["modl"], ["ccm_in"])
        S.add("pool", lambda e: e.collective_compute("AllReduce", ALU.add, replica_groups=PAIRS,
                                                     ins=[ccm_in.ap().opt()], outs=[ccm_out.ap().opt()]),
              ["ccm_in"], ["ccm_out"], cost=12.0)
        dma("sp", modT, ccm_out.ap(), ["ccm_out"], ["modT"])
        for gb in range(2):
            wb_ = wmb[gb % 2]
            dma("pool", wb_, w_mod_sh[:, 2048 + gb * 512:2048 + (gb + 1) * 512].rearrange("(k p) n -> p k n", p=128),
                writes=["wmb%d" % (gb % 2)])
            for kc in range(16):
                mm(pbank[5][:, :], scb[:, kc * 2:kc * 2 + 1].to_broadcast([128, 128]), wb_[:, kc, :], kc == 0, kc == 15,
                   ["scb", "wmb%d" % (gb % 2)], ["ps5"])
            tt(G2[0:1, gb * 512:(gb + 1) * 512], pbank[5][0:1, :], bgl[0:1, gb * 512:(gb + 1) * 512], ALU.add, ["ps5", "bgl"], ["G2"])
        tt(Hst[0:1, :].rearrange("p (g r n) -> p g r n", g=2, r=2),
           G2[0:1, :].rearrange("p (g o n) -> p g o n", g=2, o=1).to_broadcast([1, 2, 2, 512]),
           rmask_s[0:1, :].rearrange("p (o r q) -> p o r q", o=1, q=1).to_broadcast([1, 2, 2, 512]), ALU.mult,
           ["G2", "rmask"], ["HF", "HB"])
        dma("sp", ccg_in.ap(), Hst[0:1, :], ["HF", "HB"], ["ccg_in"])
        S.add("pool", lambda e: e.collective_compute("AllReduce", ALU.add, replica_groups=PAIRS,
                                                     ins=[ccg_in.ap().opt()], outs=[gate_scr.tensor.ap().opt() if hasattr(gate_scr, "tensor") else gate_scr.opt()]),
              ["ccg_in"], ["gate_scr"], cost=12.0)
        modv = modT.rearrange("p (c j) -> p c j", j=2)
        gs = [ctake(16), ctake(16)]
        sh = [ctake(16), ctake(16)]
        for j in range(2):
            stt(gs[j], modv[:, 16:32, j], 1.0, normg_s, ALU.add, ALU.mult, ["modT", "normg"], ["gs%d" % j])
            cp(sh[j], modv[:, 0:16, j], ["modT"], ["sh%d" % j])

        if "mod" in dbg_out:
            dma("sp", dbg_out["mod"], modT, ["modT"], ["dbg_mod"])

        def alloc_build_set():
            return dict(xb=[A.take(D * 4, F32) for _ in range(2)], xn=[A.take(D * 2, BF16) for _ in range(2)],
                        junk=A.take(D * 2, BF16), stat=[A.take(64, F32) for _ in range(2)])

        def xkeys(pfx):
            def f(t0, n):
                return [pfx + "%d" % t for t in range(t0 // 128, (t0 + n - 1) // 128 + 1)]
            return f

        def build_xmT(XT, xkp, src, ntiles, j, bs):
            xb, xn, junk, stat = bs["xb"], bs["xn"], bs["junk"], bs["stat"]
            for t in range(ntiles):
                b_ = t % 2
                kx, kn, ks = "b_xb%d" % b_, "b_xn%d" % b_, "b_st%d" % b_
                dma("sp", xb[b_], src[t * 128:(t + 1) * 128, :], writes=[kx])
                act(junk, xb[b_], AF.Square, [kx], ["b_junk", ks], accum=stat[b_][:, 0:1])
                ts(stat[b_][:, 1:2], stat[b_][:, 0:1], 1.0 / D, EPS, ALU.mult, ALU.add, [ks], [ks])
                act(stat[b_][:, 2:3], stat[b_][:, 1:2], AF.Sqrt, [ks], [ks])
                S.dve(lambda e, o=stat[b_][:, 3:4], i=stat[b_][:, 2:3]: e.reciprocal(out=o, in_=i), [ks], [ks], cost=0.1)
                ts(xn[b_], xb[b_], stat[b_][:, 3:4], None, ALU.mult, None, [kx, ks], [kn])
                for q4 in range(4):
                    pt = pbf([2, 4][q4 % 2])[:, 0:512]
                    pk = "ps%d" % [2, 4][q4 % 2]
                    for i4 in range(4):
                        kc = q4 * 4 + i4
                        tr(pt[:, i4 * 128:(i4 + 1) * 128], xn[b_][:, kc * 128:(kc + 1) * 128], [kn], [pk])
                    o = XT[:, q4 * 4:q4 * 4 + 4, t * 128:(t + 1) * 128]
                    pv = pt.rearrange("p (c n) -> p c n", c=4)
                    if q4 % 2 == 0:
                        cp(o, pv, [pk], [xkp + "%d" % t])
                    else:
                        act(o, pv, AF.Copy, [pk], [xkp + "%d" % t])

        def modulate(XT, xkp, ntiles, j):
            keys = [xkp + "%d" % t for t in range(ntiles)]
            for kc in range(16):
                o = XT[:, kc, 0:ntiles * 128]
                if kc % 2 == 0:
                    ts(o, o, gs[j][:, kc:kc + 1], sh[j][:, kc:kc + 1], ALU.mult, ALU.add, keys + ["gs%d" % j, "sh%d" % j], keys)
                else:
                    act(o, o, AF.Identity, keys + ["gs%d" % j, "sh%d" % j], keys, bias=sh[j][:, kc:kc + 1], scale=gs[j][:, kc:kc + 1])

        wslot = [0]

        def load_w(wbufs, src_cols, ncols):
            i = wslot[0] % len(wbufs)
            wslot[0] += 1
            wv = wbufs[i][:, :, 0:ncols]
            dma("pool", wv, src_cols.rearrange("(k p) n -> p k n", p=128), writes=["wbuf%d" % i])
            return wv, "wbuf%d" % i

        pacc = [0]

        def inproj_T(wv, wk, c0, nco, XT, xk, t0, nt):
            b_ = pacc[0] % 2
            pacc[0] += 1
            ps = pbank[b_][0:nco, 0:nt]
            for kc in range(16):
                mm(ps, wv[:, kc, c0:c0 + nco], XT[:, kc, t0:t0 + nt], kc == 0, kc == 15, [wk] + xk(t0, nt), ["ps%d" % b_])
            return ps, "ps%d" % b_

        def inproj_tok(wv, wk, c0, nco, XT, xk, t0, ps, pk):
            for kc in range(16):
                mm(ps, XT[:, kc, t0:t0 + 128], wv[:, kc, c0:c0 + nco], kc == 0, kc == 15, [wk] + xk(t0, 128), [pk])

        def pieces(ntok):
            npc = (ntok + 511) // 512
            sz = ((ntok + npc - 1) // npc + 1) // 2 * 2
            r = []
            t = 0
            while t < ntok:
                n = min(sz, ntok - t)
                r.append((t, n))
                t += n
            return r

        def conv_chunk(wv, wk, c0, XT, xk, ntok, has_halo, cc, taps, outT, outk, stage_f, tmp_f, tb=0, left=False):
            if not left:
                memset(stage_f[:, 0:1], 0.0, ["stage"])
            if not has_halo:
                memset(stage_f[:, ntok + 1:ntok + 2], 0.0, ["stage"])
            lo = tb - (1 if left else 0)
            ntot = ntok + (1 if has_halo else 0) + (1 if left else 0)
            so = 0 if left else 1
            for (t0, n) in pieces(ntot):
                ps, pk = inproj_T(wv, wk, c0, 128, XT, xk, lo + t0, n)
                act(stage_f[:, so + t0:so + t0 + n], ps, AF.Copy, [pk], ["stage"])
            cw = conv_s.rearrange("p (c f) -> p c f", f=4)
            wA = cw[:, cc, taps[0]:taps[0] + 1]
            wB = cw[:, cc, taps[1]:taps[1] + 1]
            wC = cw[:, cc, taps[2]:taps[2] + 1]
            bb = cw[:, cc, 3:4]
            ts(tmp_f[:, 0:ntok], stage_f[:, 0:ntok], wA, None, ALU.mult, None, ["stage", "conv"], ["ctmp"])
            stt(tmp_f[:, 0:ntok], stage_f[:, 1:ntok + 1], wB, tmp_f[:, 0:ntok], ALU.mult, ALU.add, ["stage", "conv", "ctmp"], ["ctmp"])
            stt(tmp_f[:, 0:ntok], stage_f[:, 2:ntok + 2], wC, tmp_f[:, 0:ntok], ALU.mult, ALU.add, ["stage", "conv", "ctmp"], ["ctmp"])
            act(outT[:, tb:tb + ntok], tmp_f[:, 0:ntok], AF.Silu, ["ctmp", "conv"], [outk], bias=bb)

        def to_tok(srcT, srck, nchunks, dst, dstk, c_off):
            for c0 in range(0, nchunks, 4):
                n = min(4, nchunks - c0)
                hb = (c0 // 4) % 2
                pt = pbf([2, 4][hb])[:, 0:n * 128]
                pk = "ps%d" % [2, 4][hb]
                for i in range(n):
                    tr(pt[:, i * 128:(i + 1) * 128], srcT[:, (c0 + i) * 128:(c0 + i + 1) * 128], [srck], [pk])
                cp(dst[:, c0:c0 + n, c_off:c_off + 128], pt.rearrange("p (c n) -> p c n", n=128), [pk], [dstk], eng="dve")

        def alloc_dt_set(nch):
            n = nch * 16
            return [A.take(n * 4, F32) for _ in range(5)]

        def dt_prep(wdt_v, wdtk, dcol, XT, xk, nch, dirn, tg, bufs):
            n = nch * 16
            dtv, adt, wgt, dec, tmp = [b[:, 0:n] for b in bufs]
            ps = pbank[3][:, 0:n]
            for c in range(nch):
                inproj_tok(wdt_v, wdtk, dcol, 16, XT, xk, c * 128, ps[:, c * 16:(c + 1) * 16], "ps3")
            v3 = lambda a: a.rearrange("p (c h) -> p c h", h=16)
            if CUT2 < 1:
                cp(tmp, ps, ["ps3"], [tg + "tmp"])
                return None
            tt(v3(tmp), v3(ps), dtb_s[:, dcol:dcol + 16].rearrange("p (o h) -> p o h", o=1).to_broadcast([128, nch, 16]), ALU.add,
               ["ps3", "dtb"], [tg + "tmp"])
            if CUT2 < 2:
                return None
            act(tmp, tmp, AF.Exp, [tg + "tmp"], [tg + "tmp"])
            if CUT2 < 3:
                return None
            act(dtv, tmp, AF.Ln, [tg + "tmp"], [tg + "dt"], bias=1.0)
            if CUT2 < 4:
                return None
            tt(v3(adt), v3(dtv), a_s[:, dcol:dcol + 16].rearrange("p (o h) -> p o h", o=1).to_broadcast([128, nch, 16]), ALU.mult,
               [tg + "dt", "a_s"], [tg + "adt"])
            if CUT2 < 5:
                return None
            pcs = pbank[3][:, 0:n]
            mm(pcs, TLEf if dirn == "F" else TGEf, adt, True, True, ["cf", tg + "adt"], ["ps3"])
            cp(tmp, pcs, ["ps3"], [tg + "tmp"])
            if CUT2 < 6:
                return None
            ptot = pbank[3][:, 0:n]
            mm(ptot, onesf, adt, True, True, ["onesf", tg + "adt"], ["ps3"])
            if CUT2 < 7:
                cp(tmp, ptot, ["ps3"], [tg + "tmp"])
                return None
            act(dec, ptot, AF.Exp, ["ps3"], [tg + "dec"])
            tt(tmp, ptot, tmp, ALU.subtract, ["ps3", tg + "tmp"], [tg + "tmp"])
            act(tmp, tmp, AF.Exp, [tg + "tmp"], [tg + "tmp"])
            tt(wgt, tmp, dtv, ALU.mult, [tg + "tmp", tg + "dt"], [tg + "wgt"])
            return dict(dt=v3(dtv), adt=v3(adt), wgt=v3(wgt), dec=v3(dec))

        def alloc_scan_set(ntok):
            nch = ntok // 128
            return dict(wdt=A.take(16 * 32 * 2, BF16).rearrange("p (k n) -> p k n", k=16), dt=alloc_dt_set(nch),
                        wb=[A.take(16 * 128 * 2, BF16).rearrange("p (k n) -> p k n", k=16) for _ in range(2)],
                        stage=[A.take((ntok + 2) * 4, F32)], tmp=[A.take(ntok * 4, F32)],
                        cT=A.take(ntok * 2, BF16),
                        xs_tok=A.take(nch * 256 * 2, BF16).rearrange("p (c n) -> p c n", n=256),
                        b_tok=A.take(nch * 128 * 2, BF16).rearrange("p (c n) -> p c n", n=128),
                        xsw=[A.take(256 * 2, BF16) for _ in range(2)])

        def scanF(XT, xk, ntok, has_halo, taps, dcol, Hst, Hk, ss):
            nch = ntok // 128
            wdt_v = ss["wdt"]
            dma("pool", wdt_v, w_dt.rearrange("(k p) n -> p k n", p=128), writes=["s_wdt"])
            dd = dt_prep(wdt_v, "s_wdt", dcol, XT, xk, nch, "F", "s_d", ss["dt"])
            wb = ss["wb"]
            stage_f, tmp_f, cT = ss["stage"][0], ss["tmp"][0], ss["cT"]
            xs_tok = ss["xs_tok"][:, 0:nch, :]
            b_tok = ss["b_tok"][:, 0:nch, :]
            for g in range(4):
                wv, wk = load_w(wb, w_in[:, OFF_B + g * 128:OFF_B + (g + 1) * 128], 128)
                conv_chunk(wv, wk, 0, XT, xk, ntok, has_halo, 8 + g, taps, cT, "s_cT", stage_f, tmp_f)
                to_tok(cT, "s_cT", nch, b_tok, "s_btok", 0)
                for pr in range(2):
                    wv, wk = load_w(wb, w_in[:, OFF_XS + g * 256 + pr * 128:OFF_XS + g * 256 + (pr + 1) * 128], 128)
                    conv_chunk(wv, wk, 0, XT, xk, ntok, has_halo, g * 2 + pr, taps, cT, "s_cT", stage_f, tmp_f)
                    to_tok(cT, "s_cT", nch, xs_tok, "s_xstok", pr * 128)
                Hg = Hst[:, g * 256:(g + 1) * 256]
                for c in range(nch):
                    xsw = ss["xsw"][c % 2]
                    xswk = "s_xsw%d" % (c % 2)
                    tt(xsw.rearrange("p (h d) -> p h d", d=64), xs_tok[:, c, :].rearrange("p (h d) -> p h d", d=64),
                       dd["wgt"][:, c, g * 4:(g + 1) * 4].rearrange("p (h o) -> p h o", o=1).to_broadcast([128, 4, 64]), ALU.mult,
                       ["s_xstok", "s_dwgt"], [xswk])
                    pst = pbank[7][:, 0:256]
                    mm(pst, b_tok[:, c, :], xsw, True, True, ["s_btok", xswk], ["ps7"])
                    tt(Hg.rearrange("p (h d) -> p h d", d=64), Hg.rearrange("p (h d) -> p h d", d=64),
                       dd["dec"][:, c, g * 4:(g + 1) * 4].rearrange("p (h o) -> p h o", o=1).to_broadcast([128, 4, 64]), ALU.mult,
                       [Hk, "s_ddec"], [Hk])
                    tt(Hg, Hg, pst, ALU.add, [Hk, "ps7"], [Hk])

        HF = Hst[:, 0:1024]
        HB = Hst[:, 1024:2048]
        memset(HF, 0.0, ["HF"])
        kctxT = st.enter_context(nc.sbuf_tensor("kctxT", [128, 4 * 256], BF16))
        kctxB = st.enter_context(nc.sbuf_tensor("kctxB", [128, 4 * 256], BF16))
        memset(kctxT[64:128, :], 0.0, ["kctxT"])
        memset(kctxB[0:64, :], 0.0, ["kctxB"])
        vctx = st.enter_context(nc.sbuf_tensor("vctx", [128, 2 * 256], BF16))
        TAPS_F = (0, 1, 2)
        TAPS_R = (2, 1, 0)
        m2 = A.mark()
        XT = A.take(16 * NTH * 2, BF16).rearrange("p (k n) -> p k n", k=16)
        XK = xkeys("XTl")
        m3 = A.mark()
        bset = alloc_build_set()
        XTc = A.take(16 * NCTX * 2, BF16).rearrange("p (k n) -> p k n", k=16)
        sset = alloc_scan_set(NCTX)
        assert A.off <= ARENA - 41984
        xkc = xkeys("XTc")
        build_xmT(XTc, "XTc", ctxf, 2, 1, bset)
        build_xmT(XT, "XTl", xloc, 17, 0, bset)
        modulate(XTc, "XTc", 2, 1)
        modulate(XT, "XTl", 17, 0)
        scanF(XTc, xkc, NCTX, False, TAPS_F, 0, HF, "HF", sset)
        wkv = sset["wb"]
        for g in range(4):
            wv, wk = load_w(wkv, w_in[:, OFF_K + g * 64:OFF_K + (g + 1) * 64], 64)
            wi = (wslot[0] - 1) % 2
            dma("pool", wkv[wi][:, :, 64:128], w_in[:, OFF_K + g * 64:OFF_K + (g + 1) * 64].rearrange("(k p) n -> p k n", p=128),
                writes=[wk])
            ps, pk = inproj_T(wkv[wi][:, :, 0:128], wk, 0, 128, XTc, xkc, 0, NCTX)
            cp(kctxT[0:64, g * 256:(g + 1) * 256], ps[0:64, :], [pk], ["kctxT"])
            cp(kctxB[64:128, g * 256:(g + 1) * 256], ps[64:128, :], [pk], ["kctxB"])
        for hv in range(2):
            wv, wk = load_w(wkv, w_in[:, OFF_V + hv * 128:OFF_V + (hv + 1) * 128], 128)
            for t in range(2):
                psv = pbank[7][:, 0:128]
                inproj_tok(wv, wk, 0, 128, XTc, xkc, t * 128, psv, "ps7")
                cp(vctx[:, t * 256 + hv * 128:t * 256 + (hv + 1) * 128], psv, ["ps7"], ["vctx"])
        A.reset(m3)

        if "H" in dbg_out:
            dma("sp", dbg_out["H"], Hst[:, :], ["HF", "HB"], ["dbg_H"])
        if "kctx" in dbg_out and stage >= 2:
            t_ = A.take(1024 * 4, F32)
            cp(t_, kctxT[:, :], ["kctxT"], ["t_"])
            dma("sp", dbg_out["kctx"], t_, ["t_"], ["dbg_kctx"])

        if stage >= 5:
            wdt_v = A.take(16 * 32 * 2, BF16).rearrange("p (k n) -> p k n", k=16)
            dma("pool", wdt_v, w_dt.rearrange("(k p) n -> p k n", p=128), writes=["l_wdt"])
            dF = dt_prep(wdt_v, "l_wdt", 0, XT, XK, 16, "F", "lF", alloc_dt_set(16))
            dB = dt_prep(wdt_v, "l_wdt", 16, XT, XK, 16, "B", "lB", alloc_dt_set(16))
            wb = [A.take(16 * 128 * 2, BF16).rearrange("p (k n) -> p k n", k=16) for _ in range(2)]
            BTs = [A.take(NT * 2, BF16) for _ in range(2)]
            CT = A.take(NT * 2, BF16)
            xs_toks = [A.take(16 * 256 * 2, BF16).rearrange("p (c n) -> p c n", n=256) for _ in range(2)]
            b_toks = [A.take(16 * 128 * 2, BF16).rearrange("p (c n) -> p c n", n=128) for _ in range(2)]
            prevFas = [A.take(16 * 256 * 2, BF16).rearrange("p (c n) -> p c n", n=256) for _ in range(2)]
            szT = A.take(2 * NT * 2, BF16).rearrange("p (c n) -> p c n", c=2)
            prevB = A.take(16 * 256 * 2, BF16).rearrange("p (c n) -> p c n", n=256)
            xswF = [A.take(256 * 2, BF16) for _ in range(2)]
            ccin_s = A.take(256 * 4, F32)
            ccout_s = A.take(256 * 4, F32)
            Dsk = A.take(4 * 128 * 2, BF16).rearrange("p (h n) -> p h n", h=4)
            HALF = 1024
            stage_f = A.take((HALF + 2) * 4, F32)
            tmp_f = A.take(HALF * 4, F32)
            xsT = A.take(NT * 2, BF16)
            lhs4 = A.take(512 * 4, F32).rearrange("p (h n) -> p h n", h=4)
            L4 = A.take(512 * 2, BF16).rearrange("p (h n) -> p h n", h=4)
            E4 = A.take(512 * 2, BF16).rearrange("p (h n) -> p h n", h=4)
            G4 = [A.take(512 * 2, BF16).rearrange("p (h n) -> p h n", h=4) for _ in range(2)]
            GE4 = [A.take(512 * 2, BF16).rearrange("p (h n) -> p h n", h=4) for _ in range(2)]
            CBm2 = A.take(256 * 2, BF16).rearrange("p (d n) -> p d n", d=2)
            lhs4s = [lhs4, A.take(512 * 4, F32).rearrange("p (h n) -> p h n", h=4)]
            xdt = [A.take(256 * 2, BF16) for _ in range(2)]
            xsw = A.take(256 * 2, BF16)
            ystg = [A.take(256 * 2, BF16).rearrange("p (c n) -> p c n", c=2) for _ in range(2)]
            h3 = lambda a: a.rearrange("p (h d) -> p h d", d=64)
            bch = lambda a, c, g: a[:, c, g * 4:(g + 1) * 4].rearrange("p (h o) -> p h o", o=1).to_broadcast([128, 4, 64])
            for g in range(4):
                par = g % 2
                BT, xs_tok, b_tok, prevFa = BTs[par], xs_toks[par], b_toks[par], prevFas[par]
                kBT, kxs, kbt, kpf, kHB = "BT%d" % par, "xstok%d" % par, "btok%d" % par, "prevFa%d" % par, "HB%d" % par
                HFg = HF[:, g * 256:(g + 1) * 256]
                HBg = HB[:, par * 256:(par + 1) * 256]
                for hf in range(2):
                    tb = hf * HALF
                    wv, wk = load_w(wb, w_in[:, OFF_B + g * 128:OFF_B + (g + 1) * 128], 128)
                    conv_chunk(wv, wk, 0, XT, XK, HALF, True, 8 + g, TAPS_F, BT, kBT, stage_f, tmp_f, tb=tb, left=(hf == 1))
                to_tok(BT, kBT, 16, b_tok, kbt, 0)
                for pr in range(2):
                    for hf in range(2):
                        tb = hf * HALF
                        wv, wk = load_w(wb, w_in[:, OFF_XS + g * 256 + pr * 128:OFF_XS + g * 256 + (pr + 1) * 128], 128)
                        conv_chunk(wv, wk, 0, XT, XK, HALF, True, g * 2 + pr, TAPS_F, xsT, "xsT", stage_f, tmp_f, tb=tb, left=(hf == 1))
                    to_tok(xsT, "xsT", 16, xs_tok, kxs, pr * 128)
                for c in range(16):
                    act(prevFa[:, c, :], HFg, AF.Copy, ["HF"], [kpf])
                    xw = xswF[c % 2]
                    xwk = "xswF%d" % (c % 2)
                    tt(h3(xw), h3(xs_tok[:, c, :]), bch(dF["wgt"], c, g), ALU.mult, [kxs, "lFwgt"], [xwk])
                    mm(pbank[5][:, 0:256], b_tok[:, c, :], xw, True, True, [kbt, xwk], ["ps5"])
                    tt(h3(HFg), h3(HFg), bch(dF["dec"], c, g), ALU.mult, ["HF", "lFdec"], ["HF"])
                    tt(HFg, HFg, pbank[5][:, 0:256], ALU.add, ["HF", "ps5"], ["HF"])
                cp(ccin_s, HFg, ["HF"], ["ccin_s"])
                dma("sp", cc_in[g].ap(), ccin_s, ["ccin_s"], ["cc_in%d" % g])
                S.add("pool", lambda e, g=g: e.collective_compute("AllReduce", ALU.add, replica_groups=[[2 * i, 2 * i + 1] for i in range(ncores // 2)],
                                                                  ins=[cc_in[g].ap().opt()], outs=[cc_out[g].ap().opt()]),
                      ["cc_in%d" % g], ["cc_out%d" % g], cost=25.0)
                dma("sp", ccout_s, cc_out[g].ap(), ["cc_out%d" % g], ["ccout_s"])
                tt(HBg, ccout_s, ccin_s, ALU.subtract, ["ccout_s", "ccin_s"], [kHB])
                for hf in range(2):
                    tb = hf * HALF
                    wv, wk = load_w(wb, w_in[:, OFF_C + g * 128:OFF_C + (g + 1) * 128], 128)
                    conv_chunk(wv, wk, 0, XT, XK, HALF, True, 12 + g, TAPS_F, CT, "CT", stage_f, tmp_f, tb=tb, left=(hf == 1))
                for pr in range(2):
                    wv, wk = load_w(wb, w_in[:, OFF_Z + g * 256 + pr * 128:OFF_Z + g * 256 + (pr + 1) * 128], 128)
                    for (t0, n) in pieces(NT):
                        ps, pk = inproj_T(wv, wk, 0, 128, XT, XK, t0, n)
                        act(szT[:, pr, t0:t0 + n], ps, AF.Silu, [pk], ["szT"])
                for hh in range(4):
                    ts(Dsk[:, hh, :], ident_f, dskb_s[:, g * 4 + hh:g * 4 + hh + 1], None, ALU.mult, None, ["cf", "dskb"], ["Dsk"])
                for c in range(15, -1, -1):
                    act(prevB[:, c, :], HBg, AF.Copy, [kHB], ["prevB"])
                    tt(h3(xsw), h3(xs_tok[:, c, :]), bch(dB["wgt"], c, g), ALU.mult, [kxs, "lBwgt"], ["xsw"])
                    mm(pbank[5][:, 256:512], b_tok[:, c, :], xsw, True, True, [kbt, "xsw"], ["ps5"])
                    tt(h3(HBg), h3(HBg), bch(dB["dec"], c, g), ALU.mult, [kHB, "lBdec"], [kHB])
                    tt(HBg, HBg, pbank[5][:, 256:512], ALU.add, [kHB, "ps5"], [kHB])
                for c in range(16):
                    cs_ = slice(c * 128, (c + 1) * 128)
                    mm(pbank[5][:, 0:128], BT[:, cs_], CT[:, cs_], True, True, [kBT, "CT"], ["ps5"])
                    tt(CBm2, pbank[5][:, 0:128].rearrange("p (o n) -> p o n", o=1).to_broadcast([128, 2, 128]),
                       cf[:, 128:384].rearrange("p (d n) -> p d n", d=2), ALU.mult, ["ps5", "cf"], ["CBm"])
                    for di, (dd_, msk, tri) in enumerate([(dF, SUf, TLEf), (dB, SLf, TGEf)]):
                        pD, pE = 6, 7
                        dk = "lF" if di == 0 else "lB"
                        adt4 = dd_["adt"][:, c, g * 4:(g + 1) * 4]
                        lh = lhs4s[di]
                        if True:
                            tt(lh, msk.rearrange("p (o n) -> p o n", o=1).to_broadcast([128, 4, 128]),
                               adt4.rearrange("p (h o) -> p h o", o=1).to_broadcast([128, 4, 128]), ALU.mult, ["cf", dk + "adt"], ["lhs4%d" % di])
                        else:
                            for hh in range(4):
                                act(lh[:, hh, :], msk, AF.Copy, ["cf", dk + "adt"], ["lhs4%d" % di], scale=adt4[:, hh:hh + 1])
                        for hh in range(4):
                            mm(pbank[pD][:, hh * 128:(hh + 1) * 128], lh[:, hh, :], tri, True, True, ["lhs4%d" % di, "cf"], ["ps%d" % pD])
                        act(L4, pbank[pD][:, :].rearrange("p (h n) -> p h n", h=4), AF.Exp, ["ps%d" % pD], ["L4"])
                        tt(G4[di], L4, CBm2[:, di:di + 1, :].to_broadcast([128, 4, 128]), ALU.mult,
                           ["L4", "CBm"], ["G4%d" % di])
                        for hh in range(4):
                            mm(pbank[pE][:, hh * 128:(hh + 1) * 128], adt4[:, hh:hh + 1].to_broadcast([128, 128]), tri, True, True,
                               [dk + "adt", "cf"], ["ps%d" % pE])
                        act(E4, pbank[pE][:, :].rearrange("p (h n) -> p h n", h=4), AF.Exp, ["ps%d" % pE], ["E4"])
                        tt(GE4[di], E4, CT[:, cs_].rearrange("p (o n) -> p o n", o=1).to_broadcast([128, 4, 128]), ALU.mult,
                           ["E4", "CT"], ["GE4%d" % di])
                        tt(h3(xdt[di]), h3(xs_tok[:, c, :]), bch(dd_["dt"], c, g), ALU.mult, [kxs, dk + "dt"], ["xdt%d" % di])
                    for r in range(2):
                        yo = pbank[3][:, r * 256:(r + 1) * 256]
                        hs = slice(r * 128, (r + 1) * 128)
                        h2 = slice(2 * r, 2 * r + 2)
                        mm(yo, xdt[0][:, hs], G4[0][:, h2, :], True, False, ["xdt0", "G40"], ["ps3"])
                        mm(yo, prevFa[:, c, hs], GE4[0][:, h2, :], False, False, [kpf, "GE40"], ["ps3"])
                        mm(yo, xdt[1][:, hs], G4[1][:, h2, :], False, False, ["xdt1", "G41"], ["ps3"])
                        mm(yo, prevB[:, c, hs], GE4[1][:, h2, :], False, False, ["prevB", "GE41"], ["ps3"])
                        mm(yo, xs_tok[:, c, hs], Dsk[:, h2, :], False, True, [kxs, "Dsk"], ["ps3"])
                    yk = "ystg%d" % (c % 2)
                    for b_ in range(2):
                        rw = slice(b_ * 64, (b_ + 1) * 64)
                        yv = pbank[3][rw, :].rearrange("p (r b n) -> p r b n", r=2, b=2)[:, :, b_, :]
                        tt(ystg[c % 2][rw, :, :], yv, szT[rw, :, cs_], ALU.mult, ["ps3", "szT"], [yk])
                    dma("sp", h_scr[c][:, 2 * g:2 * g + 2, :], ystg[c % 2], [yk], ["h_scr"])
            A.reset(m3)

            if stage >= 6:
                agT = A.take(8 * NT * 2, BF16).rearrange("p (c n) -> p c n", c=8)
                m4 = A.mark()
                A.off = m2
                wo = A.take(16 * D * 2, BF16).rearrange("p (k n) -> p k n", k=16)
                assert A.off <= m3
                A.off = m4
                wb = [A.take(16 * 128 * 2, BF16).rearrange("p (k n) -> p k n", k=16) for _ in range(2)]
                kTs = [A.take(NTH * 2, BF16) for _ in range(2)]
                kTBs = [A.take(NTH * 2, BF16) for _ in range(2)]
                qTs = [A.take(2 * NT * 2, BF16).rearrange("p (a n) -> p a n", a=2) for _ in range(2)]
                sgTs = [A.take(2 * NT * 2, BF16).rearrange("p (a n) -> p a n", a=2) for _ in range(2)]
                VA = A.take(19 * 128 * 2, BF16).rearrange("p (t n) -> p t n", n=128)
                VB = A.take(19 * 128 * 2, BF16).rearrange("p (t n) -> p t n", n=128)
                cos_b = A.take(NTH * 2, BF16)
                ssin_b = A.take(NTH * 2, BF16)
                qraw = A.take(512 * 2, BF16)
                rt1 = Hst[:, 1024:1536]
                rt2 = Hst[:, 1536:2048]
                PT = A.take(5 * 512 * 2, BF16).rearrange("p (c n) -> p c n", c=5)
                lnd = [Hst[:, 0:256], Hst[:, 256:512]]
                rd = lnd
                t1 = [Hst[:, 512:768], Hst[:, 768:1024]]
                for par in range(2):
                    memset(kTs[par][64:128, :], 0.0, ["kT%d" % par])
                    memset(kTBs[par][0:64, :], 0.0, ["kTB%d" % par])
                memset(VA[:, :, 64:128], 1.0, ["VA"])
                memset(VB[:, :, 0:64], 1.0, ["VB"])
                dma("pool", cos_b, cosT, writes=["cos_b"])
                dma("pool", ssin_b, ssinT, writes=["ssin_b"])

                def rope_proj(wv2, wk, ntok, dsts):
                    for (t0, n) in pieces(ntok):
                        ps, pk = inproj_T(wv2, wk, 0, 128, XT, XK, t0, n)
                        act(qraw[:, 0:n], ps, AF.Copy, [pk], ["qraw"])
                        mm(pbank[5][:, 0:n], perm_b, qraw[:, 0:n], True, True, ["cb16", "qraw"], ["ps5"])
                        tt(rt1[:, 0:n], qraw[:, 0:n], cos_b[:, t0:t0 + n], ALU.mult, ["qraw", "cos_b"], ["rt1"])
                        tt(rt2[:, 0:n], pbank[5][:, 0:n], ssin_b[:, t0:t0 + n], ALU.mult, ["ps5", "ssin_b"], ["rt2"])
                        for (dst, rw, dstk) in dsts:
                            tt(dst[rw, t0:t0 + n], rt1[rw, 0:n], rt2[rw, 0:n], ALU.add, ["rt1", "rt2"], [dstk])

                for g in range(4):
                    par = g % 2
                    kT, kTB, qT, sgT = kTs[par], kTBs[par], qTs[par], sgTs[par]
                    kkT, kkTB, kqT, ksg = "kT%d" % par, "kTB%d" % par, "qT%d" % par, "sgT%d" % par
                    wv, wk = load_w(wb, w_in[:, OFF_K + g * 64:OFF_K + (g + 1) * 64], 64)
                    wi = (wslot[0] - 1) % 2
                    dma("pool", wb[wi][:, :, 64:128], w_in[:, OFF_K + g * 64:OFF_K + (g + 1) * 64].rearrange("(k p) n -> p k n", p=128),
                        writes=[wk])
                    rope_proj(wb[wi][:, :, 0:128], wk, NTH, [(kT, slice(0, 64), kkT), (kTB, slice(64, 128), kkTB)])
                    wv, wk = load_w(wb, w_in[:, OFF_V + g * 64:OFF_V + (g + 1) * 64], 64)
                    for t4 in range(0, 17, 4):
                        nt_ = min(4, 17 - t4)
                        for i in range(nt_):
                            inproj_tok(wv, wk, 0, 64, XT, XK, (t4 + i) * 128, pbank[5][:, i * 64:(i + 1) * 64], "ps5")
                        pv = pbank[5][:, 0:nt_ * 64].rearrange("p (t n) -> p t n", n=64)
                        cp(VA[:, t4:t4 + nt_, 0:64], pv, ["ps5"], ["VA"])
                        act(VB[:, t4:t4 + nt_, 64:128], pv, AF.Copy, ["ps5"], ["VB"])
                    vc3 = vctx[:, :].rearrange("p (t n) -> p t n", t=2)
                    cp(VA[:, 17:19, 0:64], vc3[:, :, g * 64:(g + 1) * 64], ["vctx"], ["VA"])
                    cp(VB[:, 17:19, 64:128], vc3[:, :, g * 64:(g + 1) * 64], ["vctx"], ["VB"])
                    for a in range(2):
                        wv, wk = load_w(wb, w_in[:, OFF_Q + (4 * g + 2 * a) * 64:OFF_Q + (4 * g + 2 * a + 2) * 64], 128)
                        rope_proj(wv, wk, NT, [(qT[:, a, :], slice(0, 128), kqT)])
                        wv, wk = load_w(wb, w_in[:, OFF_G + (4 * g + 2 * a) * 64:OFF_G + (4 * g + 2 * a + 2) * 64], 128)
                        for (t0, n) in pieces(NT):
                            ps, pk = inproj_T(wv, wk, 0, 128, XT, XK, t0, n)
                            act(sgT[:, a, t0:t0 + n], ps, AF.Silu, [pk], [ksg])
                    if g == 3:
                        xall = ["XTl%d" % t for t in range(17)]
                        for jb in range(4):
                            dma("pool", wo[:, :, jb * 512:(jb + 1) * 512],
                                w_out[:, jb * 512:(jb + 1) * 512].rearrange("(k p) n -> p k n", p=128), writes=["wo%d" % jb] + xall)
                        for jb in range(4):
                            for ch in range(16):
                                gv = attg_s[:, ch:ch + 1] if ch < 8 else ssdg_s[:, ch - 8:ch - 7]
                                ts(wo[:, ch, jb * 512:(jb + 1) * 512], wo[:, ch, jb * 512:(jb + 1) * 512], gv, None, ALU.mult, None,
                                   ["wo%d" % jb, "attg", "ssdg"], ["wo%d" % jb])
                    for n in range(16):
                        chunks = []
                        if n > 0:
                            chunks.append((n - 1, "k", TGEb))
                        chunks.append((n, "k", None))
                        chunks.append((n + 1, "k", TLEb))
                        chunks.append((17, "c", None))
                        chunks.append((18, "c", None))
                        qs = slice(n * 128, (n + 1) * 128)
                        for ci, (tile, kind, msk) in enumerate(chunks):
                            pb_ = 6 + ci % 2
                            for b_ in range(2):
                                if kind == "k":
                                    lk = (kT if b_ == 0 else kTB)[:, tile * 128:(tile + 1) * 128]
                                    lkk = kkT if b_ == 0 else kkTB
                                else:
                                    lk = (kctxT if b_ == 0 else kctxB)[:, g * 256 + (tile - 17) * 128:g * 256 + (tile - 16) * 128]
                                    lkk = "kctxT" if b_ == 0 else "kctxB"
                                mm(pbank[pb_][:, b_ * 256:(b_ + 1) * 256], lk, qT[:, :, qs], True, True, [lkk, kqT], ["ps%d" % pb_])
                            act(PT[:, ci, :], pbank[pb_][:, :], AF.Exp, ["ps%d" % pb_], ["PT%d" % ci], scale=0.125)
                            if msk is not None:
                                p4 = PT[:, ci, :].rearrange("p (h n) -> p h n", h=4)
                                tt(p4, p4, msk.rearrange("p (o n) -> p o n", o=1).to_broadcast([128, 4, 128]), ALU.mult,
                                   ["PT%d" % ci, "cb16"], ["PT%d" % ci])
                        nci = len(chunks)
                        for ci, (tile, kind, msk) in enumerate(chunks):
                            mm(pbank[2][:, 0:256], VA[:, tile, :], PT[:, ci, 0:256], ci == 0, ci == nci - 1, ["VA", "PT%d" % ci], ["ps2"])
                        for ci, (tile, kind, msk) in enumerate(chunks):
                            mm(pbank[4][:, 0:256], VB[:, tile, :], PT[:, ci, 256:512], ci == 0, ci == nci - 1, ["VB", "PT%d" % ci], ["ps4"])
                        for b_ in range(2):
                            po = pbank[2] if b_ == 0 else pbank[4]
                            pk = "ps2" if b_ == 0 else "ps4"
                            nr = slice(b_ * 64, (b_ + 1) * 64)
                            dr = slice((1 - b_) * 64, (2 - b_) * 64)
                            for a_ in range(2):
                                h = 4 * g + 2 * a_ + b_
                                act(lnd[b_][dr, a_ * 128:(a_ + 1) * 128], po[dr, a_ * 128:(a_ + 1) * 128], AF.Ln, [pk, "esink"], ["lnd%d" % b_],
                                    bias=esink[dr, h:h + 1])
                            act(rd[b_][dr, :], lnd[b_][dr, :], AF.Exp, ["lnd%d" % b_], ["rd%d" % b_], scale=-1.0)
                            tt(t1[b_][nr, :], po[nr, 0:256], rd[b_][dr, :], ALU.mult, [pk, "rd%d" % b_], ["t1%d" % b_])
                            tt(agT[nr, 2 * g:2 * g + 2, qs], t1[b_][nr, :].rearrange("p (a n) -> p a n", a=2), sgT[nr, :, qs], ALU.mult,
                               ["t1%d" % b_, ksg], ["agT"])
                A.reset(m4)
                if "ag" in dbg_out:
                    for ch in range(4):
                        t_ = A.take(NT * 4, F32)
                        cp(t_, agT[:, ch, :], ["agT"], ["t_ag%d" % ch])
                        dma("sp", dbg_out["ag"][ch], t_, ["t_ag%d" % ch], ["dbg_ag"])
                    A.reset(m4)
                    for ch in range(4, 8):
                        t_ = A.take(NT * 4, F32)
                        cp(t_, agT[:, ch, :], ["agT"], ["t_ag%d" % ch])
                        dma("sp", dbg_out["ag"][ch], t_, ["t_ag%d" % ch], ["dbg_ag"])
                    A.reset(m4)

            if stage >= 7:
                S.barrier()
                A.off = m2
                A.off = m4
                gate_bc = A.take(D * 4, F32)
                fng_bc = A.take(D * 4, F32)
                xt_ = [A.take(D * 4, F32) for _ in range(2)]
                res = A.take(D * 4, F32)
                o1 = [A.take(512 * 4, F32) for _ in range(2)]
                sq = [A.take(128 * 2, BF16) for _ in range(2)]
                st2 = [A.take(64, F32) for _ in range(2)]
                ytile = [A.take(8 * 128 * 2, BF16) for _ in range(2)]
                dma("sp", gate_bc, gate_scr.to_broadcast([128, D]), ["gate_scr"], ["gate_bc"])
                dma("sp", fng_bc, fng.to_broadcast([128, D]), writes=["fng_bc"])
                ones2 = TLEb[:, 126:128]
                for t in range(16):
                    b_ = t % 2
                    tsl = slice(t * 128, (t + 1) * 128)
                    xk_ = "xt%d" % b_
                    sk_ = "st2%d" % b_
                    dma("sp", xt_[b_], xloc[tsl, :], writes=[xk_])
                    dma("sp", ytile[b_], h_scr[t].rearrange("p c n -> p (c n)"), ["h_scr"], ["yt%d" % b_])
                    ygT = ytile[b_].rearrange("p (c n) -> p c n", c=8)
                    for br, (src, srck, sl_) in enumerate([(agT, "agT", tsl), (ygT, "yt%d" % b_, slice(0, 128))]):
                        for ch in range(8):
                            act(sq[ch % 2], src[:, ch, sl_], AF.Square, [srck], ["sq%d" % (ch % 2)])
                            mm(pbank[3][:, br * 2:br * 2 + 2], sq[ch % 2], ones2, ch == 0, ch == 7, ["sq%d" % (ch % 2), "cb16"], ["ps3"])
                    s_ = st2[b_]
                    ts(s_[:, 0:4], pbank[3][:, 0:4], 1.0 / 1024.0, EPS, ALU.mult, ALU.add, ["ps3"], [sk_])
                    act(s_[:, 0:4], s_[:, 0:4], AF.Sqrt, [sk_], [sk_])
                    S.dve(lambda e, o=s_[:, 4:8], i=s_[:, 0:4]: e.reciprocal(out=o, in_=i), [sk_], [sk_])
                    for cb in range(4):
                        cs_ = slice(cb * 512, (cb + 1) * 512)
                        pa, pka = (pbank[0], "ps0") if cb % 2 == 0 else (pbank[6], "ps6")
                        pss, pks = (pbank[1], "ps1") if cb % 2 == 0 else (pbank[7], "ps7")
                        for ch in range(8):
                            mm(pa[:, :], agT[:, ch, tsl], wo[:, ch, cs_], ch == 0, ch == 7, ["agT", "wo%d" % cb], [pka])
                        for ch in range(8):
                            mm(pss[:, :], ygT[:, ch, :], wo[:, 8 + ch, cs_], ch == 0, ch == 7, ["yt%d" % b_, "wo%d" % cb], [pks])
                        ok_ = "o1%d" % (cb % 2)
                        oo = o1[cb % 2]
                        act(oo, pa[:, :], AF.Identity, [pka, sk_], [ok_], scale=s_[:, 5:6])
                        stt(oo, pss[:, :], s_[:, 7:8], oo, ALU.mult, ALU.add, [pks, sk_, ok_], [ok_])
                        tt(oo, oo, gate_bc[:, cs_], ALU.mult, [ok_, "gate_bc"], [ok_])
                        tt(res[:, cs_], oo, xt_[b_][:, cs_], ALU.add, [ok_, xk_], ["res"])
                    act(xt_[b_], res, AF.Square, ["res"], [xk_, sk_], accum=s_[:, 8:9])
                    ts(s_[:, 9:10], s_[:, 8:9], 1.0 / D, EPS, ALU.mult, ALU.add, [sk_], [sk_])
                    act(s_[:, 10:11], s_[:, 9:10], AF.Sqrt, [sk_], [sk_])
                    S.dve(lambda e, o=s_[:, 11:12], i=s_[:, 10:11]: e.reciprocal(out=o, in_=i), [sk_], [sk_])
                    stt(xt_[b_], res, s_[:, 11:12], fng_bc, ALU.mult, ALU.mult, ["res", sk_, "fng_bc"], [xk_])
                    dma("sp", out[tsl, :], xt_[b_], [xk_], ["out"])

        fw = [(q, i) for q in ("sp", "pool") for i, op in enumerate(S.ops[q]) if op["dma"]]
        S.emit(final_waits=fw)
    return nc


def _fm(v, nchunk):
    return np.ascontiguousarray(np.asarray(v, np.float32).reshape(nchunk, 128).T)


def _consts():
    k = np.arange(128)[:, None]
    l = np.arange(128)[None, :]
    ident = (k == l).astype(np.float32)
    tle = (k <= l).astype(np.float32)
    tge = (k >= l).astype(np.float32)
    su = (k > l).astype(np.float32)
    sl = (k < l).astype(np.float32)
    d = np.arange(128)
    sw = np.where((d % 32) < 16, d + 16, d - 16)
    perm = np.zeros((128, 128), np.float32)
    perm[sw, d] = 1.0
    return np.ascontiguousarray(np.concatenate([ident, tle, tge, su, sl, perm], axis=1))


def _rope_tables(pos):
    pos = np.asarray(pos)
    row = (pos // 64).astype(np.float32)
    col = (pos % 64).astype(np.float32)
    quarter = 16
    freq = (1.0 / (np.float32(10000.0) ** (np.arange(quarter, dtype=np.float32) / np.float32(quarter)))).astype(np.float32)
    cosT = np.zeros((128, len(pos)), np.float32)
    ssinT = np.zeros((128, len(pos)), np.float32)
    for p in range(128):
        dd = p % 64
        i = dd % 16
        pp = row if dd < 32 else col
        ang = (pp * freq[i]).astype(np.float32)
        cosT[p] = np.cos(ang)
        s = np.sin(ang)
        ssinT[p] = -s if (dd % 32) < 16 else s
    return cosT, ssinT


def prep_inputs(inp):
    g = lambda k: np.asarray(inp[k], np.float32)
    x, c, ctx, c_ctx = g("x"), g("c"), g("ctx"), g("c_ctx")
    w_mod, b_mod, norm_g, w_in = g("w_mod")[0], g("b_mod")[0], g("norm_g")[0], g("w_in")[0]
    conv_w, conv_b = g("conv_w")[0], g("conv_b")[0]
    a_f, a_b, db_f, db_b = g("a_log_f")[0], g("a_log_b")[0], g("dt_bias_f")[0], g("dt_bias_b")[0]
    d_skip, attg, ssdg, sink = g("d_skip")[0], g("att_norm_g")[0], g("ssd_norm_g")[0], g("sink")[0]
    w_out, fng = g("w_out")[0], g("final_norm_g")
    consts = _consts()
    shared = dict(w_in=np.ascontiguousarray(w_in), w_out=np.ascontiguousarray(w_out),
                  normgT=_fm(norm_g, 16), dskipT=_fm(np.repeat(d_skip, 64), 8), attgT=_fm(attg, 8), ssdgT=_fm(ssdg, 8),
                  sinkb=np.ascontiguousarray(np.broadcast_to(sink[None, :], (128, 16))),
                  dskb=np.ascontiguousarray(np.broadcast_to(d_skip[None, :], (128, 16))),
                  fng=np.ascontiguousarray(fng[None, :]), consts=consts)
    maps = []
    for core in range(8):
        b, h = core // 2, core % 2
        xb = x[b]
        xl = xb if h == 0 else xb[::-1]
        pos = np.arange(SEQ) if h == 0 else np.arange(SEQ)[::-1]
        cl = ctx[b] if h == 0 else ctx[b][::-1]
        cw = conv_w if h == 0 else conv_w[::-1]
        convT = np.stack([_fm(cw[0], 16), _fm(cw[1], 16), _fm(cw[2], 16), _fm(conv_b, 16)], axis=2).reshape(128, 64)
        if h == 0:
            wdt = w_in[:, OFF_DTF:OFF_DTF + 32]
            dtbv = np.concatenate([db_f, db_b])
            alv = np.concatenate([a_f, a_b])
        else:
            wdt = np.concatenate([w_in[:, OFF_DTB:OFF_DTB + 16], w_in[:, OFF_DTF:OFF_DTF + 16]], axis=1)
            dtbv = np.concatenate([db_b, db_f])
            alv = np.concatenate([a_b, a_f])
        cosT, ssinT = _rope_tables(pos[0:NTH])
        cvec = np.stack([_fm(c[b], 16), _fm(c_ctx, 16)], axis=2).reshape(128, 32)
        chs = [2 * i + h for i in range(16)]
        gbs = [2 * gb + h for gb in range(2)]
        wsh = np.concatenate([w_mod[:, ch * 128:(ch + 1) * 128] for ch in chs] +
                             [w_mod[:, 2 * D + k * 512:2 * D + (k + 1) * 512] for k in gbs], axis=1)
        bml = np.stack([b_mod[ch * 128:(ch + 1) * 128] for ch in chs], axis=1)
        bgl = np.concatenate([b_mod[2 * D + k * 512:2 * D + (k + 1) * 512] for k in gbs])[None, :]
        rm = np.zeros((128, 2), np.float32)
        rm[:, h] = 1.0
        m = dict(shared)
        m.update(xloc=np.ascontiguousarray(xl[0:NTH]), ctxf=np.ascontiguousarray(cl),
                 w_mod_sh=np.ascontiguousarray(wsh), bmod_l=np.ascontiguousarray(bml), bgate_l=np.ascontiguousarray(bgl), rmask=rm,
                 cvec=np.ascontiguousarray(cvec), w_dt=np.ascontiguousarray(wdt), convT=np.ascontiguousarray(convT),
                 dtb=np.ascontiguousarray(np.broadcast_to(dtbv[None, :], (128, 32))),
                 alog=np.ascontiguousarray(np.broadcast_to(alv[None, :], (128, 32))),
                 cosT=cosT, ssinT=ssinT)
        maps.append(m)
    return maps


def kernel(**inputs):
    maps = prep_inputs(inputs)
    nc = build()
    res = run_bass_kernel_spmd(nc, maps, core_ids=list(range(8)))
    outp = np.zeros((4, SEQ, D), np.float32)
    for core in range(8):
        b, h = core // 2, core % 2
        o = res.results[core]["out"]
        if h == 0:
            outp[b, 0:NT] = o
        else:
            outp[b, NT:SEQ] = o[::-1]
    return outp
```

```python
import contextlib
import os
import numpy as np
CUT = int(os.environ.get('KCUT', '99'))
CUT2 = int(os.environ.get('KCUT2', '99'))
import concourse.bass as bass
import concourse.mybir as mybir
from concourse.bass_utils import run_bass_kernel_spmd

F32 = mybir.dt.float32
BF16 = mybir.dt.bfloat16
AF = mybir.ActivationFunctionType
ALU = mybir.AluOpType

D = 2048
NT = 2048
NTH = 2176
SEQ = 4096
NCTX = 256
NIN = 5664
OFF_Q, OFF_K, OFF_V, OFF_G, OFF_Z, OFF_XS, OFF_B, OFF_C = 0, 1024, 1280, 1536, 2560, 3584, 4608, 5120
OFF_DTF, OFF_DTB = 5632, 5648
EPS = 1e-6


class Sched:
    ENGS = ["pe", "act", "dve", "pool", "sp"]
    NDMASEM = 4
    WINDOW = 96

    def __init__(self, nc):
        self.nc = nc
        self.ops = {e: [] for e in self.ENGS}
        self.lastw = {}
        self.readers = {}
        self.ndma = {e: 0 for e in self.ENGS}
        self.epoch = 0

    def barrier(self):
        self.epoch += 1
        self.lastw = {}
        self.readers = {}

    def add(self, eng, fn, reads=(), writes=(), dma=False, cost=0.3):
        idx = len(self.ops[eng])
        psr = [k for k in reads if k.startswith("ps")]
        if psr:
            reads = [k for k in reads if not k.startswith("ps")]
            writes = list(writes) + [k for k in psr if k not in writes]
        deps = set()
        for k in reads:
            if k in self.lastw:
                deps.add(self.lastw[k])
        for k in writes:
            if k in self.lastw:
                deps.add(self.lastw[k])
            for r in self.readers.get(k, ()):
                deps.add(r)
        deps.discard((eng, idx))
        odeps = set()
        if eng == "pe":
            odeps = set(d for d in deps if d[0] == "pe")
            deps = deps - odeps
        op = dict(fn=fn, deps=deps, odeps=odeps, dma=dma, marked=False, cost=cost, epoch=self.epoch)
        self.ops[eng].append(op)
        for k in reads:
            self.readers.setdefault(k, []).append((eng, idx))
        for k in writes:
            self.lastw[k] = (eng, idx)
            self.readers[k] = []
        return (eng, idx)

    def pe(self, fn, reads=(), writes=(), cost=0.15):
        return self.add("pe", fn, reads, writes, cost=cost)

    def act(self, fn, reads=(), writes=(), cost=0.4):
        return self.add("act", fn, reads, writes, cost=cost)

    def dve(self, fn, reads=(), writes=(), cost=0.4):
        return self.add("dve", fn, reads, writes, cost=cost)

    def dma(self, q, fn, reads=(), writes=(), cost=4.0):
        return self.add(q, fn, reads, writes, dma=True, cost=cost)

    def schedule(self):
        ops = self.ops
        pending = {e: list(range(len(ops[e]))) for e in self.ENGS}
        done = {e: [None] * len(ops[e]) for e in self.ENGS}
        free = {e: 0.0 for e in self.ENGS}
        order = {e: [] for e in self.ENGS}
        dma_free = [0.0]
        self.fence_pending = {}
        ntot = sum(len(v) for v in pending.values())
        SYNC = 0.2
        epoch = 0
        last_exec = {}
        for _ in range(ntot):
            while not any(pending[e] and ops[e][pending[e][0]]["epoch"] == epoch for e in self.ENGS):
                epoch += 1
                tmax = max(free.values())
                for e in self.ENGS:
                    free[e] = tmax
                    if pending[e]:
                        nxt = [i for i in pending[e] if ops[e][i]["epoch"] == epoch]
                        if nxt:
                            self.fence_pending.setdefault(e, set()).update(last_exec.values())
                for k in [k for k in last_exec if isinstance(k, tuple)]:
                    del last_exec[k]
            best = None
            for e in self.ENGS:
                pl = pending[e]
                fe = free[e]
                for wi in range(min(self.WINDOW, len(pl))):
                    i = pl[wi]
                    op = ops[e][i]
                    if op["epoch"] != epoch:
                        break
                    st = fe
                    ok = True
                    for (de, di) in op["deps"]:
                        t = done[de][di]
                        if t is None:
                            ok = False
                            break
                        if t + SYNC > st:
                            st = t + SYNC
                    if ok:
                        for (de, di) in op["odeps"]:
                            if done[de][di] is None:
                                ok = False
                                break
                    if not ok:
                        continue
                    key = (st, wi)
                    if best is None or key < best[0]:
                        best = (key, e, wi, i, st)
                    if st <= fe:
                        break
            assert best is not None, "scheduler deadlock"
            _, e, wi, i, st = best
            op = ops[e][i]
            if op["dma"]:
                free[e] = st + 0.15
                t0 = max(st + 2.0, dma_free[0])
                dma_free[0] = t0 + op["cost"]
                done[e][i] = dma_free[0]
            else:
                free[e] = st + op["cost"]
                done[e][i] = free[e]
            fp = self.fence_pending.pop(e, None)
            if fp:
                op["deps"] = set(op["deps"]) | set(d for d in fp if d != (e, i))
            if op["dma"]:
                last_exec[(e, i)] = (e, i)
            else:
                last_exec[e] = (e, i)
            order[e].append(i)
            del pending[e][wi]
        self.est_time = max(max([0.0] + [t for t in done[e] if t is not None]) for e in self.ENGS)
        return order

    def emit(self, final_waits=()):
        nc = self.nc
        ops = self.ops
        fin = dict(fn=None, deps=set(final_waits), odeps=set(), dma=False, marked=False, cost=0.0, epoch=self.epoch)
        ops["sp"].append(fin)
        order = self.schedule()
        for e in self.ENGS:
            for op in ops[e]:
                for (de, di) in op["deps"]:
                    ops[de][di]["marked"] = True
        for e in self.ENGS:
            c = 0
            nd = 0
            for i in order[e]:
                op = ops[e][i]
                if op["dma"]:
                    op["dmaslot"] = nd
                    nd += 1
                elif op["marked"]:
                    c += 1
                    op["cnt"] = c
        with contextlib.ExitStack() as st:
            esem = {e: st.enter_context(nc.semaphore("s_" + e)) for e in self.ENGS}
            dsem = {}
            for e in self.ENGS:
                if self.ndma[e] or any(op["dma"] for op in ops[e]):
                    dsem[e] = [st.enter_context(nc.semaphore("d_%s%d" % (e, i)))
                               for i in range(self.NDMASEM)]
            block = st.enter_context(nc.Block())
            engobj = {"pe": "tensor", "act": "scalar", "dve": "vector", "pool": "gpsimd", "sp": "sync"}

            def run(e, eng):
                seen = {}
                seen_dma = set()
                for i in order[e]:
                    op = ops[e][i]
                    waits = []
                    if op["dma"]:
                        slot = op["dmaslot"]
                        if slot >= self.NDMASEM:
                            waits.append((dsem[e][slot % self.NDMASEM], 16 * (slot // self.NDMASEM)))
                    cneed = {}
                    for (de, di) in op["deps"]:
                        dop = ops[de][di]
                        if dop["dma"]:
                            if (de, di) in seen_dma:
                                continue
                            seen_dma.add((de, di))
                            s = dop["dmaslot"]
                            waits.append((dsem[de][s % self.NDMASEM], 16 * (s // self.NDMASEM + 1)))
                        else:
                            c = dop["cnt"]
                            if seen.get(de, 0) >= c:
                                continue
                            cneed[de] = max(cneed.get(de, 0), c)
                    for de, c in cneed.items():
                        seen[de] = c
                        waits.append((esem[de], c))
                    for (s, v) in waits:
                        eng.wait_ge(s, v)
                    if op["fn"] is None:
                        continue
                    ins = op["fn"](eng)
                    if op["dma"]:
                        ins.then_inc(dsem[e][op["dmaslot"] % self.NDMASEM], 16)
                    elif op["marked"]:
                        ins.then_inc(esem[e], 1)

            for e in self.ENGS:
                if not ops[e]:
                    continue

                def mk(e):
                    def _f(eng):
                        run(e, eng)
                    return _f
                getattr(block, engobj[e])(mk(e))


def build(stage=99, dbg=(), ncores=8):
    nc = bass.Bass("TRN2", target_bir_lowering=False)

    def din(name, shape):
        return nc.dram_tensor(name, list(shape), F32, kind="ExternalInput").ap()

    xloc = din("xloc", [NTH, D])
    ctxf = din("ctxf", [NCTX, D])
    cvec = din("cvec", [128, 32])
    w_mod_sh = din("w_mod_sh", [D, 3 * D // 2])
    bmod_l = din("bmod_l", [128, 16])
    bgate_l = din("bgate_l", [1, D // 2])
    rmask = din("rmask", [128, 2])
    normgT = din("normgT", [128, 16])
    w_in = din("w_in", [D, NIN])
    w_dt = din("w_dt", [D, 32])
    convT = din("convT", [128, 16 * 4])
    dtb = din("dtb", [128, 32])
    alog = din("alog", [128, 32])
    dskipT = din("dskipT", [128, 8])
    attgT = din("attgT", [128, 8])
    ssdgT = din("ssdgT", [128, 8])
    sinkb = din("sinkb", [128, 16])
    dskb = din("dskb", [128, 16])
    w_out = din("w_out", [D, D])
    fng = din("fng", [1, D])
    cosT = din("cosT", [128, NTH])
    ssinT = din("ssinT", [128, NTH])
    consts = din("consts", [128, 6 * 128])
    out = nc.dram_tensor("out", [NT, D], F32, kind="ExternalOutput").ap()
    gate_scr = nc.dram_tensor("gate_scr", [1, D], F32).ap()
    ccg_in = nc.dram_tensor("ccg_in", [1, D], F32)
    ccm_in = nc.dram_tensor("ccm_in", [128, 64], F32)
    ccm_out = nc.dram_tensor("ccm_out", [128, 64], F32)
    h_scr = nc.dram_tensor("h_scr", [16, 128, 8, 128], BF16).ap()
    cc_in = [nc.dram_tensor("cc_in%d" % g, [128, 256], F32) for g in range(4)]
    cc_out = [nc.dram_tensor("cc_out%d" % g, [128, 256], F32) for g in range(4)]
    dbg_out = {}
    for (nm, shp) in dbg:
        dbg_out[nm] = nc.dram_tensor("dbg_" + nm, list(shp), F32, kind="ExternalOutput").ap()

    st = contextlib.ExitStack()
    with st:
        S = Sched(nc)
        ARENA = 189 * 1024
        arena = st.enter_context(nc.sbuf_tensor("arena", [128, ARENA // 2], BF16))
        cst = st.enter_context(nc.sbuf_tensor("cst", [128, 1472], F32))
        pbank = [st.enter_context(nc.psum_tensor("pb%d" % i, [128, 512], F32)) for i in range(8)]

        class Ar:
            def __init__(self):
                self.off = 0

            def take(self, nbytes, dt, parts=128):
                nbytes = (nbytes + 63) // 64 * 64
                o = self.off
                self.off += nbytes
                assert self.off <= ARENA, ("arena overflow", self.off)
                v = arena[0:parts, o // 2:(o + nbytes) // 2]
                if dt == F32:
                    v = v.bitcast(F32)
                return v

            def mark(self):
                return self.off

            def reset(self, m):
                self.off = m
                S.barrier()

        A = Ar()
        coff = [0]

        def ctake(n):
            o = coff[0]
            coff[0] += n
            assert coff[0] <= 1472
            return cst[:, o:o + n]

        def fsz(ap):
            n = 1
            for d in ap.shape[1:]:
                n *= d
            return n

        def dma(q, o, i, reads=(), writes=()):
            esz = 4 if i.dtype == F32 else 2
            nbytes = fsz(i) * i.shape[0] * esz
            return S.dma(q, lambda e: e.dma_start(out=o, in_=i), reads, writes, cost=nbytes / 250e3)

        def mm(o, lhsT, rhs, start, stop, reads, writes):
            c = max(0.11, 0.006 + fsz(rhs) / 2400.0)
            if rhs.dtype == F32:
                c = 0.05 + fsz(rhs) / 800.0
            return S.pe(lambda e: e.matmul(o, lhsT=lhsT, rhs=rhs, start=start, stop=stop), reads, writes, cost=c)

        def tr(o, i, reads, writes):
            return S.pe(lambda e: e.transpose(o, i, ident_b), list(reads) + ["cb16"], writes, cost=0.12)

        def act(o, i, func, reads, writes, bias=None, scale=None, accum=None):
            kw = {}
            c = 0.22 + fsz(o) / 1400.0
            if bias is not None:
                kw["bias"] = bias
            if scale is not None:
                kw["scale"] = scale
            if accum is not None:
                kw["accum_out"] = accum
                c += 0.1
            return S.act(lambda e: e.activation(out=o, in_=i, func=func, **kw), reads, writes, cost=c)

        def dcost(o, two=False):
            n = fsz(o)
            if o.dtype == F32 or two:
                return 0.08 + n / 960.0
            return 0.08 + n / 1200.0

        def tt(o, a, b, op, reads, writes, eng="dve"):
            return S.add(eng, lambda e: e.tensor_tensor(out=o, in0=a, in1=b, op=op), reads, writes, cost=dcost(o, True))

        def ts(o, a, s1, s2, op0, op1, reads, writes, eng="dve"):
            if op1 is None:
                return S.add(eng, lambda e: e.tensor_scalar(out=o, in0=a, scalar1=s1, scalar2=None, op0=op0), reads, writes,
                             cost=dcost(o))
            return S.add(eng, lambda e: e.tensor_scalar(out=o, in0=a, scalar1=s1, scalar2=s2, op0=op0, op1=op1), reads, writes,
                         cost=dcost(o))

        def stt(o, a, s, b, op0, op1, reads, writes):
            return S.dve(lambda e: e.scalar_tensor_tensor(out=o, in0=a, scalar=s, in1=b, op0=op0, op1=op1), reads, writes,
                         cost=dcost(o, True))

        def cp(o, i, reads, writes, eng="dve"):
            return S.add(eng, lambda e: e.tensor_copy(out=o, in_=i), reads, writes, cost=dcost(o))

        def memset(o, v, writes, eng="dve"):
            return S.add(eng, lambda e: e.memset(o, v), (), writes, cost=dcost(o))

        def pbf(i):
            return pbank[i][:, :].bitcast(BF16)

        Hst = st.enter_context(nc.sbuf_tensor("Hst", [128, 2048], F32))
        cf = ctake(768)
        dma("sp", cf, consts, writes=["cf"])
        ident_f = cf[:, 0:128]
        TLEf = cf[:, 128:256]
        TGEf = cf[:, 256:384]
        SUf = cf[:, 384:512]
        SLf = cf[:, 512:640]
        cb16 = A.take(6 * 128 * 2, BF16)
        cp(cb16, cf, ["cf"], ["cb16"])
        ident_b = cb16[:, 0:128]
        TLEb = cb16[:, 128:256]
        TGEb = cb16[:, 256:384]
        perm_b = cb16[:, 640:768]
        onesf = ctake(128)
        memset(onesf, 1.0, ["onesf"])
        small = ctake(32 + 32 + 16 + 64 + 32 + 32 + 8 + 8 + 8 + 16 + 16)
        rmask_s = ctake(2)
        cv = small[:, 0:32]
        bmod_s = small[:, 32:48]
        normg_s = small[:, 64:80]
        conv_s = small[:, 80:144]
        dtb_s = small[:, 144:176]
        alog_s = small[:, 176:208]
        dskip_s = small[:, 208:216]
        attg_s = small[:, 216:224]
        ssdg_s = small[:, 224:232]
        sink_s = small[:, 232:248]
        dskb_s = small[:, 248:264]
        for (dst, src, k) in [(cv, cvec, "cv"), (bmod_s, bmod_l, "bmod"), (rmask_s, rmask, "rmask"), (normg_s, normgT, "normg"), (conv_s, convT, "conv"),
                              (dtb_s, dtb, "dtb"), (alog_s, alog, "alog"), (dskip_s, dskipT, "dskip"), (attg_s, attgT, "attg"),
                              (ssdg_s, ssdgT, "ssdg"), (sink_s, sinkb, "sink"), (dskb_s, dskb, "dskb")]:
            dma("sp", dst, src, writes=[k])
        a_s = ctake(32)
        act(a_s, alog_s, AF.Exp, ["alog"], ["a_s"])
        ts(a_s, a_s, -1.0, None, ALU.mult, None, ["a_s"], ["a_s"])
        esink = ctake(16)
        act(esink, sink_s, AF.Exp, ["sink"], ["esink"])
        scb = A.take(32 * 2, BF16)
        act(scb, cv, AF.Silu, ["cv"], ["scb"])

        modT = ctake(64)
        PAIRS = [[2 * i, 2 * i + 1] for i in range(ncores // 2)]
        top = (ARENA - 41984) // 2
        wmb = [arena[:, top + i * 8192:top + (i + 1) * 8192].rearrange("p (k n) -> p k n", k=16) for i in range(2)]
        G2 = arena[:, top + 16384:top + 16384 + 2048].bitcast(F32)
        bgl = arena[:, top + 18432:top + 18432 + 2048].bitcast(F32)
        ml = arena[:, top + 20480:top + 20480 + 64].bitcast(F32)
        modl = arena[:, top + 20544:top + 20544 + 128].bitcast(F32)
        dma("sp", bgl[0:1, :], bgate_l, writes=["bgl"])
        psmod = pbank[3][:, 0:32]
        for lb in range(4):
            wb_ = wmb[lb % 2]
            dma("pool", wb_, w_mod_sh[:, lb * 512:(lb + 1) * 512].rearrange("(k p) n -> p k n", p=128), writes=["wmb%d" % (lb % 2)])
            for c4 in range(4):
                i = lb * 4 + c4
                for kc in range(16):
                    mm(psmod[:, i * 2:i * 2 + 2], wb_[:, kc, c4 * 128:(c4 + 1) * 128], scb[:, kc * 2:kc * 2 + 2],
                       kc == 0, kc == 15, ["wmb%d" % (lb % 2), "scb"], ["ps3"])
        tt(ml.rearrange("p (c j) -> p c j", j=2), psmod.rearrange("p (c j) -> p c j", j=2),
           bmod_s.rearrange("p (c o) -> p c o", o=1).to_broadcast([128, 16, 2]), ALU.add, ["ps3", "bmod"], ["ml"])
        tt(modl.rearrange("p (c r j) -> p c r j", r=2, j=2),
           ml.rearrange("p (c o j) -> p c o j", o=1, j=2).to_broadcast([128, 16, 2, 2]),
           rmask_s.rearrange("p (o r q) -> p o r q", o=1, q=1).to_broadcast([128, 16, 2, 2]), ALU.mult, ["ml", "rmask"], ["modl"])
        dma("sp", ccm_in.ap(), modl, ["modl"], ["ccm_in"])
        S.add("pool", lambda e: e.collective_compute("AllReduce", ALU.add, replica_groups=PAIRS,
                                                     ins=[ccm_in.ap().opt()], outs=[ccm_out.ap().opt()]),
              ["ccm_in"], ["ccm_out"], cost=12.0)
        dma("sp", modT, ccm_out.ap(), ["ccm_out"], ["modT"])
        for gb in range(2):
            wb_ = wmb[gb % 2]
            dma("pool", wb_, w_mod_sh[:, 2048 + gb * 512:2048 + (gb + 1) * 512].rearrange("(k p) n -> p k n", p=128),
                writes=["wmb%d" % (gb % 2)])
            for kc in range(16):
                mm(pbank[5][:, :], scb[:, kc * 2:kc * 2 + 1].to_broadcast([128, 128]), wb_[:, kc, :], kc == 0, kc == 15,
                   ["scb", "wmb%d" % (gb % 2)], ["ps5"])
            tt(G2[0:1, gb * 512:(gb + 1) * 512], pbank[5][0:1, :], bgl[0:1, gb * 512:(gb + 1) * 512], ALU.add, ["ps5", "bgl"], ["G2"])
        tt(Hst[0:1, :].rearrange("p (g r n) -> p g r n", g=2, r=2),
           G2[0:1, :].rearrange("p (g o n) -> p g o n", g=2, o=1).to_broadcast([1, 2, 2, 512]),
           rmask_s[0:1, :].rearrange("p (o r q) -> p o r q", o=1, q=1).to_broadcast([1, 2, 2, 512]), ALU.mult,
           ["G2", "rmask"], ["HF", "HB"])
        dma("sp", ccg_in.ap(), Hst[0:1, :], ["HF", "HB"], ["ccg_in"])
        S.add("pool", lambda e: e.collective_compute("AllReduce", ALU.add, replica_groups=PAIRS,
                                                     ins=[ccg_in.ap().opt()], outs=[gate_scr.tensor.ap().opt() if hasattr(gate_scr, "tensor") else gate_scr.opt()]),
              ["ccg_in"], ["gate_scr"], cost=12.0)
        modv = modT.rearrange("p (c j) -> p c j", j=2)
        gs = [ctake(16), ctake(16)]
        sh = [ctake(16), ctake(16)]
        for j in range(2):
            stt(gs[j], modv[:, 16:32, j], 1.0, normg_s, ALU.add, ALU.mult, ["modT", "normg"], ["gs%d" % j])
            cp(sh[j], modv[:, 0:16, j], ["modT"], ["sh%d" % j])

        if "mod" in dbg_out:
            dma("sp", dbg_out["mod"], modT, ["modT"], ["dbg_mod"])

        def alloc_build_set():
            return dict(xb=[A.take(D * 4, F32) for _ in range(2)], xn=[A.take(D * 2, BF16) for _ in range(2)],
                        junk=A.take(D * 2, BF16), stat=[A.take(64, F32) for _ in range(2)])

        def xkeys(pfx):
            def f(t0, n):
                return [pfx + "%d" % t for t in range(t0 // 128, (t0 + n - 1) // 128 + 1)]
            return f

        def build_xmT(XT, xkp, src, ntiles, j, bs):
            xb, xn, junk, stat = bs["xb"], bs["xn"], bs["junk"], bs["stat"]
            for t in range(ntiles):
                b_ = t % 2
                kx, kn, ks = "b_xb%d" % b_, "b_xn%d" % b_, "b_st%d" % b_
                dma("sp", xb[b_], src[t * 128:(t + 1) * 128, :], writes=[kx])
                act(junk, xb[b_], AF.Square, [kx], ["b_junk", ks], accum=stat[b_][:, 0:1])
                ts(stat[b_][:, 1:2], stat[b_][:, 0:1], 1.0 / D, EPS, ALU.mult, ALU.add, [ks], [ks])
                act(stat[b_][:, 2:3], stat[b_][:, 1:2], AF.Sqrt, [ks], [ks])
                S.dve(lambda e, o=stat[b_][:, 3:4], i=stat[b_][:, 2:3]: e.reciprocal(out=o, in_=i), [ks], [ks], cost=0.1)
                ts(xn[b_], xb[b_], stat[b_][:, 3:4], None, ALU.mult, None, [kx, ks], [kn])
                for q4 in range(4):
                    pt = pbf([2, 4][q4 % 2])[:, 0:512]
                    pk = "ps%d" % [2, 4][q4 % 2]
                    for i4 in range(4):
                        kc = q4 * 4 + i4
                        tr(pt[:, i4 * 128:(i4 + 1) * 128], xn[b_][:, kc * 128:(kc + 1) * 128], [kn], [pk])
                    o = XT[:, q4 * 4:q4 * 4 + 4, t * 128:(t + 1) * 128]
                    pv = pt.rearrange("p (c n) -> p c n", c=4)
                    if q4 % 2 == 0:
                        cp(o, pv, [pk], [xkp + "%d" % t])
                    else:
                        act(o, pv, AF.Copy, [pk], [xkp + "%d" % t])

        def modulate(XT, xkp, ntiles, j):
            keys = [xkp + "%d" % t for t in range(ntiles)]
            for kc in range(16):
                o = XT[:, kc, 0:ntiles * 128]
                if kc % 2 == 0:
                    ts(o, o, gs[j][:, kc:kc + 1], sh[j][:, kc:kc + 1], ALU.mult, ALU.add, keys + ["gs%d" % j, "sh%d" % j], keys)
                else:
                    act(o, o, AF.Identity, keys + ["gs%d" % j, "sh%d" % j], keys, bias=sh[j][:, kc:kc + 1], scale=gs[j][:, kc:kc + 1])

        wslot = [0]

        def load_w(wbufs, src_cols, ncols):
            i = wslot[0] % len(wbufs)
            wslot[0] += 1
            wv = wbufs[i][:, :, 0:ncols]
            dma("pool", wv, src_cols.rearrange("(k p) n -> p k n", p=128), writes=["wbuf%d" % i])
            return wv, "wbuf%d" % i

        pacc = [0]

        def inproj_T(wv, wk, c0, nco, XT, xk, t0, nt):
            b_ = pacc[0] % 2
            pacc[0] += 1
            ps = pbank[b_][0:nco, 0:nt]
            for kc in range(16):
                mm(ps, wv[:, kc, c0:c0 + nco], XT[:, kc, t0:t0 + nt], kc == 0, kc == 15, [wk] + xk(t0, nt), ["ps%d" % b_])
            return ps, "ps%d" % b_

        def inproj_tok(wv, wk, c0, nco, XT, xk, t0, ps, pk):
            for kc in range(16):
                mm(ps, XT[:, kc, t0:t0 + 128], wv[:, kc, c0:c0 + nco], kc == 0, kc == 15, [wk] + xk(t0, 128), [pk])

        def pieces(ntok):
            npc = (ntok + 511) // 512
            sz = ((ntok + npc - 1) // npc + 1) // 2 * 2
            r = []
            t = 0
            while t < ntok:
                n = min(sz, ntok - t)
                r.append((t, n))
                t += n
            return r

        def conv_chunk(wv, wk, c0, XT, xk, ntok, has_halo, cc, taps, outT, outk, stage_f, tmp_f, tb=0, left=False):
            if not left:
                memset(stage_f[:, 0:1], 0.0, ["stage"])
            if not has_halo:
                memset(stage_f[:, ntok + 1:ntok + 2], 0.0, ["stage"])
            lo = tb - (1 if left else 0)
            ntot = ntok + (1 if has_halo else 0) + (1 if left else 0)
            so = 0 if left else 1
            for (t0, n) in pieces(ntot):
                ps, pk = inproj_T(wv, wk, c0, 128, XT, xk, lo + t0, n)
                act(stage_f[:, so + t0:so + t0 + n], ps, AF.Copy, [pk], ["stage"])
            cw = conv_s.rearrange("p (c f) -> p c f", f=4)
            wA = cw[:, cc, taps[0]:taps[0] + 1]
            wB = cw[:, cc, taps[1]:taps[1] + 1]
            wC = cw[:, cc, taps[2]:taps[2] + 1]
            bb = cw[:, cc, 3:4]
            ts(tmp_f[:, 0:ntok], stage_f[:, 0:ntok], wA, None, ALU.mult, None, ["stage", "conv"], ["ctmp"])
            stt(tmp_f[:, 0:ntok], stage_f[:, 1:ntok + 1], wB, tmp_f[:, 0:ntok], ALU.mult, ALU.add, ["stage", "conv", "ctmp"], ["ctmp"])
            stt(tmp_f[:, 0:ntok], stage_f[:, 2:ntok + 2], wC, tmp_f[:, 0:ntok], ALU.mult, ALU.add, ["stage", "conv", "ctmp"], ["ctmp"])
            act(outT[:, tb:tb + ntok], tmp_f[:, 0:ntok], AF.Silu, ["ctmp", "conv"], [outk], bias=bb)

        def to_tok(srcT, srck, nchunks, dst, dstk, c_off):
            for c0 in range(0, nchunks, 4):
                n = min(4, nchunks - c0)
                hb = (c0 // 4) % 2
                pt = pbf([2, 4][hb])[:, 0:n * 128]
                pk = "ps%d" % [2, 4][hb]
                for i in range(n):
                    tr(pt[:, i * 128:(i + 1) * 128], srcT[:, (c0 + i) * 128:(c0 + i + 1) * 128], [srck], [pk])
                cp(dst[:, c0:c0 + n, c_off:c_off + 128], pt.rearrange("p (c n) -> p c n", n=128), [pk], [dstk], eng="dve")

        def alloc_dt_set(nch):
            n = nch * 16
            return [A.take(n * 4, F32) for _ in range(5)]

        def dt_prep(wdt_v, wdtk, dcol, XT, xk, nch, dirn, tg, bufs):
            n = nch * 16
            dtv, adt, wgt, dec, tmp = [b[:, 0:n] for b in bufs]
            ps = pbank[3][:, 0:n]
            for c in range(nch):
                inproj_tok(wdt_v, wdtk, dcol, 16, XT, xk, c * 128, ps[:, c * 16:(c + 1) * 16], "ps3")
            v3 = lambda a: a.rearrange("p (c h) -> p c h", h=16)
            if CUT2 < 1:
                cp(tmp, ps, ["ps3"], [tg + "tmp"])
                return None
            tt(v3(tmp), v3(ps), dtb_s[:, dcol:dcol + 16].rearrange("p (o h) -> p o h", o=1).to_broadcast([128, nch, 16]), ALU.add,
               ["ps3", "dtb"], [tg + "tmp"])
            if CUT2 < 2:
                return None
            act(tmp, tmp, AF.Exp, [tg + "tmp"], [tg + "tmp"])
            if CUT2 < 3:
                return None
            act(dtv, tmp, AF.Ln, [tg + "tmp"], [tg + "dt"], bias=1.0)
            if CUT2 < 4:
                return None
            tt(v3(adt), v3(dtv), a_s[:, dcol:dcol + 16].rearrange("p (o h) -> p o h", o=1).to_broadcast([128, nch, 16]), ALU.mult,
               [tg + "dt", "a_s"], [tg + "adt"])
            if CUT2 < 5:
                return None
            pcs = pbank[3][:, 0:n]
            mm(pcs, TLEf if dirn == "F" else TGEf, adt, True, True, ["cf", tg + "adt"], ["ps3"])
            cp(tmp, pcs, ["ps3"], [tg + "tmp"])
            if CUT2 < 6:
                return None
            ptot = pbank[3][:, 0:n]
            mm(ptot, onesf, adt, True, True, ["onesf", tg + "adt"], ["ps3"])
            if CUT2 < 7:
                cp(tmp, ptot, ["ps3"], [tg + "tmp"])
                return None
            act(dec, ptot, AF.Exp, ["ps3"], [tg + "dec"])
            tt(tmp, ptot, tmp, ALU.subtract, ["ps3", tg + "tmp"], [tg + "tmp"])
            act(tmp, tmp, AF.Exp, [tg + "tmp"], [tg + "tmp"])
            tt(wgt, tmp, dtv, ALU.mult, [tg + "tmp", tg + "dt"], [tg + "wgt"])
            return dict(dt=v3(dtv), adt=v3(adt), wgt=v3(wgt), dec=v3(dec))

        def alloc_scan_set(ntok):
            nch = ntok // 128
            return dict(wdt=A.take(16 * 32 * 2, BF16).rearrange("p (k n) -> p k n", k=16), dt=alloc_dt_set(nch),
                        wb=[A.take(16 * 128 * 2, BF16).rearrange("p (k n) -> p k n", k=16) for _ in range(2)],
                        stage=[A.take((ntok + 2) * 4, F32)], tmp=[A.take(ntok * 4, F32)],
                        cT=A.take(ntok * 2, BF16),
                        xs_tok=A.take(nch * 256 * 2, BF16).rearrange("p (c n) -> p c n", n=256),
                        b_tok=A.take(nch * 128 * 2, BF16).rearrange("p (c n) -> p c n", n=128),
                        xsw=[A.take(256 * 2, BF16) for _ in range(2)])

        def scanF(XT, xk, ntok, has_halo, taps, dcol, Hst, Hk, ss):
            nch = ntok // 128
            wdt_v = ss["wdt"]
            dma("pool", wdt_v, w_dt.rearrange("(k p) n -> p k n", p=128), writes=["s_wdt"])
            dd = dt_prep(wdt_v, "s_wdt", dcol, XT, xk, nch, "F", "s_d", ss["dt"])
            wb = ss["wb"]
            stage_f, tmp_f, cT = ss["stage"][0], ss["tmp"][0], ss["cT"]
            xs_tok = ss["xs_tok"][:, 0:nch, :]
            b_tok = ss["b_tok"][:, 0:nch, :]
            for g in range(4):
                wv, wk = load_w(wb, w_in[:, OFF_B + g * 128:OFF_B + (g + 1) * 128], 128)
                conv_chunk(wv, wk, 0, XT, xk, ntok, has_halo, 8 + g, taps, cT, "s_cT", stage_f, tmp_f)
                to_tok(cT, "s_cT", nch, b_tok, "s_btok", 0)
                for pr in range(2):
                    wv, wk = load_w(wb, w_in[:, OFF_XS + g * 256 + pr * 128:OFF_XS + g * 256 + (pr + 1) * 128], 128)
                    conv_chunk(wv, wk, 0, XT, xk, ntok, has_halo, g * 2 + pr, taps, cT, "s_cT", stage_f, tmp_f)
                    to_tok(cT, "s_cT", nch, xs_tok, "s_xstok", pr * 128)
                Hg = Hst[:, g * 256:(g + 1) * 256]
                for c in range(nch):
                    xsw = ss["xsw"][c % 2]
                    xswk = "s_xsw%d" % (c % 2)
                    tt(xsw.rearrange("p (h d) -> p h d", d=64), xs_tok[:, c, :].rearrange("p (h d) -> p h d", d=64),
                       dd["wgt"][:, c, g * 4:(g + 1) * 4].rearrange("p (h o) -> p h o", o=1).to_broadcast([128, 4, 64]), ALU.mult,
                       ["s_xstok", "s_dwgt"], [xswk])
                    pst = pbank[7][:, 0:256]
                    mm(pst, b_tok[:, c, :], xsw, True, True, ["s_btok", xswk], ["ps7"])
                    tt(Hg.rearrange("p (h d) -> p h d", d=64), Hg.rearrange("p (h d) -> p h d", d=64),
                       dd["dec"][:, c, g * 4:(g + 1) * 4].rearrange("p (h o) -> p h o", o=1).to_broadcast([128, 4, 64]), ALU.mult,
                       [Hk, "s_ddec"], [Hk])
                    tt(Hg, Hg, pst, ALU.add, [Hk, "ps7"], [Hk])

        HF = Hst[:, 0:1024]
        HB = Hst[:, 1024:2048]
        memset(HF, 0.0, ["HF"])
        kctxT = st.enter_context(nc.sbuf_tensor("kctxT", [128, 4 * 256], BF16))
        kctxB = st.enter_context(nc.sbuf_tensor("kctxB", [128, 4 * 256], BF16))
        memset(kctxT[64:128, :], 0.0, ["kctxT"])
        memset(kctxB[0:64, :], 0.0, ["kctxB"])
        vctx = st.enter_context(nc.sbuf_tensor("vctx", [128, 2 * 256], BF16))
        TAPS_F = (0, 1, 2)
        TAPS_R = (2, 1, 0)
        m2 = A.mark()
        XT = A.take(16 * NTH * 2, BF16).rearrange("p (k n) -> p k n", k=16)
        XK = xkeys("XTl")
        m3 = A.mark()
        bset = alloc_build_set()
        XTc = A.take(16 * NCTX * 2, BF16).rearrange("p (k n) -> p k n", k=16)
        sset = alloc_scan_set(NCTX)
        assert A.off <= ARENA - 41984
        xkc = xkeys("XTc")
        build_xmT(XTc, "XTc", ctxf, 2, 1, bset)
        build_xmT(XT, "XTl", xloc, 17, 0, bset)
        modulate(XTc, "XTc", 2, 1)
        modulate(XT, "XTl", 17, 0)
        scanF(XTc, xkc, NCTX, False, TAPS_F, 0, HF, "HF", sset)
        wkv = sset["wb"]
        for g in range(4):
            wv, wk = load_w(wkv, w_in[:, OFF_K + g * 64:OFF_K + (g + 1) * 64], 64)
            wi = (wslot[0] - 1) % 2
            dma("pool", wkv[wi][:, :, 64:128], w_in[:, OFF_K + g * 64:OFF_K + (g + 1) * 64].rearrange("(k p) n -> p k n", p=128),
                writes=[wk])
            ps, pk = inproj_T(wkv[wi][:, :, 0:128], wk, 0, 128, XTc, xkc, 0, NCTX)
            cp(kctxT[0:64, g * 256:(g + 1) * 256], ps[0:64, :], [pk], ["kctxT"])
            cp(kctxB[64:128, g * 256:(g + 1) * 256], ps[64:128, :], [pk], ["kctxB"])
        for hv in range(2):
            wv, wk = load_w(wkv, w_in[:, OFF_V + hv * 128:OFF_V + (hv + 1) * 128], 128)
            for t in range(2):
                psv = pbank[7][:, 0:128]
                inproj_tok(wv, wk, 0, 128, XTc, xkc, t * 128, psv, "ps7")
                cp(vctx[:, t * 256 + hv * 128:t * 256 + (hv + 1) * 128], psv, ["ps7"], ["vctx"])
        A.reset(m3)

        if "H" in dbg_out:
            dma("sp", dbg_out["H"], Hst[:, :], ["HF", "HB"], ["dbg_H"])
        if "kctx" in dbg_out and stage >= 2:
            t_ = A.take(1024 * 4, F32)
            cp(t_, kctxT[:, :], ["kctxT"], ["t_"])
            dma("sp", dbg_out["kctx"], t_, ["t_"], ["dbg_kctx"])

        if stage >= 5:
            wdt_v = A.take(16 * 32 * 2, BF16).rearrange("p (k n) -> p k n", k=16)
            dma("pool", wdt_v, w_dt.rearrange("(k p) n -> p k n", p=128), writes=["l_wdt"])
            dF = dt_prep(wdt_v, "l_wdt", 0, XT, XK, 16, "F", "lF", alloc_dt_set(16))
            dB = dt_prep(wdt_v, "l_wdt", 16, XT, XK, 16, "B", "lB", alloc_dt_set(16))
            wb = [A.take(16 * 128 * 2, BF16).rearrange("p (k n) -> p k n", k=16) for _ in range(2)]
            BTs = [A.take(NT * 2, BF16) for _ in range(2)]
            CT = A.take(NT * 2, BF16)
            xs_toks = [A.take(16 * 256 * 2, BF16).rearrange("p (c n) -> p c n", n=256) for _ in range(2)]
            b_toks = [A.take(16 * 128 * 2, BF16).rearrange("p (c n) -> p c n", n=128) for _ in range(2)]
            prevFas = [A.take(16 * 256 * 2, BF16).rearrange("p (c n) -> p c n", n=256) for _ in range(2)]
            szT = A.take(2 * NT * 2, BF16).rearrange("p (c n) -> p c n", c=2)
            prevB = A.take(16 * 256 * 2, BF16).rearrange("p (c n) -> p c n", n=256)
            xswF = [A.take(256 * 2, BF16) for _ in range(2)]
            ccin_s = A.take(256 * 4, F32)
            ccout_s = A.take(256 * 4, F32)
            Dsk = A.take(4 * 128 * 2, BF16).rearrange("p (h n) -> p h n", h=4)
            HALF = 1024
            stage_f = A.take((HALF + 2) * 4, F32)
            tmp_f = A.take(HALF * 4, F32)
            xsT = A.take(NT * 2, BF16)
            lhs4 = A.take(512 * 4, F32).rearrange("p (h n) -> p h n", h=4)
            L4 = A.take(512 * 2, BF16).rearrange("p (h n) -> p h n", h=4)
            E4 = A.take(512 * 2, BF16).rearrange("p (h n) -> p h n", h=4)
            G4 = [A.take(512 * 2, BF16).rearrange("p (h n) -> p h n", h=4) for _ in range(2)]
            GE4 = [A.take(512 * 2, BF16).rearrange("p (h n) -> p h n", h=4) for _ in range(2)]
            CBm2 = A.take(256 * 2, BF16).rearrange("p (d n) -> p d n", d=2)
            lhs4s = [lhs4, A.take(512 * 4, F32).rearrange("p (h n) -> p h n", h=4)]
            xdt = [A.take(256 * 2, BF16) for _ in range(2)]
            xsw = A.take(256 * 2, BF16)
            ystg = [A.take(256 * 2, BF16).rearrange("p (c n) -> p c n", c=2) for _ in range(2)]
            h3 = lambda a: a.rearrange("p (h d) -> p h d", d=64)
            bch = lambda a, c, g: a[:, c, g * 4:(g + 1) * 4].rearrange("p (h o) -> p h o", o=1).to_broadcast([128, 4, 64])
            for g in range(4):
                par = g % 2
                BT, xs_tok, b_tok, prevFa = BTs[par], xs_toks[par], b_toks[par], prevFas[par]
                kBT, kxs, kbt, kpf, kHB = "BT%d" % par, "xstok%d" % par, "btok%d" % par, "prevFa%d" % par, "HB%d" % par
                HFg = HF[:, g * 256:(g + 1) * 256]
                HBg = HB[:, par * 256:(par + 1) * 256]
                for hf in range(2):
                    tb = hf * HALF
                    wv, wk = load_w(wb, w_in[:, OFF_B + g * 128:OFF_B + (g + 1) * 128], 128)
                    conv_chunk(wv, wk, 0, XT, XK, HALF, True, 8 + g, TAPS_F, BT, kBT, stage_f, tmp_f, tb=tb, left=(hf == 1))
                to_tok(BT, kBT, 16, b_tok, kbt, 0)
                for pr in range(2):
                    for hf in range(2):
                        tb = hf * HALF
                        wv, wk = load_w(wb, w_in[:, OFF_XS + g * 256 + pr * 128:OFF_XS + g * 256 + (pr + 1) * 128], 128)
                        conv_chunk(wv, wk, 0, XT, XK, HALF, True, g * 2 + pr, TAPS_F, xsT, "xsT", stage_f, tmp_f, tb=tb, left=(hf == 1))
                    to_tok(xsT, "xsT", 16, xs_tok, kxs, pr * 128)
                for c in range(16):
                    act(prevFa[:, c, :], HFg, AF.Copy, ["HF"], [kpf])
                    xw = xswF[c % 2]
                    xwk = "xswF%d" % (c % 2)
                    tt(h3(xw), h3(xs_tok[:, c, :]), bch(dF["wgt"], c, g), ALU.mult, [kxs, "lFwgt"], [xwk])
                    mm(pbank[5][:, 0:256], b_tok[:, c, :], xw, True, True, [kbt, xwk], ["ps5"])
                    tt(h3(HFg), h3(HFg), bch(dF["dec"], c, g), ALU.mult, ["HF", "lFdec"], ["HF"])
                    tt(HFg, HFg, pbank[5][:, 0:256], ALU.add, ["HF", "ps5"], ["HF"])
                cp(ccin_s, HFg, ["HF"], ["ccin_s"])
                dma("sp", cc_in[g].ap(), ccin_s, ["ccin_s"], ["cc_in%d" % g])
                S.add("pool", lambda e, g=g: e.collective_compute("AllReduce", ALU.add, replica_groups=[[2 * i, 2 * i + 1] for i in range(ncores // 2)],
                                                                  ins=[cc_in[g].ap().opt()], outs=[cc_out[g].ap().opt()]),
                      ["cc_in%d" % g], ["cc_out%d" % g], cost=25.0)
                dma("sp", ccout_s, cc_out[g].ap(), ["cc_out%d" % g], ["ccout_s"])
                tt(HBg, ccout_s, ccin_s, ALU.subtract, ["ccout_s", "ccin_s"], [kHB])
                for hf in range(2):
                    tb = hf * HALF
                    wv, wk = load_w(wb, w_in[:, OFF_C + g * 128:OFF_C + (g + 1) * 128], 128)
                    conv_chunk(wv, wk, 0, XT, XK, HALF, True, 12 + g, TAPS_F, CT, "CT", stage_f, tmp_f, tb=tb, left=(hf == 1))
                for pr in range(2):
                    wv, wk = load_w(wb, w_in[:, OFF_Z + g * 256 + pr * 128:OFF_Z + g * 256 + (pr + 1) * 128], 128)
                    for (t0, n) in pieces(NT):
                        ps, pk = inproj_T(wv, wk, 0, 128, XT, XK, t0, n)
                        act(szT[:, pr, t0:t0 + n], ps, AF.Silu, [pk], ["szT"])
                for hh in range(4):
                    ts(Dsk[:, hh, :], ident_f, dskb_s[:, g * 4 + hh:g * 4 + hh + 1], None, ALU.mult, None, ["cf", "dskb"], ["Dsk"])
                for c in range(15, -1, -1):
                    act(prevB[:, c, :], HBg, AF.Copy, [kHB], ["prevB"])
                    tt(h3(xsw), h3(xs_tok[:, c, :]), bch(dB["wgt"], c, g), ALU.mult, [kxs, "lBwgt"], ["xsw"])
                    mm(pbank[5][:, 256:512], b_tok[:, c, :], xsw, True, True, [kbt, "xsw"], ["ps5"])
                    tt(h3(HBg), h3(HBg), bch(dB["dec"], c, g), ALU.mult, [kHB, "lBdec"], [kHB])
                    tt(HBg, HBg, pbank[5][:, 256:512], ALU.add, [kHB, "ps5"], [kHB])
                for c in range(16):
                    cs_ = slice(c * 128, (c + 1) * 128)
                    mm(pbank[5][:, 0:128], BT[:, cs_], CT[:, cs_], True, True, [kBT, "CT"], ["ps5"])
                    tt(CBm2, pbank[5][:, 0:128].rearrange("p (o n) -> p o n", o=1).to_broadcast([128, 2, 128]),
                       cf[:, 128:384].rearrange("p (d n) -> p d n", d=2), ALU.mult, ["ps5", "cf"], ["CBm"])
                    for di, (dd_, msk, tri) in enumerate([(dF, SUf, TLEf), (dB, SLf, TGEf)]):
                        pD, pE = 6, 7
                        dk = "lF" if di == 0 else "lB"
                        adt4 = dd_["adt"][:, c, g * 4:(g + 1) * 4]
                        lh = lhs4s[di]
                        if True:
                            tt(lh, msk.rearrange("p (o n) -> p o n", o=1).to_broadcast([128, 4, 128]),
                               adt4.rearrange("p (h o) -> p h o", o=1).to_broadcast([128, 4, 128]), ALU.mult, ["cf", dk + "adt"], ["lhs4%d" % di])
                        else:
                            for hh in range(4):
                                act(lh[:, hh, :], msk, AF.Copy, ["cf", dk + "adt"], ["lhs4%d" % di], scale=adt4[:, hh:hh + 1])
                        for hh in range(4):
                            mm(pbank[pD][:, hh * 128:(hh + 1) * 128], lh[:, hh, :], tri, True, True, ["lhs4%d" % di, "cf"], ["ps%d" % pD])
                        act(L4, pbank[pD][:, :].rearrange("p (h n) -> p h n", h=4), AF.Exp, ["ps%d" % pD], ["L4"])
                        tt(G4[di], L4, CBm2[:, di:di + 1, :].to_broadcast([128, 4, 128]), ALU.mult,
                           ["L4", "CBm"], ["G4%d" % di])
                        for hh in range(4):
                            mm(pbank[pE][:, hh * 128:(hh + 1) * 128], adt4[:, hh:hh + 1].to_broadcast([128, 128]), tri, True, True,
                               [dk + "adt", "cf"], ["ps%d" % pE])
                        act(E4, pbank[pE][:, :].rearrange("p (h n) -> p h n", h=4), AF.Exp, ["ps%d" % pE], ["E4"])
                        tt(GE4[di], E4, CT[:, cs_].rearrange("p (o n) -> p o n", o=1).to_broadcast([128, 4, 128]), ALU.mult,
                           ["E4", "CT"], ["GE4%d" % di])
                        tt(h3(xdt[di]), h3(xs_tok[:, c, :]), bch(dd_["dt"], c, g), ALU.mult, [kxs, dk + "dt"], ["xdt%d" % di])
                    for r in range(2):
                        yo = pbank[3][:, r * 256:(r + 1) * 256]
                        hs = slice(r * 128, (r + 1) * 128)
                        h2 = slice(2 * r, 2 * r + 2)
                        mm(yo, xdt[0][:, hs], G4[0][:, h2, :], True, False, ["xdt0", "G40"], ["ps3"])
                        mm(yo, prevFa[:, c, hs], GE4[0][:, h2, :], False, False, [kpf, "GE40"], ["ps3"])
                        mm(yo, xdt[1][:, hs], G4[1][:, h2, :], False, False, ["xdt1", "G41"], ["ps3"])
                        mm(yo, prevB[:, c, hs], GE4[1][:, h2, :], False, False, ["prevB", "GE41"], ["ps3"])
                        mm(yo, xs_tok[:, c, hs], Dsk[:, h2, :], False, True, [kxs, "Dsk"], ["ps3"])
                    yk = "ystg%d" % (c % 2)
                    for b_ in range(2):
                        rw = slice(b_ * 64, (b_ + 1) * 64)
                        yv = pbank[3][rw, :].rearrange("p (r b n) -> p r b n", r=2, b=2)[:, :, b_, :]
                        tt(ystg[c % 2][rw, :, :], yv, szT[rw, :, cs_], ALU.mult, ["ps3", "szT"], [yk])
                    dma("sp", h_scr[c][:, 2 * g:2 * g + 2, :], ystg[c % 2], [yk], ["h_scr"])
            A.reset(m3)

            if stage >= 6:
                agT = A.take(8 * NT * 2, BF16).rearrange("p (c n) -> p c n", c=8)
                m4 = A.mark()
                A.off = m2
                wo = A.take(16 * D * 2, BF16).rearrange("p (k n) -> p k n", k=16)
                assert A.off <= m3
                A.off = m4
                wb = [A.take(16 * 128 * 2, BF16).rearrange("p (k n) -> p k n", k=16) for _ in range(2)]
                kTs = [A.take(NTH * 2, BF16) for _ in range(2)]
                kTBs = [A.take(NTH * 2, BF16) for _ in range(2)]
                qTs = [A.take(2 * NT * 2, BF16).rearrange("p (a n) -> p a n", a=2) for _ in range(2)]
                sgTs = [A.take(2 * NT * 2, BF16).rearrange("p (a n) -> p a n", a=2) for _ in range(2)]
                VA = A.take(19 * 128 * 2, BF16).rearrange("p (t n) -> p t n", n=128)
                VB = A.take(19 * 128 * 2, BF16).rearrange("p (t n) -> p t n", n=128)
                cos_b = A.take(NTH * 2, BF16)
                ssin_b = A.take(NTH * 2, BF16)
                qraw = A.take(512 * 2, BF16)
                rt1 = Hst[:, 1024:1536]
                rt2 = Hst[:, 1536:2048]
                PT = A.take(5 * 512 * 2, BF16).rearrange("p (c n) -> p c n", c=5)
                lnd = [Hst[:, 0:256], Hst[:, 256:512]]
                rd = lnd
                t1 = [Hst[:, 512:768], Hst[:, 768:1024]]
                for par in range(2):
                    memset(kTs[par][64:128, :], 0.0, ["kT%d" % par])
                    memset(kTBs[par][0:64, :], 0.0, ["kTB%d" % par])
                memset(VA[:, :, 64:128], 1.0, ["VA"])
                memset(VB[:, :, 0:64], 1.0, ["VB"])
                dma("pool", cos_b, cosT, writes=["cos_b"])
                dma("pool", ssin_b, ssinT, writes=["ssin_b"])

                def rope_proj(wv2, wk, ntok, dsts):
                    for (t0, n) in pieces(ntok):
                        ps, pk = inproj_T(wv2, wk, 0, 128, XT, XK, t0, n)
                        act(qraw[:, 0:n], ps, AF.Copy, [pk], ["qraw"])
                        mm(pbank[5][:, 0:n], perm_b, qraw[:, 0:n], True, True, ["cb16", "qraw"], ["ps5"])
                        tt(rt1[:, 0:n], qraw[:, 0:n], cos_b[:, t0:t0 + n], ALU.mult, ["qraw", "cos_b"], ["rt1"])
                        tt(rt2[:, 0:n], pbank[5][:, 0:n], ssin_b[:, t0:t0 + n], ALU.mult, ["ps5", "ssin_b"], ["rt2"])
                        for (dst, rw, dstk) in dsts:
                            tt(dst[rw, t0:t0 + n], rt1[rw, 0:n], rt2[rw, 0:n], ALU.add, ["rt1", "rt2"], [dstk])

                for g in range(4):
                    par = g % 2
                    kT, kTB, qT, sgT = kTs[par], kTBs[par], qTs[par], sgTs[par]
                    kkT, kkTB, kqT, ksg = "kT%d" % par, "kTB%d" % par, "qT%d" % par, "sgT%d" % par
                    wv, wk = load_w(wb, w_in[:, OFF_K + g * 64:OFF_K + (g + 1) * 64], 64)
                    wi = (wslot[0] - 1) % 2
                    dma("pool", wb[wi][:, :, 64:128], w_in[:, OFF_K + g * 64:OFF_K + (g + 1) * 64].rearrange("(k p) n -> p k n", p=128),
                        writes=[wk])
                    rope_proj(wb[wi][:, :, 0:128], wk, NTH, [(kT, slice(0, 64), kkT), (kTB, slice(64, 128), kkTB)])
                    wv, wk = load_w(wb, w_in[:, OFF_V + g * 64:OFF_V + (g + 1) * 64], 64)
                    for t4 in range(0, 17, 4):
                        nt_ = min(4, 17 - t4)
                        for i in range(nt_):
                            inproj_tok(wv, wk, 0, 64, XT, XK, (t4 + i) * 128, pbank[5][:, i * 64:(i + 1) * 64], "ps5")
                        pv = pbank[5][:, 0:nt_ * 64].rearrange("p (t n) -> p t n", n=64)
                        cp(VA[:, t4:t4 + nt_, 0:64], pv, ["ps5"], ["VA"])
                        act(VB[:, t4:t4 + nt_, 64:128], pv, AF.Copy, ["ps5"], ["VB"])
                    vc3 = vctx[:, :].rearrange("p (t n) -> p t n", t=2)
                    cp(VA[:, 17:19, 0:64], vc3[:, :, g * 64:(g + 1) * 64], ["vctx"], ["VA"])
                    cp(VB[:, 17:19, 64:128], vc3[:, :, g * 64:(g + 1) * 64], ["vctx"], ["VB"])
                    for a in range(2):
                        wv, wk = load_w(wb, w_in[:, OFF_Q + (4 * g + 2 * a) * 64:OFF_Q + (4 * g + 2 * a + 2) * 64], 128)
                        rope_proj(wv, wk, NT, [(qT[:, a, :], slice(0, 128), kqT)])
                        wv, wk = load_w(wb, w_in[:, OFF_G + (4 * g + 2 * a) * 64:OFF_G + (4 * g + 2 * a + 2) * 64], 128)
                        for (t0, n) in pieces(NT):
                            ps, pk = inproj_T(wv, wk, 0, 128, XT, XK, t0, n)
                            act(sgT[:, a, t0:t0 + n], ps, AF.Silu, [pk], [ksg])
                    if g == 3:
                        xall = ["XTl%d" % t for t in range(17)]
                        for jb in range(4):
                            dma("pool", wo[:, :, jb * 512:(jb + 1) * 512],
                                w_out[:, jb * 512:(jb + 1) * 512].rearrange("(k p) n -> p k n", p=128), writes=["wo%d" % jb] + xall)
                        for jb in range(4):
                            for ch in range(16):
                                gv = attg_s[:, ch:ch + 1] if ch < 8 else ssdg_s[:, ch - 8:ch - 7]
                                ts(wo[:, ch, jb * 512:(jb + 1) * 512], wo[:, ch, jb * 512:(jb + 1) * 512], gv, None, ALU.mult, None,
                                   ["wo%d" % jb, "attg", "ssdg"], ["wo%d" % jb])
                    for n in range(16):
                        chunks = []
                        if n > 0:
                            chunks.append((n - 1, "k", TGEb))
                        chunks.append((n, "k", None))
                        chunks.append((n + 1, "k", TLEb))
                        chunks.append((17, "c", None))
                        chunks.append((18, "c", None))
                        qs = slice(n * 128, (n + 1) * 128)
                        for ci, (tile, kind, msk) in enumerate(chunks):
                            pb_ = 6 + ci % 2
                            for b_ in range(2):
                                if kind == "k":
                                    lk = (kT if b_ == 0 else kTB)[:, tile * 128:(tile + 1) * 128]
                                    lkk = kkT if b_ == 0 else kkTB
                                else:
                                    lk = (kctxT if b_ == 0 else kctxB)[:, g * 256 + (tile - 17) * 128:g * 256 + (tile - 16) * 128]
                                    lkk = "kctxT" if b_ == 0 else "kctxB"
                                mm(pbank[pb_][:, b_ * 256:(b_ + 1) * 256], lk, qT[:, :, qs], True, True, [lkk, kqT], ["ps%d" % pb_])
                            act(PT[:, ci, :], pbank[pb_][:, :], AF.Exp, ["ps%d" % pb_], ["PT%d" % ci], scale=0.125)
                            if msk is not None:
                                p4 = PT[:, ci, :].rearrange("p (h n) -> p h n", h=4)
                                tt(p4, p4, msk.rearrange("p (o n) -> p o n", o=1).to_broadcast([128, 4, 128]), ALU.mult,
                                   ["PT%d" % ci, "cb16"], ["PT%d" % ci])
                        nci = len(chunks)
                        for ci, (tile, kind, msk) in enumerate(chunks):
                            mm(pbank[2][:, 0:256], VA[:, tile, :], PT[:, ci, 0:256], ci == 0, ci == nci - 1, ["VA", "PT%d" % ci], ["ps2"])
                        for ci, (tile, kind, msk) in enumerate(chunks):
                            mm(pbank[4][:, 0:256], VB[:, tile, :], PT[:, ci, 256:512], ci == 0, ci == nci - 1, ["VB", "PT%d" % ci], ["ps4"])
                        for b_ in range(2):
                            po = pbank[2] if b_ == 0 else pbank[4]
                            pk = "ps2" if b_ == 0 else "ps4"
                            nr = slice(b_ * 64, (b_ + 1) * 64)
                            dr = slice((1 - b_) * 64, (2 - b_) * 64)
                            for a_ in range(2):
                                h = 4 * g + 2 * a_ + b_
                                act(lnd[b_][dr, a_ * 128:(a_ + 1) * 128], po[dr, a_ * 128:(a_ + 1) * 128], AF.Ln, [pk, "esink"], ["lnd%d" % b_],
                                    bias=esink[dr, h:h + 1])
                            act(rd[b_][dr, :], lnd[b_][dr, :], AF.Exp, ["lnd%d" % b_], ["rd%d" % b_], scale=-1.0)
                            tt(t1[b_][nr, :], po[nr, 0:256], rd[b_][dr, :], ALU.mult, [pk, "rd%d" % b_], ["t1%d" % b_])
                            tt(agT[nr, 2 * g:2 * g + 2, qs], t1[b_][nr, :].rearrange("p (a n) -> p a n", a=2), sgT[nr, :, qs], ALU.mult,
                               ["t1%d" % b_, ksg], ["agT"])
                A.reset(m4)
                if "ag" in dbg_out:
                    for ch in range(4):
                        t_ = A.take(NT * 4, F32)
                        cp(t_, agT[:, ch, :], ["agT"], ["t_ag%d" % ch])
                        dma("sp", dbg_out["ag"][ch], t_, ["t_ag%d" % ch], ["dbg_ag"])
                    A.reset(m4)
                    for ch in range(4, 8):
                        t_ = A.take(NT * 4, F32)
                        cp(t_, agT[:, ch, :], ["agT"], ["t_ag%d" % ch])
                        dma("sp", dbg_out["ag"][ch], t_, ["t_ag%d" % ch], ["dbg_ag"])
                    A.reset(m4)

            if stage >= 7:
                S.barrier()
                A.off = m2
                A.off = m4
                gate_bc = A.take(D * 4, F32)
                fng_bc = A.take(D * 4, F32)
                xt_ = [A.take(D * 4, F32) for _ in range(2)]
                res = A.take(D * 4, F32)
                o1 = [A.take(512 * 4, F32) for _ in range(2)]
                sq = [A.take(128 * 2, BF16) for _ in range(2)]
                st2 = [A.take(64, F32) for _ in range(2)]
                ytile = [A.take(8 * 128 * 2, BF16) for _ in range(2)]
                dma("sp", gate_bc, gate_scr.to_broadcast([128, D]), ["gate_scr"], ["gate_bc"])
                dma("sp", fng_bc, fng.to_broadcast([128, D]), writes=["fng_bc"])
                ones2 = TLEb[:, 126:128]
                for t in range(16):
                    b_ = t % 2
                    tsl = slice(t * 128, (t + 1) * 128)
                    xk_ = "xt%d" % b_
                    sk_ = "st2%d" % b_
                    dma("sp", xt_[b_], xloc[tsl, :], writes=[xk_])
                    dma("sp", ytile[b_], h_scr[t].rearrange("p c n -> p (c n)"), ["h_scr"], ["yt%d" % b_])
                    ygT = ytile[b_].rearrange("p (c n) -> p c n", c=8)
                    for br, (src, srck, sl_) in enumerate([(agT, "agT", tsl), (ygT, "yt%d" % b_, slice(0, 128))]):
                        for ch in range(8):
                            act(sq[ch % 2], src[:, ch, sl_], AF.Square, [srck], ["sq%d" % (ch % 2)])
                            mm(pbank[3][:, br * 2:br * 2 + 2], sq[ch % 2], ones2, ch == 0, ch == 7, ["sq%d" % (ch % 2), "cb16"], ["ps3"])
                    s_ = st2[b_]
                    ts(s_[:, 0:4], pbank[3][:, 0:4], 1.0 / 1024.0, EPS, ALU.mult, ALU.add, ["ps3"], [sk_])
                    act(s_[:, 0:4], s_[:, 0:4], AF.Sqrt, [sk_], [sk_])
                    S.dve(lambda e, o=s_[:, 4:8], i=s_[:, 0:4]: e.reciprocal(out=o, in_=i), [sk_], [sk_])
                    for cb in range(4):
                        cs_ = slice(cb * 512, (cb + 1) * 512)
                        pa, pka = (pbank[0], "ps0") if cb % 2 == 0 else (pbank[6], "ps6")
                        pss, pks = (pbank[1], "ps1") if cb % 2 == 0 else (pbank[7], "ps7")
                        for ch in range(8):
                            mm(pa[:, :], agT[:, ch, tsl], wo[:, ch, cs_], ch == 0, ch == 7, ["agT", "wo%d" % cb], [pka])
                        for ch in range(8):
                            mm(pss[:, :], ygT[:, ch, :], wo[:, 8 + ch, cs_], ch == 0, ch == 7, ["yt%d" % b_, "wo%d" % cb], [pks])
                        ok_ = "o1%d" % (cb % 2)
                        oo = o1[cb % 2]
                        act(oo, pa[:, :], AF.Identity, [pka, sk_], [ok_], scale=s_[:, 5:6])
                        stt(oo, pss[:, :], s_[:, 7:8], oo, ALU.mult, ALU.add, [pks, sk_, ok_], [ok_])
                        tt(oo, oo, gate_bc[:, cs_], ALU.mult, [ok_, "gate_bc"], [ok_])
                        tt(res[:, cs_], oo, xt_[b_][:, cs_], ALU.add, [ok_, xk_], ["res"])
                    act(xt_[b_], res, AF.Square, ["res"], [xk_, sk_], accum=s_[:, 8:9])
                    ts(s_[:, 9:10], s_[:, 8:9], 1.0 / D, EPS, ALU.mult, ALU.add, [sk_], [sk_])
                    act(s_[:, 10:11], s_[:, 9:10], AF.Sqrt, [sk_], [sk_])
                    S.dve(lambda e, o=s_[:, 11:12], i=s_[:, 10:11]: e.reciprocal(out=o, in_=i), [sk_], [sk_])
                    stt(xt_[b_], res, s_[:, 11:12], fng_bc, ALU.mult, ALU.mult, ["res", sk_, "fng_bc"], [xk_])
                    dma("sp", out[tsl, :], xt_[b_], [xk_], ["out"])

        fw = [(q, i) for q in ("sp", "pool") for i, op in enumerate(S.ops[q]) if op["dma"]]
        S.emit(final_waits=fw)
    return nc


def _fm(v, nchunk):
    return np.ascontiguousarray(np.asarray(v, np.float32).reshape(nchunk, 128).T)


def _consts():
    k = np.arange(128)[:, None]
    l = np.arange(128)[None, :]
    ident = (k == l).astype(np.float32)
    tle = (k <= l).astype(np.float32)
    tge = (k >= l).astype(np.float32)
    su = (k > l).astype(np.float32)
    sl = (k < l).astype(np.float32)
    d = np.arange(128)
    sw = np.where((d % 32) < 16, d + 16, d - 16)
    perm = np.zeros((128, 128), np.float32)
    perm[sw, d] = 1.0
    return np.ascontiguousarray(np.concatenate([ident, tle, tge, su, sl, perm], axis=1))


def _rope_tables(pos):
    pos = np.asarray(pos)
    row = (pos // 64).astype(np.float32)
    col = (pos % 64).astype(np.float32)
    quarter = 16
    freq = (1.0 / (np.float32(10000.0) ** (np.arange(quarter, dtype=np.float32) / np.float32(quarter)))).astype(np.float32)
    cosT = np.zeros((128, len(pos)), np.float32)
    ssinT = np.zeros((128, len(pos)), np.float32)
    for p in range(128):
        dd = p % 64
        i = dd % 16
        pp = row if dd < 32 else col
        ang = (pp * freq[i]).astype(np.float32)
        cosT[p] = np.cos(ang)
        s = np.sin(ang)
        ssinT[p] = -s if (dd % 32) < 16 else s
    return cosT, ssinT


def prep_inputs(inp):
    g = lambda k: np.asarray(inp[k], np.float32)
    x, c, ctx, c_ctx = g("x"), g("c"), g("ctx"), g("c_ctx")
    w_mod, b_mod, norm_g, w_in = g("w_mod")[0], g("b_mod")[0], g("norm_g")[0], g("w_in")[0]
    conv_w, conv_b = g("conv_w")[0], g("conv_b")[0]
    a_f, a_b, db_f, db_b = g("a_log_f")[0], g("a_log_b")[0], g("dt_bias_f")[0], g("dt_bias_b")[0]
    d_skip, attg, ssdg, sink = g("d_skip")[0], g("att_norm_g")[0], g("ssd_norm_g")[0], g("sink")[0]
    w_out, fng = g("w_out")[0], g("final_norm_g")
    consts = _consts()
    shared = dict(w_in=np.ascontiguousarray(w_in), w_out=np.ascontiguousarray(w_out),
                  normgT=_fm(norm_g, 16), dskipT=_fm(np.repeat(d_skip, 64), 8), attgT=_fm(attg, 8), ssdgT=_fm(ssdg, 8),
                  sinkb=np.ascontiguousarray(np.broadcast_to(sink[None, :], (128, 16))),
                  dskb=np.ascontiguousarray(np.broadcast_to(d_skip[None, :], (128, 16))),
                  fng=np.ascontiguousarray(fng[None, :]), consts=consts)
    maps = []
    for core in range(8):
        b, h = core // 2, core % 2
        xb = x[b]
        xl = xb if h == 0 else xb[::-1]
        pos = np.arange(SEQ) if h == 0 else np.arange(SEQ)[::-1]
        cl = ctx[b] if h == 0 else ctx[b][::-1]
        cw = conv_w if h == 0 else conv_w[::-1]
        convT = np.stack([_fm(cw[0], 16), _fm(cw[1], 16), _fm(cw[2], 16), _fm(conv_b, 16)], axis=2).reshape(128, 64)
        if h == 0:
            wdt = w_in[:, OFF_DTF:OFF_DTF + 32]
            dtbv = np.concatenate([db_f, db_b])
            alv = np.concatenate([a_f, a_b])
        else:
            wdt = np.concatenate([w_in[:, OFF_DTB:OFF_DTB + 16], w_in[:, OFF_DTF:OFF_DTF + 16]], axis=1)
            dtbv = np.concatenate([db_b, db_f])
            alv = np.concatenate([a_b, a_f])
        cosT, ssinT = _rope_tables(pos[0:NTH])
        cvec = np.stack([_fm(c[b], 16), _fm(c_ctx, 16)], axis=2).reshape(128, 32)
        chs = [2 * i + h for i in range(16)]
        gbs = [2 * gb + h for gb in range(2)]
        wsh = np.concatenate([w_mod[:, ch * 128:(ch + 1) * 128] for ch in chs] +
                             [w_mod[:, 2 * D + k * 512:2 * D + (k + 1) * 512] for k in gbs], axis=1)
        bml = np.stack([b_mod[ch * 128:(ch + 1) * 128] for ch in chs], axis=1)
        bgl = np.concatenate([b_mod[2 * D + k * 512:2 * D + (k + 1) * 512] for k in gbs])[None, :]
        rm = np.zeros((128, 2), np.float32)
        rm[:, h] = 1.0
        m = dict(shared)
        m.update(xloc=np.ascontiguousarray(xl[0:NTH]), ctxf=np.ascontiguousarray(cl),
                 w_mod_sh=np.ascontiguousarray(wsh), bmod_l=np.ascontiguousarray(bml), bgate_l=np.ascontiguousarray(bgl), rmask=rm,
                 cvec=np.ascontiguousarray(cvec), w_dt=np.ascontiguousarray(wdt), convT=np.ascontiguousarray(convT),
                 dtb=np.ascontiguousarray(np.broadcast_to(dtbv[None, :], (128, 32))),
                 alog=np.ascontiguousarray(np.broadcast_to(alv[None, :], (128, 32))),
                 cosT=cosT, ssinT=ssinT)
        maps.append(m)
    return maps


def kernel(**inputs):
    maps = prep_inputs(inputs)
    nc = build()
    res = run_bass_kernel_spmd(nc, maps, core_ids=list(range(8)))
    outp = np.zeros((4, SEQ, D), np.float32)
    for core in range(8):
        b, h = core // 2, core % 2
        o = res.results[core]["out"]
        if h == 0:
            outp[b, 0:NT] = o
        else:
            outp[b, NT:SEQ] = o[::-1]
    return outp
```
